# Optimizing a Trainium2 kernel written in Bass

```python
import math
import jax
import jax.numpy as jnp
from jax import lax
import numpy as np

D_MODEL = 1024
BATCH = 4
SEQ = 8192
DEPTH = 2

RET_HEADS = 4
RET_QK_DIM = 128
RET_V_DIM = 128
RET_CHUNK = 128
ROPE_BASE = 10000.0
SSD_HEADS = 16
SSD_HEAD_DIM = 64
SSD_GROUPS = 2
SSD_STATE = 128
SSD_CONV = 4
SSD_CHUNK = 128
GLA_HEADS = 4
GLA_K_DIM = 64
GLA_V_DIM = 128
GLA_GATE_RANK = 16
GLA_TAU = 16.0
GLA_CHUNK = 16
D_FF = 4 * D_MODEL
EPS = 1e-6
N_BRANCHES = 3

RET_QK = RET_HEADS * RET_QK_DIM
RET_V = RET_HEADS * RET_V_DIM
SSD_INNER = SSD_HEADS * SSD_HEAD_DIM
SSD_CONV_DIM = SSD_INNER + 2 * SSD_GROUPS * SSD_STATE
GLA_QK = GLA_HEADS * GLA_K_DIM
GLA_V = GLA_HEADS * GLA_V_DIM
IN_SIZES = (RET_QK, RET_QK, RET_V, RET_V,
            SSD_INNER, SSD_CONV_DIM, SSD_HEADS,
            GLA_QK, GLA_QK, GLA_V, GLA_V, GLA_GATE_RANK,
            N_BRANCHES * D_MODEL)
IN_WIDTH = sum(IN_SIZES)

kernel_name = 'hybrid_retention_ssd_gla_gated_block'


def _split_points(sizes):
    pts, acc = [], 0
    for s in sizes[:-1]:
        acc += s
        pts.append(acc)
    return pts


def _rmsnorm(x, w):
    xf = x.astype(jnp.float32)
    y = xf * lax.rsqrt(jnp.mean(xf * xf, axis=-1, keepdims=True) + EPS)
    return (y * w.astype(jnp.float32)).astype(x.dtype)


def _group_rmsnorm(x, w, groups):
    shp = x.shape
    xf = x.astype(jnp.float32).reshape(shp[:-1] + (groups, shp[-1] // groups))
    y = xf * lax.rsqrt(jnp.mean(xf * xf, axis=-1, keepdims=True) + EPS)
    return (y.reshape(shp) * w.astype(jnp.float32)).astype(x.dtype)


def _rotary(x, cos, sin):
    x1, x2 = jnp.split(x, 2, axis=-1)
    c = cos[None, :, None, :]
    s = sin[None, :, None, :]
    return jnp.concatenate([x1 * c - x2 * s, x1 * s + x2 * c], axis=-1)


def _causal_conv(x, w, b):
    k, ch = w.shape
    y = lax.conv_general_dilated(x, w[:, None, :].astype(x.dtype), window_strides=(1,),
                                 padding=[(k - 1, 0)],
                                 dimension_numbers=('NWC', 'WIO', 'NWC'),
                                 feature_group_count=ch)
    return y + b


def _to_chunks(t, c):
    bsz, seq, h, d = t.shape
    return t.astype(jnp.float32).reshape(bsz, seq // c, c, h, d).transpose(0, 3, 1, 2, 4)


def _retention(q, k, v):
    out_dtype = q.dtype
    f32 = jnp.float32
    bsz, seq, h, dk = q.shape
    dv = v.shape[-1]
    c = RET_CHUNK
    qc = _to_chunks(q, c)
    kc = _to_chunks(k, c) * (dk ** -0.5)
    vc = _to_chunks(v, c)
    log_gamma = jnp.log1p(-jnp.exp2(-5.0 - jnp.arange(h, dtype=f32)))
    pos = jnp.arange(c, dtype=f32)
    dist = pos[:, None] - pos[None, :]
    causal = dist >= 0
    intra_decay = jnp.where(causal, jnp.exp(log_gamma[:, None, None] * jnp.where(causal, dist, 0.0)), 0.0)
    scores = jnp.einsum('bhnid,bhnjd->bhnij', qc, kc) * intra_decay[None, :, None]
    o_intra = jnp.einsum('bhnij,bhnje->bhnie', scores, vc)
    q_decay = jnp.exp(log_gamma[:, None] * (pos + 1.0))[None, :, :, None]
    k_decay = jnp.exp(log_gamma[:, None] * (c - 1.0 - pos))[None, :, :, None]
    chunk_decay = jnp.exp(log_gamma * c)[None, :, None, None]

    def step(state, inp):
        qn, kn, vn = inp
        o = jnp.einsum('bhcd,bhde->bhce', qn * q_decay, state)
        state = chunk_decay * state + jnp.einsum('bhcd,bhce->bhde', kn * k_decay, vn)
        return state, o

    init = jnp.zeros((bsz, h, dk, dv), f32)
    _, o_inter = lax.scan(step, init, (jnp.moveaxis(qc, 2, 0), jnp.moveaxis(kc, 2, 0), jnp.moveaxis(vc, 2, 0)))
    o = o_intra + jnp.moveaxis(o_inter, 0, 2)
    return o.transpose(0, 2, 3, 1, 4).reshape(bsz, seq, h, dv).astype(out_dtype)


def _ssd(x, dt, a_log, b_in, c_in, d_skip):
    out_dtype = x.dtype
    f32 = jnp.float32
    bsz, seq, n_heads, p = x.shape
    g, n_state = b_in.shape[2], b_in.shape[3]
    r = n_heads // g
    c = SSD_CHUNK
    nc = seq // c
    x = x.astype(f32)
    dt = dt.astype(f32)
    a = dt * -jnp.exp(a_log.astype(f32))
    xs = (x * dt[..., None]).reshape(bsz, nc, c, g, r, p)
    a = a.reshape(bsz, nc, c, g, r).transpose(0, 3, 4, 1, 2)
    bc = b_in.astype(f32).reshape(bsz, nc, c, g, n_state)
    cc = c_in.astype(f32).reshape(bsz, nc, c, g, n_state)
    a_cs = jnp.cumsum(a, axis=-1)
    causal = jnp.tril(jnp.ones((c, c), dtype=bool))
    l_mat = jnp.exp(jnp.where(causal, a_cs[..., :, None] - a_cs[..., None, :], -jnp.inf))
    cb = jnp.einsum('bnlgs,bnmgs->bgnlm', cc, bc)
    y_diag = jnp.einsum('bgnlm,bgrnlm,bnmgrp->bnlgrp', cb, l_mat, xs)
    decay_to_end = jnp.exp(a_cs[..., -1:] - a_cs)
    chunk_states = jnp.einsum('bnmgs,bgrnm,bnmgrp->nbgrps', bc, decay_to_end, xs)
    chunk_decay = jnp.moveaxis(jnp.exp(a_cs[..., -1]), -1, 0)

    def step(state, inp):
        st, dec = inp
        return dec[..., None, None] * state + st, state

    init = jnp.zeros((bsz, g, r, p, n_state), f32)
    _, prev_states = lax.scan(step, init, (chunk_states, chunk_decay))
    y_off = jnp.einsum('bnlgs,nbgrps,bgrnl->bnlgrp', cc, prev_states, jnp.exp(a_cs))
    y = (y_diag + y_off).reshape(bsz, seq, n_heads, p) + x * d_skip.astype(f32)[:, None]
    return y.astype(out_dtype)


def _gla(q, k, v, log_alpha):
    out_dtype = q.dtype
    f32 = jnp.float32
    bsz, seq, h, dk = q.shape
    dv = v.shape[-1]
    c = GLA_CHUNK
    qc = _to_chunks(q, c) * (dk ** -0.5)
    kc = _to_chunks(k, c)
    vc = _to_chunks(v, c)
    b = jnp.cumsum(_to_chunks(log_alpha, c), axis=3)
    b_last = b[:, :, :, -1:, :]
    q_in = qc * jnp.exp(b)
    k_in = kc * jnp.exp(-b)
    causal = jnp.tril(jnp.ones((c, c), dtype=bool))
    scores = jnp.where(causal, jnp.einsum('bhnid,bhnjd->bhnij', q_in, k_in), 0.0)
    o_intra = jnp.einsum('bhnij,bhnje->bhnie', scores, vc)
    k_state = kc * jnp.exp(b_last - b)
    chunk_decay = jnp.exp(b_last[:, :, :, 0, :])

    def step(state, inp):
        qn, kn, vn, dn = inp
        o = jnp.einsum('bhcd,bhde->bhce', qn, state)
        state = dn[..., None] * state + jnp.einsum('bhcd,bhce->bhde', kn, vn)
        return state, o

    init = jnp.zeros((bsz, h, dk, dv), f32)
    _, o_inter = lax.scan(step, init, (jnp.moveaxis(q_in, 2, 0), jnp.moveaxis(k_state, 2, 0),
                                       jnp.moveaxis(vc, 2, 0), jnp.moveaxis(chunk_decay, 2, 0)))
    o = o_intra + jnp.moveaxis(o_inter, 0, 2)
    return o.transpose(0, 2, 3, 1, 4).reshape(bsz, seq, h, dv).astype(out_dtype)


def _mixer_block(x, norm_w, w_in, ret_norm_w, ret_w_o, conv_w, conv_b, dt_bias, a_log, d_skip,
                 ssd_norm_w, ssd_w_o, gla_gate_w, gla_gate_b, gla_norm_w, gla_w_o,
                 merge_b, w_out, cos, sin):
    bsz, seq, _ = x.shape
    h = _rmsnorm(x, norm_w)
    proj = h @ w_in
    (rq, rk, rv, rg, sz, sxbc, sdt, gq, gk, gv, gr, glr, mg) = jnp.split(proj, _split_points(IN_SIZES), axis=-1)

    rq = _rotary(rq.reshape(bsz, seq, RET_HEADS, RET_QK_DIM), cos, sin)
    rk = _rotary(rk.reshape(bsz, seq, RET_HEADS, RET_QK_DIM), cos, sin)
    ro = _retention(rq, rk, rv.reshape(bsz, seq, RET_HEADS, RET_V_DIM))
    ro = _group_rmsnorm(ro.reshape(bsz, seq, RET_V), ret_norm_w, RET_HEADS)
    ret_out = (jax.nn.silu(rg) * ro) @ ret_w_o

    xbc = jax.nn.silu(_causal_conv(sxbc, conv_w, conv_b))
    sx, sb, sc = jnp.split(xbc, [SSD_INNER, SSD_INNER + SSD_GROUPS * SSD_STATE], axis=-1)
    dt = jax.nn.softplus(sdt + dt_bias)
    sy = _ssd(sx.reshape(bsz, seq, SSD_HEADS, SSD_HEAD_DIM), dt, a_log,
              sb.reshape(bsz, seq, SSD_GROUPS, SSD_STATE), sc.reshape(bsz, seq, SSD_GROUPS, SSD_STATE), d_skip)
    sy = _group_rmsnorm(sy.reshape(bsz, seq, SSD_INNER) * jax.nn.silu(sz), ssd_norm_w, SSD_GROUPS)
    ssd_out = sy @ ssd_w_o

    log_alpha = jax.nn.log_sigmoid((glr @ gla_gate_w + gla_gate_b).astype(jnp.float32)) / GLA_TAU
    go = _gla(gq.reshape(bsz, seq, GLA_HEADS, GLA_K_DIM), gk.reshape(bsz, seq, GLA_HEADS, GLA_K_DIM),
              gv.reshape(bsz, seq, GLA_HEADS, GLA_V_DIM), log_alpha.reshape(bsz, seq, GLA_HEADS, GLA_K_DIM))
    go = _group_rmsnorm(go.reshape(bsz, seq, GLA_V), gla_norm_w, GLA_HEADS)
    gla_out = (jax.nn.silu(gr) * go) @ gla_w_o

    g_ret, g_ssd, g_gla = jnp.split(jax.nn.sigmoid(mg + merge_b), N_BRANCHES, axis=-1)
    merged = g_ret * ret_out + g_ssd * ssd_out + g_gla * gla_out
    return x + merged @ w_out


def _mlp_block(x, norm_w, w_up, w_down):
    h = _rmsnorm(x, norm_w)
    return x + jnp.square(jax.nn.relu(h @ w_up)) @ w_down


def setup_inputs(seed: int = 0) -> dict:
    key = jax.random.key(seed)
    ks = jax.random.split(key, 22)
    f32 = jnp.float32

    def normal(k, shape, scale):
        return jax.random.normal(k, shape, f32) * scale

    def gain(k, shape):
        return 1.0 + 0.02 * jax.random.normal(k, shape, f32)

    dt0 = jnp.exp(jax.random.uniform(ks[7], (DEPTH, SSD_HEADS), f32, math.log(1e-3), math.log(1e-1)))
    dt_bias = dt0 + jnp.log(-jnp.expm1(-dt0))
    a_log = jnp.log(jax.random.uniform(ks[8], (DEPTH, SSD_HEADS), f32, 1.0, 16.0))
    return {
        'x': normal(ks[0], (BATCH, SEQ, D_MODEL), 1.0),
        'attn_norm_w': gain(ks[1], (DEPTH, D_MODEL)),
        'w_in': normal(ks[2], (DEPTH, D_MODEL, IN_WIDTH), D_MODEL ** -0.5),
        'ret_norm_w': gain(ks[3], (DEPTH, RET_V)),
        'ret_w_o': normal(ks[4], (DEPTH, RET_V, D_MODEL), RET_V ** -0.5),
        'ssd_conv_w': normal(ks[5], (DEPTH, SSD_CONV, SSD_CONV_DIM), SSD_CONV ** -0.5),
        'ssd_conv_b': normal(ks[6], (DEPTH, SSD_CONV_DIM), 0.02),
        'ssd_dt_bias': dt_bias,
        'ssd_a_log': a_log,
        'ssd_d': gain(ks[9], (DEPTH, SSD_HEADS)),
        'ssd_norm_w': gain(ks[10], (DEPTH, SSD_INNER)),
        'ssd_w_o': normal(ks[11], (DEPTH, SSD_INNER, D_MODEL), SSD_INNER ** -0.5),
        'gla_gate_w': normal(ks[12], (DEPTH, GLA_GATE_RANK, GLA_QK), GLA_GATE_RANK ** -0.5),
        'gla_gate_b': normal(ks[13], (DEPTH, GLA_QK), 0.02),
        'gla_norm_w': gain(ks[14], (DEPTH, GLA_V)),
        'gla_w_o': normal(ks[15], (DEPTH, GLA_V, D_MODEL), GLA_V ** -0.5),
        'merge_gate_b': normal(ks[16], (DEPTH, N_BRANCHES * D_MODEL), 0.02),
        'w_out': normal(ks[17], (DEPTH, D_MODEL, D_MODEL), D_MODEL ** -0.5),
        'mlp_norm_w': gain(ks[18], (DEPTH, D_MODEL)),
        'w_up': normal(ks[19], (DEPTH, D_MODEL, D_FF), D_MODEL ** -0.5),
        'w_down': normal(ks[20], (DEPTH, D_FF, D_MODEL), D_FF ** -0.5),
        'final_norm_w': gain(ks[21], (D_MODEL,)),
    }


def reference(x, attn_norm_w, w_in, ret_norm_w, ret_w_o, ssd_conv_w, ssd_conv_b, ssd_dt_bias,
              ssd_a_log, ssd_d, ssd_norm_w, ssd_w_o, gla_gate_w, gla_gate_b, gla_norm_w, gla_w_o,
              merge_gate_b, w_out, mlp_norm_w, w_up, w_down, final_norm_w):
    seq = x.shape[1]
    inv_freq = ROPE_BASE ** (-jnp.arange(0, RET_QK_DIM, 2, dtype=jnp.float32) / RET_QK_DIM)
    ang = jnp.arange(seq, dtype=jnp.float32)[:, None] * inv_freq[None, :]
    cos = jnp.cos(ang).astype(x.dtype)
    sin = jnp.sin(ang).astype(x.dtype)
    for layer in range(DEPTH):
        x = _mixer_block(x, attn_norm_w[layer], w_in[layer], ret_norm_w[layer], ret_w_o[layer],
                         ssd_conv_w[layer], ssd_conv_b[layer], ssd_dt_bias[layer], ssd_a_log[layer],
                         ssd_d[layer], ssd_norm_w[layer], ssd_w_o[layer], gla_gate_w[layer],
                         gla_gate_b[layer], gla_norm_w[layer], gla_w_o[layer], merge_gate_b[layer],
                         w_out[layer], cos, sin)
        x = _mlp_block(x, mlp_norm_w[layer], w_up[layer], w_down[layer])
    return _rmsnorm(x, final_norm_w)
```

```python
import math
from contextlib import ExitStack

import numpy as np
import concourse.bass as bass
import concourse.mybir as mybir
from concourse.bass_utils import run_bass_kernel_spmd

F32 = mybir.dt.float32
BF16 = mybir.dt.bfloat16
F32R = mybir.dt.float32r
AF = mybir.ActivationFunctionType
ALU = mybir.AluOpType
GRAN = 512
ESZ = {F32: 4, BF16: 2, F32R: 4}
PE, ACT, DVE, POOL, SP = "tensor", "scalar", "vector", "gpsimd", "sync"
ENGS = [PE, ACT, DVE, POOL, SP]

D = 1024
TT = 512
NCH = 4
EPS = 1e-6
NBLK = 40
N_CORES = 8


class Reg:
    __slots__ = ("t", "c0", "c1", "ap")

    def __init__(self, t, c0, c1, ap):
        self.t, self.c0, self.c1, self.ap = t, c0, c1, ap

    def grans(self):
        e = self.t.esz
        tid = self.t.tid
        return [(tid, g) for g in range((self.c0 * e) // GRAN, (self.c1 * e - 1) // GRAN + 1)]

    def m(self, fn):
        return Reg(self.t, self.c0, self.c1, fn(self.ap))


class T:
    _n = 0

    def __init__(self, handle, cols, dtype, parts=128, psum=False):
        self.h, self.cols, self.dtype, self.esz, self.parts = handle, cols, dtype, ESZ[dtype], parts
        T._n += 1
        self.tid = T._n
        self.psum = psum

    def r(self, c0=0, c1=None, p0=0, p1=None):
        c1 = self.cols if c1 is None else c1
        p1 = self.parts if p1 is None else p1
        return Reg(self, c0, c1, self.h[p0:p1, c0:c1])

    def as_dtype(self, dtype):
        t = T.__new__(T)
        t.h = self.h.bitcast(dtype)
        t.dtype, t.esz, t.parts, t.tid = dtype, ESZ[dtype], self.parts, self.tid
        t.psum = self.psum
        t.cols = self.cols * self.esz // t.esz
        return t


class Sub:
    def __init__(self, parent, off, cols, parts=128):
        self.p, self.off, self.cols, self.parts = parent, off, cols, parts

    def r(self, c0=0, c1=None, p0=0, p1=None):
        c1 = self.cols if c1 is None else c1
        p1 = self.parts if p1 is None else p1
        return self.p.r(self.off + c0, self.off + c1, p0, p1)


class DTrack:
    def __init__(self):
        T._n += 1
        self.tid, self.esz, self.psum = T._n, 4, False

    def r(self):
        return Reg(self, 0, 1, None)


class Prog:
    def __init__(self, nc):
        self.nc = nc
        self.ops = {e: [] for e in ENGS}
        self.gr = {}
        self.waited = {e: {} for e in ENGS}
        self.needed = {e: set() for e in ENGS}
        self.dma_cnt = {}
        self.cc_cnt = {}
        self.final = []
        self.banks = []
        self.bank_i = 0
        self.bank_last = {}

    def psum(self):
        b = self.banks[self.bank_i % len(self.banks)]
        self.bank_i += 1
        return b

    def _deps(self, eng, reads, writes):
        deps = {}

        def add(ev):
            if ev is None:
                return
            k, v = ev
            if k == PE and eng == PE:
                return
            if deps.get(k, -1) < v:
                deps[k] = v
        gr = self.gr
        for r in reads:
            for g in r.grans():
                st = gr.get(g)
                if st is not None:
                    add(st[0])
        for w in writes:
            for g in w.grans():
                st = gr.get(g)
                if st is not None:
                    add(st[0])
                    for ev in st[1]:
                        add(ev)
        for r in list(reads) + list(writes):
            if r.t.psum:
                bl = self.bank_last.get(r.t.tid)
                if bl:
                    for e2, ev in bl.items():
                        if e2 != eng:
                            add(ev)
        out = []
        wd = self.waited[eng]
        for k, v in deps.items():
            if wd.get(k, -1) >= v:
                continue
            wd[k] = v
            out.append((k, v))
            if k in self.needed:
                self.needed[k].add(v)
        return out

    def _mark(self, ev, reads, writes):
        gr = self.gr
        for r in list(reads) + list(writes):
            if r.t.psum:
                self.bank_last.setdefault(r.t.tid, {})[ev[0]] = ev
        for r in reads:
            for g in r.grans():
                st = gr.get(g)
                if st is None:
                    gr[g] = [None, [ev]]
                else:
                    st[1].append(ev)
        for w in writes:
            for g in w.grans():
                gr[g] = [ev, []]

    def op(self, eng, fn, reads=(), writes=()):
        waits = self._deps(eng, reads, writes)
        seq = len(self.ops[eng])
        self.ops[eng].append([fn, waits, None])
        self._mark((eng, seq), reads, writes)

    def dma(self, q, semkey, out_ap, in_ap, reads=(), writes=(), final=False):
        waits = self._deps(q, reads, writes)
        n = self.dma_cnt.get(semkey, 0) + 1
        self.dma_cnt[semkey] = n
        ev = (("dma", semkey), n * 16)
        self.ops[q].append([lambda e: e.dma_start(out=out_ap, in_=in_ap), waits, ev])
        self._mark(ev, reads, writes)
        if final:
            self.final.append(ev)

    def cc(self, semkey, fn, reads=(), writes=()):
        waits = self._deps(POOL, reads, writes)
        n = self.cc_cnt.get(semkey, 0) + 1
        self.cc_cnt[semkey] = n
        ev = (("cc", semkey), n)
        self.ops[POOL].append([fn, waits, ev])
        self._mark(ev, reads, writes)

    def emit(self):
        nc = self.nc
        with ExitStack() as es:
            sems = {}
            for e in ENGS:
                sems[e] = es.enter_context(nc.semaphore("s_" + e))
            for k in self.dma_cnt:
                sems[("dma", k)] = es.enter_context(nc.semaphore("d_" + str(k)))
            for k in self.cc_cnt:
                sems[("cc", k)] = es.enter_context(nc.semaphore("c_" + str(k)))
            block = es.enter_context(nc.Block())
            val = {}
            for e in ENGS:
                c = 0
                nd = self.needed[e]
                for i in range(len(self.ops[e])):
                    if i in nd:
                        c += 1
                        val[(e, i)] = c

            def run(ename, eobj):
                nd = self.needed[ename]
                for i, (fn, waits, dmaev) in enumerate(self.ops[ename]):
                    for k, v in waits:
                        if isinstance(k, tuple):
                            eobj.wait_ge(sems[k], v)
                        else:
                            eobj.wait_ge(sems[k], val[(k, v)])
                    ins = fn(eobj)
                    if dmaev is not None and dmaev[0][0] == "cc":
                        ins.then_inc(sems[dmaev[0]])
                    elif dmaev is not None:
                        ins.then_inc(sems[dmaev[0]], 16)
                    elif i in nd:
                        ins.then_inc(sems[ename], 1)
                if ename == SP:
                    fin = {}
                    for k, v in self.final:
                        fin[k] = max(fin.get(k, 0), v)
                    for k, v in fin.items():
                        eobj.wait_ge(sems[k], v)

            @block.tensor
            def _(e):
                run(PE, e)

            @block.scalar
            def _(e):
                run(ACT, e)

            @block.vector
            def _(e):
                run(DVE, e)

            @block.gpsimd
            def _(e):
                run(POOL, e)

            @block.sync
            def _(e):
                run(SP, e)


def _regs(*xs):
    return [x for x in xs if isinstance(x, Reg)]


def _a(x):
    return x.ap if isinstance(x, Reg) else x


def mm(P, out, lhsT, rhs, start=True, stop=True):
    P.op(PE, lambda e: e.matmul(out.ap, lhsT.ap, rhs.ap, start=start, stop=stop), reads=[lhsT, rhs], writes=[out])


def tr(P, out, in_, ident):
    P.op(PE, lambda e: e.transpose(out.ap, in_.ap, ident.ap), reads=[in_, ident], writes=[out])


def act(P, out, in_, func, bias=None, scale=None, accum=None, eng=ACT):
    kw = {}
    if bias is not None:
        kw["bias"] = _a(bias)
    if scale is not None:
        kw["scale"] = _a(scale)
    if accum is not None:
        kw["accum_out"] = accum.ap
    P.op(eng, lambda e: e.activation(out.ap, in_.ap, func, **kw),
         reads=_regs(in_, bias, scale), writes=_regs(out, accum))


def tt(P, eng, out, in0, in1, op):
    P.op(eng, lambda e: e.tensor_tensor(out.ap, in0.ap, in1.ap, op), reads=[in0, in1], writes=[out])


def ts(P, eng, out, in0, s1, s2, op0, op1=None):
    if op1 is None:
        P.op(eng, lambda e: e.tensor_scalar(out.ap, in0.ap, _a(s1), None, op0), reads=_regs(in0, s1), writes=[out])
    else:
        P.op(eng, lambda e: e.tensor_scalar(out.ap, in0.ap, _a(s1), _a(s2), op0, op1),
             reads=_regs(in0, s1, s2), writes=[out])


def stt(P, eng, out, in0, scalar, in1, op0, op1):
    P.op(eng, lambda e: e.scalar_tensor_tensor(out.ap, in0.ap, _a(scalar), in1.ap, op0, op1),
         reads=_regs(in0, scalar, in1), writes=[out])


def cp(P, eng, out, in_):
    if eng == ACT:
        act(P, out, in_, AF.Copy)
    else:
        P.op(eng, lambda e: e.tensor_copy(out.ap, in_.ap), reads=[in_], writes=[out])


C_IDENT, C_TRI, C_NEG, C_QDEC, C_KDEC, C_GC, C_ONES, C_TRIR, NCST = 0, 128, 256, 384, 388, 392, 396, 524, 652
PT_AN, PT_MN, PT_RN, PT_SN, PT_GN, PT_CW, PT_CB, PT_MB, NPT = 0, 8, 16, 20, 28, 32, 80, 92, 116
BT_DTB, BT_ALOG, BT_D, BT_GB, NBT = 0, 16, 32, 48, 304

O_RQ, O_RK, O_RV, O_RG, O_SZ, O_SXBC, O_SDT, O_GQ, O_GK, O_GV, O_GR, O_GLR, O_MG = (
    0, 512, 1024, 1536, 2048, 3072, 4608, 4624, 4880, 5136, 5648, 6160, 6176)


def host_consts():
    c = np.zeros((128, NCST), np.float32)
    idx = np.arange(128)
    c[:, C_IDENT:C_IDENT + 128] = np.eye(128, dtype=np.float32)
    tri = (idx[:, None] <= idx[None, :]).astype(np.float32)
    c[:, C_TRI:C_TRI + 128] = tri
    c[:, C_NEG:C_NEG + 128] = (tri - 1.0) * 30000.0
    h = np.arange(4, dtype=np.float64)
    lg = np.log1p(-np.exp2(-5.0 - h))
    pos = idx.astype(np.float64)
    c[:, C_QDEC:C_QDEC + 4] = np.exp(lg[None, :] * (pos[:, None] + 1.0))
    c[:, C_KDEC:C_KDEC + 4] = np.exp(-lg[None, :] * (pos[:, None] + 1.0)) * (128.0 ** -0.5)
    c[:, C_GC:C_GC + 4] = np.exp(lg * 128.0)[None, :]
    c[:, C_ONES:C_ONES + 128] = 1.0
    c[:, C_TRIR:C_TRIR + 128] = (idx[:, None] > idx[None, :]).astype(np.float32)
    return c


def host_cs(seq):
    inv_freq = (10000.0 ** (-np.arange(0, 128, 2, dtype=np.float32) / np.float32(128))).astype(np.float32)
    ang = np.arange(seq, dtype=np.float32)[:, None] * inv_freq[None, :]
    return np.concatenate([np.cos(ang), np.sin(ang)], axis=1).astype(np.float32)


def _blk(w):
    K, N = w.shape
    kc = K // 128
    return np.ascontiguousarray(w.reshape(kc, 128, N).transpose(1, 0, 2).reshape(128, kc * N))


def host_layer_weights(inp, l):
    w_in = inp["w_in"][l]
    blocks = []
    add = lambda w: blocks.append(_blk(w))
    add(w_in[:, O_RQ:O_RQ + 512]); add(w_in[:, O_RK:O_RK + 512]); add(w_in[:, O_RV:O_RV + 512]); add(w_in[:, O_RG:O_RG + 512])
    add(inp["ret_w_o"][l])
    add(w_in[:, O_MG:O_MG + 512]); add(w_in[:, O_MG + 512:O_MG + 1024])
    add(w_in[:, O_SZ:O_SZ + 512]); add(w_in[:, O_SZ + 512:O_SZ + 1024])
    for j in range(3):
        add(w_in[:, O_SXBC + 512 * j:O_SXBC + 512 * (j + 1)])
    add(inp["ssd_w_o"][l][:, 0:512]); add(w_in[:, O_MG + 1024:O_MG + 1536])
    add(inp["ssd_w_o"][l][:, 512:1024]); add(w_in[:, O_MG + 1536:O_MG + 2048])
    add(w_in[:, O_GQ:O_GQ + 512]); add(w_in[:, O_GV:O_GV + 512]); add(w_in[:, O_GR:O_GR + 512])
    add(inp["gla_w_o"][l])
    add(w_in[:, O_MG + 2048:O_MG + 2560]); add(w_in[:, O_MG + 2560:O_MG + 3072])
    add(inp["w_out"][l][:, 0:512]); add(inp["w_out"][l][:, 512:1024])
    for g in range(4):
        add(inp["w_up"][l][:, g * 1024:g * 1024 + 512]); add(inp["w_up"][l][:, g * 1024 + 512:(g + 1) * 1024])
        add(inp["w_down"][l][g * 1024:(g + 1) * 1024, 0:512]); add(inp["w_down"][l][g * 1024:(g + 1) * 1024, 512:1024])
    assert len(blocks) == NBLK
    wst = np.stack(blocks, 0)
    sm = np.concatenate([w_in[:, O_SDT:O_SDT + 16], w_in[:, O_GLR:O_GLR + 16]], axis=1)
    wsm = _blk(sm)
    pt = np.zeros((128, NPT), np.float32)
    colmaj = lambda v: np.ascontiguousarray(v.reshape(-1, 128).T)
    pt[:, PT_AN:PT_AN + 8] = colmaj(inp["attn_norm_w"][l])
    pt[:, PT_MN:PT_MN + 8] = colmaj(inp["mlp_norm_w"][l])
    pt[:, PT_RN:PT_RN + 4] = colmaj(inp["ret_norm_w"][l])
    pt[:, PT_SN:PT_SN + 8] = colmaj(inp["ssd_norm_w"][l])
    pt[:, PT_GN:PT_GN + 4] = colmaj(inp["gla_norm_w"][l])
    cw = inp["ssd_conv_w"][l]
    pt[:, PT_CW:PT_CW + 48] = cw.T.reshape(12, 128, 4).transpose(1, 0, 2).reshape(128, 48)
    pt[:, PT_CB:PT_CB + 12] = colmaj(inp["ssd_conv_b"][l])
    pt[:, PT_MB:PT_MB + 24] = colmaj(inp["merge_gate_b"][l])
    bt = np.concatenate([inp["ssd_dt_bias"][l], inp["ssd_a_log"][l], inp["ssd_d"][l], inp["gla_gate_b"][l]]).astype(np.float32)
    return wst, wsm, pt, bt, np.ascontiguousarray(inp["gla_gate_w"][l])


STOP = [None]


class _Stop(Exception):
    pass


def build_program(n_tiles, n_layers, pp_groups=None):
    PP = pp_groups is not None
    n_steps = n_tiles + 1 if PP else n_tiles
    S = n_tiles * TT
    nc = bass.Bass("TRN2", target_bir_lowering=False)
    nc.dge_precook = False
    T._n = 0
    dram = lambda name, shape, dt, kind="ExternalInput": nc.dram_tensor(name, shape, dt, kind=kind).ap()
    x_d = dram("x", [S, D], F32)
    cs_d = dram("cs", [n_steps * TT, 128], F32)
    if PP:
        flag_d = dram("flag", [128, 2], F32)
        cc_in = nc.dram_tensor("cc_in", [128, 8 * TT], F32)
        gath = nc.dram_tensor("gath", [256, 8 * TT], F32)
    wst_d = dram("wst", [n_layers * NBLK, 128, 4096], F32R)
    wsm_d = dram("wsm", [n_layers, 128, 256], F32R)
    ptab_d = dram("ptab", [n_layers, 128, NPT], F32)
    btab_d = dram("btab", [n_layers, NBT], F32)
    gw_d = dram("gw", [n_layers, 16, 256], F32)
    fnw_d = dram("fnw", [128, 8], F32)
    fnwb_d = dram("fnwb", [D], F32)
    cst_d = dram("cst", [128, NCST], F32)
    cstr_d = dram("cstr", [128, 128], F32R)
    out_d = dram("out", [S, D], F32, kind="ExternalOutput")
    dbg_d = {}

    with ExitStack() as es:
        P = Prog(nc)

        def sb(name, cols, dt=F32, parts=128):
            return T(es.enter_context(nc.sbuf_tensor("sb_" + name, [parts, cols], dt)), cols, dt, parts)

        for i in range(8):
            P.banks.append(T(es.enter_context(nc.psum_tensor("psb%d" % i, [128, 512], F32)), 512, F32, psum=True))

        xres = sb("xres", 8 * TT)
        hbuf = sb("hbuf", 8 * TT, F32R)
        mrg = sb("mrg", 8 * TT, F32R)
        big = sb("big", 4096)
        bigr = big.as_dtype(F32R)
        NWB = 4 if PP else 3
        wbuf = [sb("wbuf%d" % i, 4096, F32R) for i in range(NWB)]
        mix = sb("mix", 10240, BF16)
        cst = sb("cst", NCST)
        cstr = sb("cstr", 128, F32R)
        identb = sb("identb", 128, BF16)
        csb = [sb("csb%d" % i, 512) for i in range(2)]
        fnw = sb("fnw", 8)
        flag = sb("flag", 2)
        fnw_bc = sb("fnw_bc", D)
        oss = sb("oss", 8)
        ptab = [sb("ptab%d" % l, NPT) for l in range(n_layers)]
        btab = [sb("btab%d" % l, NBT) for l in range(n_layers)]
        negA = [sb("negA%d" % l, 16) for l in range(n_layers)]
        gwt = [sb("gwt%d" % l, 256, F32, 16) for l in range(n_layers)]
        wsm = [sb("wsm%d" % l, 256, F32R) for l in range(n_layers)]
        retS = [sb("retS%d" % l, 512) for l in range(n_layers)]
        retSb = [sb("retSb%d" % l, 512, BF16) for l in range(n_layers)]
        ssdS = [sb("ssdS%d" % l, 1024) for l in range(n_layers)]
        ssdSb = [sb("ssdSb%d" % l, 1024, BF16) for l in range(n_layers)]
        glaS = [sb("glaS%d" % l, 256) for l in range(n_layers)]
        glaSb = [sb("glaSb%d" % l, 256, BF16) for l in range(n_layers)]
        halo = [sb("halo%d" % l, 48) for l in range(n_layers)]
        scr = sb("scr", 7168)
        scr_r = scr.as_dtype(F32R)
        scr_b = scr.as_dtype(BF16)
        K32 = lambda kb, cols, parts=128: Sub(scr, int(kb * 256), cols, parts)
        K32R = lambda kb, cols: Sub(scr_r, int(kb * 256), cols)
        KBF = lambda kb, cols: Sub(scr_b, int(kb * 512), cols)
        sq = [sb("sq0", TT, F32R), sb("sq1", TT, F32R)]
        nrm_a = K32(4, TT)
        nrm_r = K32(6, TT)
        gate_sb = [K32(8, TT), K32(10, TT)]
        gtmp = [K32(12, TT), K32(14, TT)]
        relu_t = [K32(16, TT), K32(18, TT)]
        rtmp = [K32(i, 256) for i in range(4)]
        rot = K32(4, 512)
        qT_r, kT_r, sT_r = KBF(6, 512), KBF(7, 512), KBF(8, 512)
        junk_r, og_r, stmp = K32(10, 512), K32(12, 512), K32(14, 512)
        stage = [K32(16, TT + 4), K32(18.5, TT + 4)]
        cacc = [K32(21, TT), K32(23, TT)]
        A1 = [K32(0, 512), K32(2, 512)]
        ET = [K32(4, 512), K32(6, 512)]
        GT = KBF(8, 2048)
        cb_sb = K32(12, 256)
        Xbf = KBF(13, 1024)
        XW = KBF(15, 1024)
        Bbf = KBF(17, 256)
        y1 = K32(18, 1024)
        y2 = K32(22, 1024)
        junk_s = K32(26, 512)
        ostage = [K32(16, 1024), K32(20, 1024)]
        ojunk = K32(24, 512)
        glr_sb = K32(0, TT, 16)
        lp, gtm, eb, enb, erev = K32(2, 256), K32(3, 256), K32(4, 256), K32(5, 256), K32(6, 256)
        qin, kin, kst = KBF(7, 256), KBF(7.5, 256), KBF(8, 256)
        qT_g, kT_g, sT_g = KBF(9, 512), KBF(10, 512), KBF(11, 512)
        junk_g, og_g = K32(12, 512), K32(14, 512)
        ss4 = sb("ss4", 8)
        rs4 = sb("rs4", 8)
        dt_sb = sb("dt_sb", 16)
        dtt = sb("dtt", 16)
        a_sb = sb("a_sb", 16)
        acs_sb = sb("acs_sb", 16)
        eacs = sb("eacs", 16)
        wdec = sb("wdec", 16)
        sdec = sb("sdec", 16)
        gdec = sb("gdec", 2)

        def C(c0, n, p0=0, p1=128):
            return cst.r(c0, c0 + n, p0, p1)

        ident = C(C_IDENT, 128)
        tri = C(C_TRI, 128)
        identb_r = identb.r()

        P.dma(SP, "cst", cst.r().ap, cst_d, writes=[cst.r()])
        P.dma(SP, "cstr", cstr.r().ap, cstr_d, writes=[cstr.r()])
        P.dma(SP, "fnw", fnw.r().ap, fnw_d, writes=[fnw.r()])
        P.dma(POOL, "fnwb", fnw_bc.r().ap, fnwb_d.partition_broadcast(128), writes=[fnw_bc.r()])
        if PP:
            P.dma(SP, "flag", flag.r().ap, flag_d, writes=[flag.r()])
            tin, tg = DTrack(), DTrack()
        for l in range(n_layers):
            P.dma(SP, "ptab", ptab[l].r().ap, ptab_d[l], writes=[ptab[l].r()])
            P.dma(POOL, "btab", btab[l].r().ap, btab_d[l].partition_broadcast(128), writes=[btab[l].r()])
            P.dma(SP, "gwt", gwt[l].r().ap, gw_d[l], writes=[gwt[l].r()])
            P.dma(SP, "wsm", wsm[l].r().ap, wsm_d[l], writes=[wsm[l].r()])
        cp(P, DVE, identb_r, ident)
        for l in range(n_layers):
            act(P, negA[l].r(), btab[l].r(BT_ALOG, BT_ALOG + 16), AF.Exp)
            ts(P, DVE, negA[l].r(), negA[l].r(), -1.0, None, ALU.mult)
            for t_ in (retS[l], ssdS[l], glaS[l], halo[l]):
                P.op(POOL, (lambda tt_: (lambda e: e.memset(tt_.r().ap, 0.0)))(t_), writes=[t_.r()])
            for t_ in (retSb[l], ssdSb[l], glaSb[l]):
                P.op(POOL, (lambda tt_: (lambda e: e.memset(tt_.r().ap, 0.0)))(t_), writes=[t_.r()])

        sched = []
        for ti in range(n_steps):
            for l in range(n_layers):
                for b in range(NBLK):
                    sched.append(l * NBLK + b)
        wstate = {"issued": 0, "used": 0}

        def w_issue():
            i = wstate["issued"]
            if i >= len(sched):
                return
            buf = wbuf[i % NWB]
            P.dma(SP, "w%d" % (i % NWB), buf.r().ap, wst_d[sched[i]], writes=[buf.r()])
            wstate["issued"] = i + 1

        def w_next():
            i = wstate["used"]
            wstate["used"] = i + 1
            return wbuf[i % NWB]

        def w_done():
            w_issue()

        for _ in range(NWB):
            w_issue()

        def xk(kc, c0=0, c1=TT):
            return xres.r(kc * TT + c0, kc * TT + c1)

        def hk(kc, c0=0, c1=TT):
            return hbuf.r(kc * TT + c0, kc * TT + c1)

        def rmsnorm(wcol, out_fn):
            ps = P.psum()
            for kc in range(8):
                s = sq[kc % 2]
                act(P, s.r(), xk(kc), AF.Square)
                mm(P, ps.r(), cstr.r(), s.r(), start=(kc == 0), stop=(kc == 7))
            act(P, nrm_a.r(), ps.r(), AF.Ln, bias=EPS)
            act(P, nrm_r.r(), nrm_a.r(), AF.Exp, scale=-0.5)
            for kc in range(8):
                stt(P, DVE, out_fn(kc), xk(kc), wcol(kc), nrm_r.r(), ALU.mult, ALU.mult)

        def proj_tm(wb, ncols, evac, kstride=512, col0=0):
            for c in range(NCH):
                ps = P.psum()
                for kc in range(8):
                    mm(P, ps.r(0, ncols), hk(kc, c * 128, (c + 1) * 128),
                       wb.r(kc * kstride + col0, kc * kstride + col0 + ncols), start=(kc == 0), stop=(kc == 7))
                evac(c, ps)

        def proj_fm(wb, rhs_fn, nk, kstride, col0, m=128):
            ps = P.psum()
            for kc in range(nk):
                mm(P, ps.r(0, TT, 0, m), wb.r(kc * kstride + col0, kc * kstride + col0 + m), rhs_fn(kc),
                   start=(kc == 0), stop=(kc == nk - 1))
            return ps

        def outT(kc, c0=0, c1=TT):
            return bigr.r(kc * TT + c0, kc * TT + c1)

        def group_post(ps_o, G_fn, nw_col, c, junk, og):
            for h in range(4):
                act(P, junk.r(h * 128, (h + 1) * 128), ps_o.r(h * 128, (h + 1) * 128), AF.Square,
                    scale=128.0 ** -0.5, accum=ss4.r(h, h + 1))
            act(P, rs4.r(0, 4), ss4.r(0, 4), AF.Ln, bias=EPS)
            act(P, rs4.r(4, 8), rs4.r(0, 4), AF.Exp, scale=-0.5)
            for h in range(4):
                stt(P, DVE, og.r(h * 128, (h + 1) * 128), ps_o.r(h * 128, (h + 1) * 128), rs4.r(4 + h, 5 + h),
                    G_fn(h), ALU.mult, ALU.mult)
            pt_ = P.psum()
            for h in range(4):
                tr(P, pt_.r(h * 128, (h + 1) * 128), og.r(h * 128, (h + 1) * 128), ident)
            for h in range(4):
                act(P, outT(h, c * 128, (c + 1) * 128), pt_.r(h * 128, (h + 1) * 128), AF.Copy, scale=nw_col(h))

        def merge_branch(br, l, wo, nk, kstride_o, wmg_blocks, dcs=range(8)):
            for dc in dcs:
                wo_b, col_o = wo(dc)
                ps_o = proj_fm(wo_b, lambda kc: outT(kc), nk, kstride_o, col_o)
                ps_g = proj_fm(wmg_blocks[dc // 4], lambda kc: hk(kc), 8, 512, (dc % 4) * 128)
                g = gate_sb[dc % 2]
                act(P, g.r(), ps_g.r(), AF.Sigmoid, bias=ptab[l].r(PT_MB + br * 8 + dc, PT_MB + br * 8 + dc + 1))
                dst = mrg.r(dc * TT, (dc + 1) * TT)
                if br == 0:
                    tt(P, DVE, dst, ps_o.r(), g.r(), ALU.mult)
                else:
                    tmp = gtmp[dc % 2]
                    tt(P, DVE, tmp.r(), ps_o.r(), g.r(), ALU.mult)
                    tt(P, POOL, dst, dst, tmp.r(), ALU.add)

        def b3(ap, a, b):
            return ap.unsqueeze(2).broadcast_to([ap.shape[0], a, b])

        def layer(l, ti, after_gla=None):
            pt = ptab[l]
            pcol = lambda base: (lambda k: pt.r(base + k, base + k + 1))
            csr = csb[ti % 2]
            rmsnorm(pcol(PT_AN), hk)

            if STOP[0] == 1:
                raise _Stop()
            Qt = lambda c, c0=0, c1=512: mix.r(c * 512 + c0, c * 512 + c1)
            Kt = lambda c, c0=0, c1=512: mix.r(2048 + c * 512 + c0, 2048 + c * 512 + c1)
            Vt = lambda c, c0=0, c1=512: mix.r(4096 + c * 512 + c0, 4096 + c * 512 + c1)
            Gt = lambda c, c0=0, c1=512: mix.r(6144 + c * 512 + c0, 6144 + c * 512 + c1)

            def rotary_evac(dst_fn, dec_c0):
                def f(c, ps):
                    v4 = lambda r_: r_.m(lambda ap: ap.rearrange("p (h t d) -> p h t d", h=4, t=2))
                    x1 = v4(ps.r()).m(lambda ap: ap[:, :, 0, :])
                    x2 = v4(ps.r()).m(lambda ap: ap[:, :, 1, :])
                    cos = csr.r(c * 128, c * 128 + 64).m(lambda ap: ap.unsqueeze(1).broadcast_to([128, 4, 64]))
                    sin = csr.r(c * 128 + 64, c * 128 + 128).m(lambda ap: ap.unsqueeze(1).broadcast_to([128, 4, 64]))
                    v3 = lambda r_: r_.m(lambda ap: ap.rearrange("p (h d) -> p h d", h=4))
                    t1, t2, t3, t4 = [v3(rtmp[i].r()) for i in range(4)]
                    tt(P, DVE, t1, x1, cos, ALU.mult)
                    tt(P, DVE, t2, x2, sin, ALU.mult)
                    tt(P, DVE, t3, x1, sin, ALU.mult)
                    tt(P, DVE, t4, x2, cos, ALU.mult)
                    r1 = v4(rot.r()).m(lambda ap: ap[:, :, 0, :])
                    r2 = v4(rot.r()).m(lambda ap: ap[:, :, 1, :])
                    tt(P, POOL, r1, t1, t2, ALU.subtract)
                    tt(P, POOL, r2, t3, t4, ALU.add)
                    dec = C(dec_c0, 4).m(lambda ap: b3(ap, 4, 128))
                    tt(P, POOL, dst_fn(c).m(lambda ap: ap.rearrange("p (h d) -> p h d", h=4)),
                       rot.r().m(lambda ap: ap.rearrange("p (h d) -> p h d", h=4)), dec, ALU.mult)
                return f

            wb = w_next(); proj_tm(wb, 512, rotary_evac(Qt, C_QDEC)); w_done()
            wb = w_next(); proj_tm(wb, 512, rotary_evac(Kt, C_KDEC)); w_done()
            wb = w_next(); proj_tm(wb, 512, lambda c, ps: cp(P, ACT, Vt(c), ps.r())); w_done()
            wb = w_next(); proj_tm(wb, 512, lambda c, ps: act(P, Gt(c), ps.r(), AF.Silu)); w_done()
            if STOP[0] == 2:
                raise _Stop()
            S_, Sb_ = retS[l], retSb[l]
            for c in range(NCH):
                pq = P.psum().as_dtype(BF16)
                for h in range(4):
                    tr(P, pq.r(h * 128, (h + 1) * 128), Qt(c, h * 128, (h + 1) * 128), identb_r)
                cp(P, ACT, qT_r.r(), pq.r(0, 512))
                pk = P.psum().as_dtype(BF16)
                for h in range(4):
                    tr(P, pk.r(h * 128, (h + 1) * 128), Kt(c, h * 128, (h + 1) * 128), identb_r)
                cp(P, DVE, kT_r.r(), pk.r(0, 512))
                psc = P.psum()
                for h in range(4):
                    mm(P, psc.r(h * 128, (h + 1) * 128), kT_r.r(h * 128, (h + 1) * 128), qT_r.r(h * 128, (h + 1) * 128))
                tt(P, DVE, sT_r.r().m(lambda ap: ap.rearrange("p (h d) -> p h d", h=4)),
                   psc.r().m(lambda ap: ap.rearrange("p (h d) -> p h d", h=4)),
                   tri.m(lambda ap: ap.unsqueeze(1).broadcast_to([128, 4, 128])), ALU.mult)
                po = P.psum()
                for h in range(4):
                    hs = slice(h * 128, (h + 1) * 128)
                    mm(P, po.r(h * 128, (h + 1) * 128), sT_r.r(h * 128, (h + 1) * 128), Vt(c, h * 128, (h + 1) * 128), True, False)
                    mm(P, po.r(h * 128, (h + 1) * 128), qT_r.r(h * 128, (h + 1) * 128), Sb_.r(h * 128, (h + 1) * 128), False, True)
                pds = P.psum()
                for h in range(4):
                    mm(P, pds.r(h * 128, (h + 1) * 128), Kt(c, h * 128, (h + 1) * 128), Vt(c, h * 128, (h + 1) * 128))
                tt(P, DVE, stmp.r(0, 512), pds.r(), S_.r(), ALU.add)
                tt(P, POOL, S_.r().m(lambda ap: ap.rearrange("p (h d) -> p h d", h=4)),
                   stmp.r(0, 512).m(lambda ap: ap.rearrange("p (h d) -> p h d", h=4)),
                   C(C_GC, 4).m(lambda ap: b3(ap, 4, 128)), ALU.mult)
                cp(P, POOL, Sb_.r(), S_.r())
                group_post(po, lambda h: Gt(c, h * 128, (h + 1) * 128), pcol(PT_RN), c, junk_r, og_r)
            if STOP[0] == 3:
                raise _Stop()
            wo_b = w_next()
            wmg = [w_next(), w_next()]
            merge_branch(0, l, lambda dc: (wo_b, dc * 128), 4, 1024, wmg)
            w_done(); w_done(); w_done()

            if STOP[0] == 4:
                raise _Stop()
            Zt = lambda c, c0=0, c1=1024: mix.r(c * 1024 + c0, c * 1024 + c1)
            XBC = lambda cc, c0=0, c1=TT: mix.r(4096 + cc * TT + c0, 4096 + cc * TT + c1)
            for half in range(2):
                wb = w_next()
                proj_tm(wb, 512, (lambda hf: (lambda c, ps: act(P, Zt(c, hf * 512, hf * 512 + 512), ps.r(), AF.Silu)))(half))
                w_done()
            for j in range(3):
                wb = w_next()
                for q in range(4):
                    cc = j * 4 + q
                    ps = proj_fm(wb, lambda kc: hk(kc), 8, 512, q * 128)
                    st = stage[cc % 2]
                    ca = cacc[cc % 2]
                    cp(P, POOL, st.r(0, 3), halo[l].r(cc * 4, cc * 4 + 3))
                    cp(P, ACT, st.r(3, 3 + TT), ps.r())
                    cp(P, POOL, halo[l].r(cc * 4, cc * 4 + 3), st.r(TT, TT + 3))
                    act(P, ca.r(), st.r(0, TT), AF.Copy, scale=pt.r(PT_CW + cc * 4, PT_CW + cc * 4 + 1))
                    for k in range(1, 4):
                        stt(P, DVE, ca.r(), st.r(k, k + TT), pt.r(PT_CW + cc * 4 + k, PT_CW + cc * 4 + k + 1),
                            ca.r(), ALU.mult, ALU.add)
                    act(P, XBC(cc), ca.r(), AF.Silu, bias=pt.r(PT_CB + cc, PT_CB + cc + 1))
                w_done()
            if STOP[0] == 5:
                raise _Stop()
            S_, Sb_ = ssdS[l], ssdSb[l]
            bt = btab[l]
            for c in range(NCH):
                cs_ = slice(c * 128, (c + 1) * 128)
                pdt = P.psum()
                for kc in range(8):
                    mm(P, pdt.r(0, 16), hk(kc, c * 128, (c + 1) * 128), wsm[l].r(kc * 32, kc * 32 + 16), kc == 0, kc == 7)
                tt(P, DVE, dtt.r(), pdt.r(0, 16), bt.r(BT_DTB, BT_DTB + 16), ALU.add)
                act(P, dtt.r(), dtt.r(), AF.Exp)
                act(P, dt_sb.r(), dtt.r(), AF.Ln, bias=1.0)
                tt(P, DVE, a_sb.r(), dt_sb.r(), negA[l].r(), ALU.mult)
                pacs = P.psum()
                mm(P, pacs.r(0, 16), tri, a_sb.r())
                cp(P, DVE, acs_sb.r(), pacs.r(0, 16))
                act(P, eacs.r(), pacs.r(0, 16), AF.Exp)
                pcb = P.psum()
                for g in range(2):
                    mm(P, pcb.r(g * 128, (g + 1) * 128), XBC(8 + g, c * 128, (c + 1) * 128), XBC(10 + g, c * 128, (c + 1) * 128))
                cp(P, ACT, cb_sb.r(), pcb.r(0, 256))
                for hg in range(4):
                    a1 = A1[hg % 2]
                    tt(P, POOL, a1.r().m(lambda ap: ap.rearrange("p (h l) -> p h l", h=4)),
                       a_sb.r(hg * 4, hg * 4 + 4).m(lambda ap: b3(ap, 4, 128)),
                       tri.m(lambda ap: ap.unsqueeze(1).broadcast_to([128, 4, 128])), ALU.mult)
                    pb = P.psum()
                    mm(P, pb.r(), C(C_ONES, 128), a1.r())
                    et = ET[hg % 2]
                    for hh in range(4):
                        h = hg * 4 + hh
                        stt(P, DVE, et.r(hh * 128, (hh + 1) * 128), pb.r(hh * 128, (hh + 1) * 128), acs_sb.r(h, h + 1),
                            C(C_NEG, 128), ALU.subtract, ALU.add)
                    cp(P, DVE, sdec.r(hg * 4, hg * 4 + 4),
                       pb.r().m(lambda ap: ap.rearrange("p (h l) -> p h l", h=4)[:, :, 127]))
                    lt = et
                    act(P, lt.r(), et.r(), AF.Exp)
                    g = hg // 2
                    v4_ = lambda r_: r_.m(lambda ap: ap.rearrange("p (h l) -> p h l", h=4))
                    tt(P, DVE, v4_(lt.r()), v4_(lt.r()),
                       cb_sb.r(g * 128, (g + 1) * 128).m(lambda ap: ap.unsqueeze(1).broadcast_to([128, 4, 128])), ALU.mult)
                    tt(P, POOL, v4_(GT.r(hg * 512, (hg + 1) * 512)), v4_(lt.r()),
                       dt_sb.r(hg * 4, hg * 4 + 4).m(lambda ap: b3(ap, 4, 128)), ALU.mult)
                tt(P, DVE, wdec.r(), sdec.r(), acs_sb.r(), ALU.subtract)
                act(P, wdec.r(), wdec.r(), AF.Exp)
                tt(P, DVE, wdec.r(), wdec.r(), dt_sb.r(), ALU.mult)
                act(P, sdec.r(), sdec.r(), AF.Exp)
                px = P.psum().as_dtype(BF16)
                for hc in range(8):
                    tr(P, px.r(hc * 128, (hc + 1) * 128), XBC(hc, c * 128, (c + 1) * 128), identb_r)
                cp(P, ACT, Xbf.r(), px.r(0, 1024))
                tt(P, DVE, XW.r().m(lambda ap: ap.rearrange("p (h d) -> p h d", h=16)),
                   px.r(0, 1024).m(lambda ap: ap.rearrange("p (h d) -> p h d", h=16)),
                   wdec.r().m(lambda ap: b3(ap, 16, 64)), ALU.mult)
                pbt = P.psum().as_dtype(BF16)
                for g in range(2):
                    tr(P, pbt.r(g * 128, (g + 1) * 128), XBC(8 + g, c * 128, (c + 1) * 128), identb_r)
                cp(P, ACT, Bbf.r(), pbt.r(0, 256))
                pyd = [P.psum(), P.psum()]
                for h in range(16):
                    g, hl = h // 8, h % 8
                    mm(P, pyd[g].r(hl * 64, (hl + 1) * 64), GT.r(h * 128, (h + 1) * 128), Xbf.r(h * 64, (h + 1) * 64))
                pyo = [P.psum(), P.psum()]
                for g in range(2):
                    mm(P, pyo[g].r(), XBC(10 + g, c * 128, (c + 1) * 128), Sb_.r(g * 512, (g + 1) * 512))
                for g in range(2):
                    gs = slice(g * 512, (g + 1) * 512)
                    v8 = lambda r_: r_.m(lambda ap: ap.rearrange("p (h d) -> p h d", h=8))
                    tt(P, DVE, v8(y1.r(g * 512, (g + 1) * 512)), v8(pyo[g].r()),
                       eacs.r(g * 8, g * 8 + 8).m(lambda ap: b3(ap, 8, 64)), ALU.mult)
                    tt(P, DVE, y1.r(g * 512, (g + 1) * 512), y1.r(g * 512, (g + 1) * 512), pyd[g].r(), ALU.add)
                    tt(P, POOL, v8(y2.r(g * 512, (g + 1) * 512)), v8(Xbf.r(g * 512, (g + 1) * 512)),
                       bt.r(BT_D + g * 8, BT_D + g * 8 + 8).m(lambda ap: b3(ap, 8, 64)), ALU.mult)
                    tt(P, POOL, y1.r(g * 512, (g + 1) * 512), y1.r(g * 512, (g + 1) * 512), y2.r(g * 512, (g + 1) * 512), ALU.add)
                    tt(P, DVE, y1.r(g * 512, (g + 1) * 512), y1.r(g * 512, (g + 1) * 512), Zt(c, g * 512, (g + 1) * 512), ALU.mult)
                    act(P, junk_s.r(), y1.r(g * 512, (g + 1) * 512), AF.Square, scale=512.0 ** -0.5, accum=ss4.r(g, g + 1))
                act(P, rs4.r(0, 2), ss4.r(0, 2), AF.Ln, bias=EPS)
                act(P, rs4.r(4, 6), rs4.r(0, 2), AF.Exp, scale=-0.5)
                for g in range(2):
                    ts(P, DVE, y2.r(g * 512, (g + 1) * 512), y1.r(g * 512, (g + 1) * 512), rs4.r(4 + g, 5 + g), None, ALU.mult)
                for g in range(2):
                    pt_ = P.psum()
                    for hl in range(4):
                        hc = g * 4 + hl
                        tr(P, pt_.r(hl * 128, (hl + 1) * 128), y2.r(hc * 128, (hc + 1) * 128), ident)
                    for hl in range(4):
                        hc = g * 4 + hl
                        act(P, outT(hc, c * 128, (c + 1) * 128), pt_.r(hl * 128, (hl + 1) * 128), AF.Copy,
                            scale=pt.r(PT_SN + hc, PT_SN + hc + 1))
                pds = [P.psum(), P.psum()]
                for g in range(2):
                    mm(P, pds[g].r(), Bbf.r(g * 128, (g + 1) * 128), XW.r(g * 512, (g + 1) * 512))
                for g in range(2):
                    v8 = lambda r_: r_.m(lambda ap: ap.rearrange("p (h d) -> p h d", h=8))
                    tt(P, POOL, v8(S_.r(g * 512, (g + 1) * 512)), v8(S_.r(g * 512, (g + 1) * 512)),
                       sdec.r(g * 8, g * 8 + 8).m(lambda ap: b3(ap, 8, 64)), ALU.mult)
                    tt(P, DVE, S_.r(g * 512, (g + 1) * 512), S_.r(g * 512, (g + 1) * 512), pds[g].r(), ALU.add)
                    cp(P, ACT, Sb_.r(g * 512, (g + 1) * 512), S_.r(g * 512, (g + 1) * 512))
            if STOP[0] == 6:
                raise _Stop()
            for q in range(2):
                wo_q = w_next()
                wmg_q = w_next()
                merge_branch(1, l, (lambda wq: (lambda dc: (wq, (dc % 4) * 128)))(wo_q), 8, 512, [wmg_q, wmg_q], range(q * 4, q * 4 + 4))
                w_done(); w_done()

            if STOP[0] == 7:
                raise _Stop()
            GQK = lambda c, c0=0, c1=512: mix.r(c * 512 + c0, c * 512 + c1)
            GV = lambda c, c0=0, c1=512: mix.r(2048 + c * 512 + c0, 2048 + c * 512 + c1)
            GG = lambda c, c0=0, c1=512: mix.r(4096 + c * 512 + c0, 4096 + c * 512 + c1)
            wb = w_next(); proj_tm(wb, 512, lambda c, ps: cp(P, ACT, GQK(c), ps.r())); w_done()
            wb = w_next(); proj_tm(wb, 512, lambda c, ps: cp(P, ACT, GV(c), ps.r())); w_done()
            wb = w_next(); proj_tm(wb, 512, lambda c, ps: act(P, GG(c), ps.r(), AF.Silu)); w_done()
            pg = P.psum()
            for kc in range(8):
                mm(P, pg.r(0, TT, 0, 16), wsm[l].r(kc * 32 + 16, kc * 32 + 32), hk(kc), kc == 0, kc == 7)
            cp(P, ACT, glr_sb.r(), pg.r(0, TT, 0, 16))
            if STOP[0] == 8:
                raise _Stop()
            S_, Sb_ = glaS[l], glaSb[l]
            for c in range(NCH):
                pga = P.psum()
                mm(P, pga.r(0, 256), glr_sb.r(c * 128, (c + 1) * 128), gwt[l].r())
                tt(P, DVE, gtm.r(), pga.r(0, 256), bt.r(BT_GB, BT_GB + 256), ALU.add)
                act(P, gtm.r(), gtm.r(), AF.Exp, scale=-1.0)
                act(P, lp.r(), gtm.r(), AF.Ln, bias=1.0)
                if STOP[0] == 81:
                    raise _Stop()
                pb_ = P.psum()
                mm(P, pb_.r(0, 256), tri, lp.r())
                mm(P, pb_.r(256, 512), C(C_TRIR, 128), lp.r())
                if STOP[0] == 82:
                    raise _Stop()
                ptot = P.psum()
                for pr in range(2):
                    mm(P, ptot.r(pr, pr + 1), lp.r(pr * 128, (pr + 1) * 128), C(C_ONES, 1))
                if STOP[0] == 83:
                    raise _Stop()
                act(P, eb.r(), pb_.r(0, 256), AF.Exp, scale=-1.0 / 16.0)
                act(P, enb.r(), pb_.r(0, 256), AF.Exp, scale=1.0 / 16.0)
                act(P, erev.r(), pb_.r(256, 512), AF.Exp, scale=-1.0 / 16.0)
                act(P, gdec.r(), ptot.r(0, 2), AF.Exp, scale=-1.0 / 16.0)
                if STOP[0] == 84:
                    raise _Stop()
                stt(P, DVE, qin.r(), GQK(c, 0, 256), 0.125, eb.r(), ALU.mult, ALU.mult)
                tt(P, POOL, kin.r(), GQK(c, 256, 512), enb.r(), ALU.mult)
                tt(P, POOL, kst.r(), GQK(c, 256, 512), erev.r(), ALU.mult)
                if STOP[0] == 85:
                    raise _Stop()
                pq = P.psum().as_dtype(BF16)
                for pr in range(2):
                    tr(P, pq.r(pr * 128, (pr + 1) * 128), qin.r(pr * 128, (pr + 1) * 128), identb_r)
                    tr(P, pq.r(256 + pr * 128, 256 + (pr + 1) * 128), kin.r(pr * 128, (pr + 1) * 128), identb_r)
                cp(P, ACT, qT_g.r(0, 256), pq.r(0, 256))
                cp(P, DVE, kT_g.r(0, 256), pq.r(256, 512))
                if STOP[0] == 86:
                    raise _Stop()
                psc2 = [P.psum(), P.psum()]
                for h in range(4):
                    pr, hl = h // 2, h % 2
                    mm(P, psc2[hl].r(pr * 128, (pr + 1) * 128), kT_g.r(pr * 128, (pr + 1) * 128, hl * 64, (hl + 1) * 64),
                       qT_g.r(pr * 128, (pr + 1) * 128, hl * 64, (hl + 1) * 64))
                for hl in range(2):
                    tt(P, DVE, sT_g.r().m((lambda hl_: (lambda ap: ap.rearrange("p (pr hl d) -> p pr hl d", pr=2, hl=2)[:, :, hl_, :]))(hl)),
                       psc2[hl].r(0, 256).m(lambda ap: ap.rearrange("p (h d) -> p h d", h=2)),
                       tri.m(lambda ap: ap.unsqueeze(1).broadcast_to([128, 2, 128])), ALU.mult)
                if STOP[0] == 87:
                    raise _Stop()
                po = P.psum()
                for h in range(4):
                    pr, hl = h // 2, h % 2
                    mm(P, po.r(h * 128, (h + 1) * 128), sT_g.r(h * 128, (h + 1) * 128), GV(c, h * 128, (h + 1) * 128), True, False)
                    mm(P, po.r(h * 128, (h + 1) * 128), qT_g.r(pr * 128, (pr + 1) * 128, hl * 64, (hl + 1) * 64),
                       Sb_.r(pr * 128, (pr + 1) * 128, hl * 64, (hl + 1) * 64), False, True)
                if STOP[0] == 88:
                    raise _Stop()
                pds = P.psum()
                for pr in range(2):
                    mm(P, pds.r(pr * 256, (pr + 1) * 256), kst.r(pr * 128, (pr + 1) * 128), GV(c, pr * 256, (pr + 1) * 256))
                for h in range(4):
                    pr, hl = h // 2, h % 2
                    stt(P, DVE, S_.r(pr * 128, (pr + 1) * 128, hl * 64, (hl + 1) * 64),
                        S_.r(pr * 128, (pr + 1) * 128, hl * 64, (hl + 1) * 64),
                        gdec.r(pr, pr + 1, hl * 64, (hl + 1) * 64),
                        pds.r(pr * 256 + hl * 128, pr * 256 + (hl + 1) * 128, hl * 64, (hl + 1) * 64), ALU.mult, ALU.add)
                if STOP[0] == 89:
                    raise _Stop()
                cp(P, POOL, Sb_.r(), S_.r())
                group_post(po, lambda h: GG(c, h * 128, (h + 1) * 128), pcol(PT_GN), c, junk_g, og_g)
            if STOP[0] == 9:
                raise _Stop()
            wo_b = w_next()
            wmg = [w_next(), w_next()]
            merge_branch(2, l, lambda dc: (wo_b, dc * 128), 4, 1024, wmg)
            w_done(); w_done(); w_done()
            if after_gla is not None:
                after_gla()

            if STOP[0] == 10:
                raise _Stop()
            wo = [w_next(), w_next()]
            for dc in range(8):
                ps = proj_fm(wo[dc // 4], lambda kc: mrg.r(kc * TT, (kc + 1) * TT), 8, 512, (dc % 4) * 128)
                tt(P, DVE, xk(dc), xk(dc), ps.r(), ALU.add)
            w_done(); w_done()

            if STOP[0] == 11:
                raise _Stop()
            rmsnorm(pcol(PT_MN), hk)
            hid = lambda j: bigr.r(j * TT, (j + 1) * TT)
            for g in range(4):
                wu = [w_next(), w_next()]
                for j in range(8):
                    ps = proj_fm(wu[j // 4], lambda kc: hk(kc), 8, 512, (j % 4) * 128)
                    rt = relu_t[j % 2]
                    act(P, rt.r(), ps.r(), AF.Relu)
                    act(P, hid(j), rt.r(), AF.Square)
                w_done(); w_done()
                wd = [w_next(), w_next()]
                for dc in range(8):
                    ps = proj_fm(wd[dc // 4], hid, 8, 512, (dc % 4) * 128)
                    tt(P, DVE, xk(dc), xk(dc), ps.r(), ALU.add)
                w_done(); w_done()

        xin = mix.as_dtype(F32)

        def load_inputs(ti):
            t0 = min(ti, n_tiles - 1) * TT
            P.dma(SP, "xin", xin.r(0, 4096).m(lambda ap: ap.rearrange("p (c d) -> p c d", c=NCH)).ap,
                  x_d[t0:t0 + TT, :].rearrange("(c p) d -> p c d", p=128), writes=[xin.r(0, 4096)])
            csr = csb[ti % 2]
            P.dma(SP, "cs%d" % (ti % 2), csr.r().m(lambda ap: ap.rearrange("p (c d) -> p c d", c=NCH)).ap,
                  cs_d[ti * TT:(ti + 1) * TT, :].rearrange("(c p) d -> p c d", p=128), writes=[csr.r()])

        load_inputs(0)
        for ti in range(n_steps):
            blend = PP and ti >= 1
            if blend:
                P.dma(POOL, "gin", scr.r(0, 4096).ap, gath.ap()[0:128, :], reads=[tg.r()], writes=[scr.r(0, 4096)])
                for dc in range(8):
                    if dc % 2:
                        act(P, scr.r(dc * TT, (dc + 1) * TT), scr.r(dc * TT, (dc + 1) * TT), AF.Copy, scale=flag.r(1, 2))
                    else:
                        ts(P, DVE, scr.r(dc * TT, (dc + 1) * TT), scr.r(dc * TT, (dc + 1) * TT), flag.r(1, 2), None, ALU.mult)
            for dc in range(8):
                ps = P.psum()
                for c in range(NCH):
                    tr(P, ps.r(c * 128, (c + 1) * 128), xin.r(c * D + dc * 128, c * D + (dc + 1) * 128), ident)
                if blend:
                    stt(P, DVE, xk(dc), ps.r(), flag.r(0, 1), scr.r(dc * TT, (dc + 1) * TT), ALU.mult, ALU.add)
                else:
                    cp(P, ACT if dc % 2 else DVE, xk(dc), ps.r())
            nxt = (lambda t_: (lambda: load_inputs(t_)))(ti + 1) if ti + 1 < n_steps else None
            for l in range(n_layers):
                try:
                    layer(l, ti, nxt if l == n_layers - 1 else None)
                except _Stop:
                    pass
            if PP:
                P.dma(POOL, "ccin", cc_in.ap(), xres.r().ap, reads=[xres.r()], writes=[tin.r()])
                P.cc("ag", lambda e: e.collective_compute("AllGather", ALU.bypass, replica_groups=pp_groups,
                                                         ins=[cc_in.ap().opt()], outs=[gath.ap().opt()]),
                     reads=[tin.r()], writes=[tg.r()])
                if ti == 0:
                    for l in range(n_layers):
                        for t_ in (retS[l], retSb[l], ssdS[l], ssdSb[l], glaS[l], glaSb[l], halo[l]):
                            ts(P, DVE, t_.r(), t_.r(), flag.r(0, 1), None, ALU.mult)
                    continue
            o0 = (ti - 1) * TT if PP else ti * TT
            for c in range(NCH):
                pss = [P.psum(), P.psum()]
                for q in range(2):
                    for j in range(4):
                        dc = q * 4 + j
                        tr(P, pss[q].r(j * 128, (j + 1) * 128), xk(dc, c * 128, (c + 1) * 128), ident)
                for q in range(2):
                    act(P, ojunk.r(), pss[q].r(), AF.Square, scale=1.0 / 32.0, accum=oss.r(q, q + 1))
                tt(P, DVE, oss.r(2, 3), oss.r(0, 1), oss.r(1, 2), ALU.add)
                act(P, oss.r(3, 4), oss.r(2, 3), AF.Ln, bias=EPS)
                act(P, oss.r(4, 5), oss.r(3, 4), AF.Exp, scale=-0.5)
                stg = ostage[c % 2]
                for q in range(2):
                    stt(P, DVE, stg.r(q * 512, (q + 1) * 512), pss[q].r(), oss.r(4, 5), fnw_bc.r(q * 512, (q + 1) * 512),
                        ALU.mult, ALU.mult)
                P.dma(SP, "xout%d" % (c % 2), out_d[o0 + c * 128:o0 + (c + 1) * 128, :], stg.r().ap, reads=[stg.r()], final=True)
        P.emit()
    return nc


_CACHE = {}


def _get_prog(n_tiles, n_layers, groups):
    key = (n_tiles, n_layers, str(groups))
    if key not in _CACHE:
        _CACHE[key] = build_program(n_tiles, n_layers, groups)
    return _CACHE[key]


def _layer_maps(inp, layers):
    ws, wsms, pts, bts, gws = [], [], [], [], []
    for l in layers:
        wst, wsm, pt, bt, gw = host_layer_weights(inp, l)
        ws.append(wst); wsms.append(wsm); pts.append(pt); bts.append(bt); gws.append(gw)
    return {
        "wst": np.ascontiguousarray(np.concatenate(ws, 0)),
        "wsm": np.stack(wsms, 0),
        "ptab": np.stack(pts, 0),
        "btab": np.stack(bts, 0),
        "gw": np.stack(gws, 0),
        "fnw": np.ascontiguousarray(np.asarray(inp["final_norm_w"], np.float32).reshape(8, 128).T),
        "fnwb": np.ascontiguousarray(np.asarray(inp["final_norm_w"], np.float32)),
        "cst": host_consts(),
        "cstr": np.full((128, 128), 1.0 / 1024.0, np.float32),
    }


def make_in_maps(inp, seqs, n_layers):
    S = seqs[0].shape[0]
    common = _layer_maps(inp, range(n_layers))
    common["cs"] = host_cs(S)
    return [dict(common, x=np.ascontiguousarray(s, dtype=np.float32)) for s in seqs]


def make_in_maps_pp(inp, seqs):
    n = len(seqs)
    S = seqs[0].shape[0]
    cs = host_cs(S)
    pad = np.zeros((TT, 128), np.float32)
    la = _layer_maps(inp, [0])
    lb = _layer_maps(inp, [1])
    fa = np.zeros((128, 2), np.float32); fa[:, 0] = 1.0
    fb = np.zeros((128, 2), np.float32); fb[:, 1] = 1.0
    zeros = np.zeros((S, D), np.float32)
    maps = []
    for i in range(n):
        maps.append(dict(la, x=np.ascontiguousarray(seqs[i], dtype=np.float32), cs=np.concatenate([cs, pad], 0), flag=fa))
    for i in range(n):
        maps.append(dict(lb, x=zeros, cs=np.concatenate([pad, cs], 0), flag=fb))
    return maps


def kernel(**inputs):
    inp = {k: np.asarray(v) for k, v in inputs.items()}
    x = inp["x"].astype(np.float32, copy=False)
    B, S, _ = x.shape
    groups = [[b, b + B] for b in range(B)]
    nc = _get_prog(S // TT, 1, groups)
    in_maps = make_in_maps_pp(inp, [x[b] for b in range(B)])
    res = run_bass_kernel_spmd(nc, in_maps, core_ids=list(range(2 * B)))
    out = np.stack([res.results[B + b]["out"] for b in range(B)], 0)
    return out.astype(np.float32, copy=False)
```

```python
import math
from contextlib import ExitStack

import numpy as np
import concourse.bass as bass
import concourse.mybir as mybir
from concourse.bass_utils import run_bass_kernel_spmd

F32 = mybir.dt.float32
BF16 = mybir.dt.bfloat16
F32R = mybir.dt.float32r
AF = mybir.ActivationFunctionType
ALU = mybir.AluOpType
GRAN = 512
ESZ = {F32: 4, BF16: 2, F32R: 4}
PE, ACT, DVE, POOL, SP = "tensor", "scalar", "vector", "gpsimd", "sync"
ENGS = [PE, ACT, DVE, POOL, SP]

D = 1024
TT = 512
NCH = 4
EPS = 1e-6
NBLK = 40
N_CORES = 8


class Reg:
    __slots__ = ("t", "c0", "c1", "ap")

    def __init__(self, t, c0, c1, ap):
        self.t, self.c0, self.c1, self.ap = t, c0, c1, ap

    def grans(self):
        e = self.t.esz
        tid = self.t.tid
        return [(tid, g) for g in range((self.c0 * e) // GRAN, (self.c1 * e - 1) // GRAN + 1)]

    def m(self, fn):
        return Reg(self.t, self.c0, self.c1, fn(self.ap))


class T:
    _n = 0

    def __init__(self, handle, cols, dtype, parts=128, psum=False):
        self.h, self.cols, self.dtype, self.esz, self.parts = handle, cols, dtype, ESZ[dtype], parts
        T._n += 1
        self.tid = T._n
        self.psum = psum

    def r(self, c0=0, c1=None, p0=0, p1=None):
        c1 = self.cols if c1 is None else c1
        p1 = self.parts if p1 is None else p1
        return Reg(self, c0, c1, self.h[p0:p1, c0:c1])

    def as_dtype(self, dtype):
        t = T.__new__(T)
        t.h = self.h.bitcast(dtype)
        t.dtype, t.esz, t.parts, t.tid = dtype, ESZ[dtype], self.parts, self.tid
        t.psum = self.psum
        t.cols = self.cols * self.esz // t.esz
        return t


class Sub:
    def __init__(self, parent, off, cols, parts=128):
        self.p, self.off, self.cols, self.parts = parent, off, cols, parts

    def r(self, c0=0, c1=None, p0=0, p1=None):
        c1 = self.cols if c1 is None else c1
        p1 = self.parts if p1 is None else p1
        return self.p.r(self.off + c0, self.off + c1, p0, p1)


class DTrack:
    def __init__(self):
        T._n += 1
        self.tid, self.esz, self.psum = T._n, 4, False

    def r(self):
        return Reg(self, 0, 1, None)


class Prog:
    def __init__(self, nc):
        self.nc = nc
        self.ops = {e: [] for e in ENGS}
        self.gr = {}
        self.waited = {e: {} for e in ENGS}
        self.needed = {e: set() for e in ENGS}
        self.dma_cnt = {}
        self.cc_cnt = {}
        self.final = []
        self.banks = []
        self.bank_i = 0
        self.bank_last = {}
        self.phase = None
        self.annotate = False

    def psum(self):
        b = self.banks[self.bank_i % len(self.banks)]
        self.bank_i += 1
        return b

    def _deps(self, eng, reads, writes):
        deps = {}

        def add(ev):
            if ev is None:
                return
            k, v = ev
            if k == PE and eng == PE:
                return
            if deps.get(k, -1) < v:
                deps[k] = v
        gr = self.gr
        for r in reads:
            for g in r.grans():
                st = gr.get(g)
                if st is not None:
                    add(st[0])
        for w in writes:
            for g in w.grans():
                st = gr.get(g)
                if st is not None:
                    add(st[0])
                    for ev in st[1]:
                        add(ev)
        for r in list(reads) + list(writes):
            if r.t.psum:
                bl = self.bank_last.get(r.t.tid)
                if bl:
                    for e2, ev in bl.items():
                        if e2 != eng:
                            add(ev)
        out = []
        wd = self.waited[eng]
        for k, v in deps.items():
            if wd.get(k, -1) >= v:
                continue
            wd[k] = v
            out.append((k, v))
            if k in self.needed:
                self.needed[k].add(v)
        return out

    def _mark(self, ev, reads, writes):
        gr = self.gr
        for r in list(reads) + list(writes):
            if r.t.psum:
                self.bank_last.setdefault(r.t.tid, {})[ev[0]] = ev
        for r in reads:
            for g in r.grans():
                st = gr.get(g)
                if st is None:
                    gr[g] = [None, [ev]]
                else:
                    st[1].append(ev)
        for w in writes:
            for g in w.grans():
                gr[g] = [ev, []]

    def op(self, eng, fn, reads=(), writes=()):
        waits = self._deps(eng, reads, writes)
        seq = len(self.ops[eng])
        self.ops[eng].append([fn, waits, None, self.phase])
        self._mark((eng, seq), reads, writes)

    def dma(self, q, semkey, out_ap, in_ap, reads=(), writes=(), final=False):
        waits = self._deps(q, reads, writes)
        n = self.dma_cnt.get(semkey, 0) + 1
        self.dma_cnt[semkey] = n
        ev = (("dma", semkey), n * 16)
        self.ops[q].append([lambda e: e.dma_start(out=out_ap, in_=in_ap), waits, ev, self.phase])
        self._mark(ev, reads, writes)
        if final:
            self.final.append(ev)

    def cc(self, semkey, fn, reads=(), writes=()):
        waits = self._deps(POOL, reads, writes)
        n = self.cc_cnt.get(semkey, 0) + 1
        self.cc_cnt[semkey] = n
        ev = (("cc", semkey), n)
        self.ops[POOL].append([fn, waits, ev, self.phase])
        self._mark(ev, reads, writes)

    def emit(self):
        nc = self.nc
        with ExitStack() as es:
            sems = {}
            for e in ENGS:
                sems[e] = es.enter_context(nc.semaphore("s_" + e))
            for k in self.dma_cnt:
                sems[("dma", k)] = es.enter_context(nc.semaphore("d_" + str(k)))
            for k in self.cc_cnt:
                sems[("cc", k)] = es.enter_context(nc.semaphore("c_" + str(k)))
            block = es.enter_context(nc.Block())
            val = {}
            for e in ENGS:
                c = 0
                nd = self.needed[e]
                for i in range(len(self.ops[e])):
                    if i in nd:
                        c += 1
                        val[(e, i)] = c

            def run(ename, eobj):
                nd = self.needed[ename]
                for i, (fn, waits, dmaev, phase) in enumerate(self.ops[ename]):
                    for k, v in waits:
                        if isinstance(k, tuple):
                            eobj.wait_ge(sems[k], v)
                        else:
                            eobj.wait_ge(sems[k], val[(k, v)])
                    ins = fn(eobj)
                    if self.annotate and phase:
                        ins.annotate(phase)
                    if dmaev is not None and dmaev[0][0] == "cc":
                        ins.then_inc(sems[dmaev[0]])
                    elif dmaev is not None:
                        ins.then_inc(sems[dmaev[0]], 16)
                    elif i in nd:
                        ins.then_inc(sems[ename], 1)
                if ename == SP:
                    fin = {}
                    for k, v in self.final:
                        fin[k] = max(fin.get(k, 0), v)
                    for k, v in fin.items():
                        eobj.wait_ge(sems[k], v)

            @block.tensor
            def _(e):
                run(PE, e)

            @block.scalar
            def _(e):
                run(ACT, e)

            @block.vector
            def _(e):
                run(DVE, e)

            @block.gpsimd
            def _(e):
                run(POOL, e)

            @block.sync
            def _(e):
                run(SP, e)


def _regs(*xs):
    return [x for x in xs if isinstance(x, Reg)]


def _a(x):
    return x.ap if isinstance(x, Reg) else x


def mm(P, out, lhsT, rhs, start=True, stop=True):
    P.op(PE, lambda e: e.matmul(out.ap, lhsT.ap, rhs.ap, start=start, stop=stop), reads=[lhsT, rhs], writes=[out])


def tr(P, out, in_, ident):
    P.op(PE, lambda e: e.transpose(out.ap, in_.ap, ident.ap), reads=[in_, ident], writes=[out])


def act(P, out, in_, func, bias=None, scale=None, accum=None, eng=ACT):
    kw = {}
    if bias is not None:
        kw["bias"] = _a(bias)
    if scale is not None:
        kw["scale"] = _a(scale)
    if accum is not None:
        kw["accum_out"] = accum.ap
    P.op(eng, lambda e: e.activation(out.ap, in_.ap, func, **kw),
         reads=_regs(in_, bias, scale), writes=_regs(out, accum))


def tt(P, eng, out, in0, in1, op):
    P.op(eng, lambda e: e.tensor_tensor(out.ap, in0.ap, in1.ap, op), reads=[in0, in1], writes=[out])


def ts(P, eng, out, in0, s1, s2, op0, op1=None):
    if op1 is None:
        P.op(eng, lambda e: e.tensor_scalar(out.ap, in0.ap, _a(s1), None, op0), reads=_regs(in0, s1), writes=[out])
    else:
        P.op(eng, lambda e: e.tensor_scalar(out.ap, in0.ap, _a(s1), _a(s2), op0, op1),
             reads=_regs(in0, s1, s2), writes=[out])


def stt(P, eng, out, in0, scalar, in1, op0, op1):
    P.op(eng, lambda e: e.scalar_tensor_tensor(out.ap, in0.ap, _a(scalar), in1.ap, op0, op1),
         reads=_regs(in0, scalar, in1), writes=[out])


def cp(P, eng, out, in_):
    if eng == ACT:
        act(P, out, in_, AF.Copy)
    else:
        P.op(eng, lambda e: e.tensor_copy(out.ap, in_.ap), reads=[in_], writes=[out])


C_IDENT, C_TRI, C_NEG, C_QDEC, C_KDEC, C_GC, C_ONES, C_TRIR, NCST = 0, 128, 256, 384, 388, 392, 396, 524, 652
PT_AN, PT_MN, PT_RN, PT_SN, PT_GN, PT_CW, PT_CB, PT_MB, NPT = 0, 8, 16, 20, 28, 32, 80, 92, 116
BT_DTB, BT_ALOG, BT_D, BT_GB, NBT = 0, 16, 32, 48, 304

O_RQ, O_RK, O_RV, O_RG, O_SZ, O_SXBC, O_SDT, O_GQ, O_GK, O_GV, O_GR, O_GLR, O_MG = (
    0, 512, 1024, 1536, 2048, 3072, 4608, 4624, 4880, 5136, 5648, 6160, 6176)


def host_consts():
    c = np.zeros((128, NCST), np.float32)
    idx = np.arange(128)
    c[:, C_IDENT:C_IDENT + 128] = np.eye(128, dtype=np.float32)
    tri = (idx[:, None] <= idx[None, :]).astype(np.float32)
    c[:, C_TRI:C_TRI + 128] = tri
    c[:, C_NEG:C_NEG + 128] = (tri - 1.0) * 30000.0
    h = np.arange(4, dtype=np.float64)
    lg = np.log1p(-np.exp2(-5.0 - h))
    pos = idx.astype(np.float64)
    c[:, C_QDEC:C_QDEC + 4] = np.exp(lg[None, :] * (pos[:, None] + 1.0))
    c[:, C_KDEC:C_KDEC + 4] = np.exp(-lg[None, :] * (pos[:, None] + 1.0)) * (128.0 ** -0.5)
    c[:, C_GC:C_GC + 4] = np.exp(lg * 128.0)[None, :]
    c[:, C_ONES:C_ONES + 128] = 1.0
    c[:, C_TRIR:C_TRIR + 128] = (idx[:, None] > idx[None, :]).astype(np.float32)
    return c


def host_cs(seq):
    inv_freq = (10000.0 ** (-np.arange(0, 128, 2, dtype=np.float32) / np.float32(128))).astype(np.float32)
    ang = np.arange(seq, dtype=np.float32)[:, None] * inv_freq[None, :]
    return np.concatenate([np.cos(ang), np.sin(ang)], axis=1).astype(np.float32)


def _blk(w):
    K, N = w.shape
    kc = K // 128
    return np.ascontiguousarray(w.reshape(kc, 128, N).transpose(1, 0, 2).reshape(128, kc * N))


def host_layer_weights(inp, l):
    w_in = inp["w_in"][l]
    blocks = []
    add = lambda w: blocks.append(_blk(w))
    add(w_in[:, O_RQ:O_RQ + 512]); add(w_in[:, O_RK:O_RK + 512]); add(w_in[:, O_RV:O_RV + 512]); add(w_in[:, O_RG:O_RG + 512])
    add(inp["ret_w_o"][l])
    add(w_in[:, O_MG:O_MG + 512]); add(w_in[:, O_MG + 512:O_MG + 1024])
    add(w_in[:, O_SZ:O_SZ + 512]); add(w_in[:, O_SZ + 512:O_SZ + 1024])
    for j in range(3):
        add(w_in[:, O_SXBC + 512 * j:O_SXBC + 512 * (j + 1)])
    add(inp["ssd_w_o"][l][:, 0:512]); add(w_in[:, O_MG + 1024:O_MG + 1536])
    add(inp["ssd_w_o"][l][:, 512:1024]); add(w_in[:, O_MG + 1536:O_MG + 2048])
    add(w_in[:, O_GQ:O_GQ + 512]); add(w_in[:, O_GV:O_GV + 512]); add(w_in[:, O_GR:O_GR + 512])
    add(inp["gla_w_o"][l])
    add(w_in[:, O_MG + 2048:O_MG + 2560]); add(w_in[:, O_MG + 2560:O_MG + 3072])
    add(inp["w_out"][l][:, 0:512]); add(inp["w_out"][l][:, 512:1024])
    for g in range(4):
        add(inp["w_up"][l][:, g * 1024:g * 1024 + 512]); add(inp["w_up"][l][:, g * 1024 + 512:(g + 1) * 1024])
        add(inp["w_down"][l][g * 1024:(g + 1) * 1024, 0:512]); add(inp["w_down"][l][g * 1024:(g + 1) * 1024, 512:1024])
    assert len(blocks) == NBLK
    wst = np.stack(blocks, 0)
    sm = np.concatenate([w_in[:, O_SDT:O_SDT + 16], w_in[:, O_GLR:O_GLR + 16]], axis=1)
    wsm = _blk(sm)
    pt = np.zeros((128, NPT), np.float32)
    colmaj = lambda v: np.ascontiguousarray(v.reshape(-1, 128).T)
    pt[:, PT_AN:PT_AN + 8] = colmaj(inp["attn_norm_w"][l])
    pt[:, PT_MN:PT_MN + 8] = colmaj(inp["mlp_norm_w"][l])
    pt[:, PT_RN:PT_RN + 4] = colmaj(inp["ret_norm_w"][l])
    pt[:, PT_SN:PT_SN + 8] = colmaj(inp["ssd_norm_w"][l])
    pt[:, PT_GN:PT_GN + 4] = colmaj(inp["gla_norm_w"][l])
    cw = inp["ssd_conv_w"][l]
    pt[:, PT_CW:PT_CW + 48] = cw.T.reshape(12, 128, 4).transpose(1, 0, 2).reshape(128, 48)
    pt[:, PT_CB:PT_CB + 12] = colmaj(inp["ssd_conv_b"][l])
    pt[:, PT_MB:PT_MB + 24] = colmaj(inp["merge_gate_b"][l])
    bt = np.concatenate([inp["ssd_dt_bias"][l], inp["ssd_a_log"][l], inp["ssd_d"][l], inp["gla_gate_b"][l]]).astype(np.float32)
    return wst, wsm, pt, bt, np.ascontiguousarray(inp["gla_gate_w"][l])


STOP = [None]
ANNOTATE = [False]


class _Stop(Exception):
    pass


def build_program(n_tiles, n_layers, pp_groups=None):
    PP = pp_groups is not None
    n_steps = n_tiles + 1 if PP else n_tiles
    S = n_tiles * TT
    nc = bass.Bass("TRN2", target_bir_lowering=False)
    nc.dge_precook = False
    T._n = 0
    dram = lambda name, shape, dt, kind="ExternalInput": nc.dram_tensor(name, shape, dt, kind=kind).ap()
    x_d = dram("x", [S, D], F32)
    cs_d = dram("cs", [n_steps * TT, 128], F32)
    if PP:
        flag_d = dram("flag", [128, 2], F32)
        cc_in = nc.dram_tensor("cc_in", [128, 8 * TT], F32)
        gath = nc.dram_tensor("gath", [256, 8 * TT], F32)
    wst_d = dram("wst", [n_layers * NBLK, 128, 4096], F32R)
    wsm_d = dram("wsm", [n_layers, 128, 256], F32R)
    ptab_d = dram("ptab", [n_layers, 128, NPT], F32)
    btab_d = dram("btab", [n_layers, NBT], F32)
    gw_d = dram("gw", [n_layers, 16, 256], F32)
    fnw_d = dram("fnw", [128, 8], F32)
    fnwb_d = dram("fnwb", [D], F32)
    cst_d = dram("cst", [128, NCST], F32)
    cstr_d = dram("cstr", [128, 128], F32R)
    out_d = dram("out", [S, D], F32, kind="ExternalOutput")
    dbg_d = {}

    with ExitStack() as es:
        P = Prog(nc)
        P.annotate = ANNOTATE[0]

        def sb(name, cols, dt=F32, parts=128):
            return T(es.enter_context(nc.sbuf_tensor("sb_" + name, [parts, cols], dt)), cols, dt, parts)

        for i in range(8):
            P.banks.append(T(es.enter_context(nc.psum_tensor("psb%d" % i, [128, 512], F32)), 512, F32, psum=True))

        xres = sb("xres", 8 * TT)
        hbuf = sb("hbuf", 8 * TT, F32R)
        mrg = sb("mrg", 8 * TT, F32R)
        big = sb("big", 4096)
        bigr = big.as_dtype(F32R)
        NWB = 4 if PP else 3
        wbuf = [sb("wbuf%d" % i, 4096, F32R) for i in range(NWB)]
        mix = sb("mix", 10240, BF16)
        cst = sb("cst", NCST)
        cstr = sb("cstr", 128, F32R)
        identb = sb("identb", 128, BF16)
        csb = [sb("csb%d" % i, 512) for i in range(2)]
        fnw = sb("fnw", 8)
        flag = sb("flag", 2)
        fnw_bc = sb("fnw_bc", D)
        oss = sb("oss", 8)
        ptab = [sb("ptab%d" % l, NPT) for l in range(n_layers)]
        btab = [sb("btab%d" % l, NBT) for l in range(n_layers)]
        negA = [sb("negA%d" % l, 16) for l in range(n_layers)]
        gwt = [sb("gwt%d" % l, 256, F32, 16) for l in range(n_layers)]
        wsm = [sb("wsm%d" % l, 256, F32R) for l in range(n_layers)]
        retS = [sb("retS%d" % l, 512) for l in range(n_layers)]
        retSb = [sb("retSb%d" % l, 512, BF16) for l in range(n_layers)]
        ssdS = [sb("ssdS%d" % l, 1024) for l in range(n_layers)]
        ssdSb = [sb("ssdSb%d" % l, 1024, BF16) for l in range(n_layers)]
        glaS = [sb("glaS%d" % l, 256) for l in range(n_layers)]
        glaSb = [sb("glaSb%d" % l, 256, BF16) for l in range(n_layers)]
        halo = [sb("halo%d" % l, 48) for l in range(n_layers)]
        scr = sb("scr", 7168)
        scr_r = scr.as_dtype(F32R)
        scr_b = scr.as_dtype(BF16)
        K32 = lambda kb, cols, parts=128: Sub(scr, int(kb * 256), cols, parts)
        K32R = lambda kb, cols: Sub(scr_r, int(kb * 256), cols)
        KBF = lambda kb, cols: Sub(scr_b, int(kb * 512), cols)
        sq = [sb("sq0", TT, F32R), sb("sq1", TT, F32R)]
        nrm_a = K32(4, TT)
        nrm_r = K32(6, TT)
        gate_sb = [K32(8, TT), K32(10, TT)]
        gtmp = [K32(12, TT), K32(14, TT)]
        relu_t = [K32(16, TT), K32(18, TT)]
        rtmp = [K32(i, 256) for i in range(4)]
        rot = K32(4, 512)
        qT_r, kT_r, sT_r = KBF(6, 512), KBF(7, 512), KBF(8, 512)
        junk_r, og_r, stmp = K32(10, 512), K32(12, 512), K32(14, 512)
        stage = [K32(16, TT + 4), K32(18.5, TT + 4)]
        cacc = [K32(21, TT), K32(23, TT)]
        A1 = [K32(0, 512), K32(2, 512)]
        ET = [K32(4, 512), K32(6, 512)]
        GT = KBF(8, 2048)
        cb_sb = K32(12, 256)
        Xbf = KBF(13, 1024)
        XW = KBF(15, 1024)
        Bbf = KBF(17, 256)
        y1 = K32(18, 1024)
        y2 = K32(22, 1024)
        junk_s = K32(26, 512)
        ostage = [K32(16, 1024), K32(20, 1024)]
        ojunk = K32(24, 512)
        glr_sb = K32(0, TT, 16)
        lp, gtm, eb, enb, erev = K32(2, 256), K32(3, 256), K32(4, 256), K32(5, 256), K32(6, 256)
        qin, kin, kst = KBF(7, 256), KBF(7.5, 256), KBF(8, 256)
        qT_g, kT_g, sT_g = KBF(9, 512), KBF(10, 512), KBF(11, 512)
        junk_g, og_g = K32(12, 512), K32(14, 512)
        ss4 = sb("ss4", 8)
        rs4 = sb("rs4", 8)
        dt_all = sb("dt_all", 64)
        dtt = sb("dtt", 64)
        a_all = sb("a_all", 64)
        acs_all = sb("acs_all", 64)
        eacs_all = sb("eacs_all", 64)
        wdec = sb("wdec", 16)
        sdec = sb("sdec", 16)
        gdec = sb("gdec", 2)

        def C(c0, n, p0=0, p1=128):
            return cst.r(c0, c0 + n, p0, p1)

        ident = C(C_IDENT, 128)
        tri = C(C_TRI, 128)
        identb_r = identb.r()

        P.dma(SP, "cst", cst.r().ap, cst_d, writes=[cst.r()])
        P.dma(SP, "cstr", cstr.r().ap, cstr_d, writes=[cstr.r()])
        P.dma(SP, "fnw", fnw.r().ap, fnw_d, writes=[fnw.r()])
        P.dma(POOL, "fnwb", fnw_bc.r().ap, fnwb_d.partition_broadcast(128), writes=[fnw_bc.r()])
        if PP:
            P.dma(SP, "flag", flag.r().ap, flag_d, writes=[flag.r()])
            tin, tg = DTrack(), DTrack()
        for l in range(n_layers):
            P.dma(SP, "ptab", ptab[l].r().ap, ptab_d[l], writes=[ptab[l].r()])
            P.dma(POOL, "btab", btab[l].r().ap, btab_d[l].partition_broadcast(128), writes=[btab[l].r()])
            P.dma(SP, "gwt", gwt[l].r().ap, gw_d[l], writes=[gwt[l].r()])
            P.dma(SP, "wsm", wsm[l].r().ap, wsm_d[l], writes=[wsm[l].r()])
        cp(P, DVE, identb_r, ident)
        for l in range(n_layers):
            act(P, negA[l].r(), btab[l].r(BT_ALOG, BT_ALOG + 16), AF.Exp)
            ts(P, DVE, negA[l].r(), negA[l].r(), -1.0, None, ALU.mult)
            for t_ in (retS[l], ssdS[l], glaS[l], halo[l]):
                P.op(POOL, (lambda tt_: (lambda e: e.memset(tt_.r().ap, 0.0)))(t_), writes=[t_.r()])
            for t_ in (retSb[l], ssdSb[l], glaSb[l]):
                P.op(POOL, (lambda tt_: (lambda e: e.memset(tt_.r().ap, 0.0)))(t_), writes=[t_.r()])

        sched = []
        for ti in range(n_steps):
            for l in range(n_layers):
                for b in range(NBLK):
                    sched.append(l * NBLK + b)
        wstate = {"issued": 0, "used": 0}

        def w_issue():
            i = wstate["issued"]
            if i >= len(sched):
                return
            buf = wbuf[i % NWB]
            P.dma(SP, "w%d" % (i % NWB), buf.r().ap, wst_d[sched[i]], writes=[buf.r()])
            wstate["issued"] = i + 1

        def w_next():
            i = wstate["used"]
            wstate["used"] = i + 1
            return wbuf[i % NWB]

        def w_done():
            w_issue()

        for _ in range(NWB):
            w_issue()

        def xk(kc, c0=0, c1=TT):
            return xres.r(kc * TT + c0, kc * TT + c1)

        def hk(kc, c0=0, c1=TT):
            return hbuf.r(kc * TT + c0, kc * TT + c1)

        def rmsnorm(wcol, out_fn):
            ps = P.psum()
            for kc in range(8):
                s = sq[kc % 2]
                act(P, s.r(), xk(kc), AF.Square)
                mm(P, ps.r(), cstr.r(), s.r(), start=(kc == 0), stop=(kc == 7))
            act(P, nrm_a.r(), ps.r(), AF.Ln, bias=EPS)
            act(P, nrm_r.r(), nrm_a.r(), AF.Exp, scale=-0.5)
            for kc in range(8):
                stt(P, DVE, out_fn(kc), xk(kc), wcol(kc), nrm_r.r(), ALU.mult, ALU.mult)

        def proj_tm(wb, ncols, evac, kstride=512, col0=0):
            for c in range(NCH):
                ps = P.psum()
                for kc in range(8):
                    mm(P, ps.r(0, ncols), hk(kc, c * 128, (c + 1) * 128),
                       wb.r(kc * kstride + col0, kc * kstride + col0 + ncols), start=(kc == 0), stop=(kc == 7))
                evac(c, ps)

        def proj_fm(wb, rhs_fn, nk, kstride, col0, m=128):
            ps = P.psum()
            for kc in range(nk):
                mm(P, ps.r(0, TT, 0, m), wb.r(kc * kstride + col0, kc * kstride + col0 + m), rhs_fn(kc),
                   start=(kc == 0), stop=(kc == nk - 1))
            return ps

        def outT(kc, c0=0, c1=TT):
            return bigr.r(kc * TT + c0, kc * TT + c1)

        def group_post(ps_o, G_fn, nw_col, c, junk, og):
            for h in range(4):
                act(P, junk.r(h * 128, (h + 1) * 128), ps_o.r(h * 128, (h + 1) * 128), AF.Square,
                    scale=128.0 ** -0.5, accum=ss4.r(h, h + 1))
            act(P, rs4.r(0, 4), ss4.r(0, 4), AF.Ln, bias=EPS)
            act(P, rs4.r(4, 8), rs4.r(0, 4), AF.Exp, scale=-0.5)
            for h in range(4):
                stt(P, DVE, og.r(h * 128, (h + 1) * 128), ps_o.r(h * 128, (h + 1) * 128), rs4.r(4 + h, 5 + h),
                    G_fn(h), ALU.mult, ALU.mult)
            pt_ = P.psum()
            for h in range(4):
                tr(P, pt_.r(h * 128, (h + 1) * 128), og.r(h * 128, (h + 1) * 128), ident)
            for h in range(4):
                act(P, outT(h, c * 128, (c + 1) * 128), pt_.r(h * 128, (h + 1) * 128), AF.Copy, scale=nw_col(h))

        def merge_branch(br, l, wo, nk, kstride_o, wmg_blocks, dcs=range(8)):
            for dc in dcs:
                wo_b, col_o = wo(dc)
                ps_o = proj_fm(wo_b, lambda kc: outT(kc), nk, kstride_o, col_o)
                ps_g = proj_fm(wmg_blocks[dc // 4], lambda kc: hk(kc), 8, 512, (dc % 4) * 128)
                g = gate_sb[dc % 2]
                act(P, g.r(), ps_g.r(), AF.Sigmoid, bias=ptab[l].r(PT_MB + br * 8 + dc, PT_MB + br * 8 + dc + 1))
                dst = mrg.r(dc * TT, (dc + 1) * TT)
                if br == 0:
                    tt(P, DVE, dst, ps_o.r(), g.r(), ALU.mult)
                else:
                    tmp = gtmp[dc % 2]
                    tt(P, DVE, tmp.r(), ps_o.r(), g.r(), ALU.mult)
                    tt(P, POOL, dst, dst, tmp.r(), ALU.add)

        def b3(ap, a, b):
            return ap.unsqueeze(2).broadcast_to([ap.shape[0], a, b])

        def layer(l, ti, after_gla=None):
            pt = ptab[l]
            pcol = lambda base: (lambda k: pt.r(base + k, base + k + 1))
            csr = csb[ti % 2]
            P.phase = "norm1"
            rmsnorm(pcol(PT_AN), hk)

            if STOP[0] == 1:
                raise _Stop()
            P.phase = "ret_proj"
            Qt = lambda c, c0=0, c1=512: mix.r(c * 512 + c0, c * 512 + c1)
            Kt = lambda c, c0=0, c1=512: mix.r(2048 + c * 512 + c0, 2048 + c * 512 + c1)
            Vt = lambda c, c0=0, c1=512: mix.r(4096 + c * 512 + c0, 4096 + c * 512 + c1)
            Gt = lambda c, c0=0, c1=512: mix.r(6144 + c * 512 + c0, 6144 + c * 512 + c1)

            def rotary_evac(dst_fn, dec_c0):
                def f(c, ps):
                    v4 = lambda r_: r_.m(lambda ap: ap.rearrange("p (h t d) -> p h t d", h=4, t=2))
                    x1 = v4(ps.r()).m(lambda ap: ap[:, :, 0, :])
                    x2 = v4(ps.r()).m(lambda ap: ap[:, :, 1, :])
                    cos = csr.r(c * 128, c * 128 + 64).m(lambda ap: ap.unsqueeze(1).broadcast_to([128, 4, 64]))
                    sin = csr.r(c * 128 + 64, c * 128 + 128).m(lambda ap: ap.unsqueeze(1).broadcast_to([128, 4, 64]))
                    v3 = lambda r_: r_.m(lambda ap: ap.rearrange("p (h d) -> p h d", h=4))
                    t1, t2, t3, t4 = [v3(rtmp[i].r()) for i in range(4)]
                    tt(P, DVE, t1, x1, cos, ALU.mult)
                    tt(P, DVE, t2, x2, sin, ALU.mult)
                    tt(P, DVE, t3, x1, sin, ALU.mult)
                    tt(P, DVE, t4, x2, cos, ALU.mult)
                    r1 = v4(rot.r()).m(lambda ap: ap[:, :, 0, :])
                    r2 = v4(rot.r()).m(lambda ap: ap[:, :, 1, :])
                    tt(P, POOL, r1, t1, t2, ALU.subtract)
                    tt(P, POOL, r2, t3, t4, ALU.add)
                    dec = C(dec_c0, 4).m(lambda ap: b3(ap, 4, 128))
                    tt(P, POOL, dst_fn(c).m(lambda ap: ap.rearrange("p (h d) -> p h d", h=4)),
                       rot.r().m(lambda ap: ap.rearrange("p (h d) -> p h d", h=4)), dec, ALU.mult)
                return f

            wb = w_next(); proj_tm(wb, 512, rotary_evac(Qt, C_QDEC)); w_done()
            wb = w_next(); proj_tm(wb, 512, rotary_evac(Kt, C_KDEC)); w_done()
            wb = w_next(); proj_tm(wb, 512, lambda c, ps: cp(P, ACT, Vt(c), ps.r())); w_done()
            wb = w_next(); proj_tm(wb, 512, lambda c, ps: act(P, Gt(c), ps.r(), AF.Silu)); w_done()
            if STOP[0] == 2:
                raise _Stop()
            P.phase = "ret_chunks"
            S_, Sb_ = retS[l], retSb[l]
            for c in range(NCH):
                pq = P.psum().as_dtype(BF16)
                for h in range(4):
                    tr(P, pq.r(h * 128, (h + 1) * 128), Qt(c, h * 128, (h + 1) * 128), identb_r)
                cp(P, ACT, qT_r.r(), pq.r(0, 512))
                pk = P.psum().as_dtype(BF16)
                for h in range(4):
                    tr(P, pk.r(h * 128, (h + 1) * 128), Kt(c, h * 128, (h + 1) * 128), identb_r)
                cp(P, DVE, kT_r.r(), pk.r(0, 512))
                psc = P.psum()
                for h in range(4):
                    mm(P, psc.r(h * 128, (h + 1) * 128), kT_r.r(h * 128, (h + 1) * 128), qT_r.r(h * 128, (h + 1) * 128))
                tt(P, DVE, sT_r.r().m(lambda ap: ap.rearrange("p (h d) -> p h d", h=4)),
                   psc.r().m(lambda ap: ap.rearrange("p (h d) -> p h d", h=4)),
                   tri.m(lambda ap: ap.unsqueeze(1).broadcast_to([128, 4, 128])), ALU.mult)
                po = P.psum()
                for h in range(4):
                    hs = slice(h * 128, (h + 1) * 128)
                    mm(P, po.r(h * 128, (h + 1) * 128), sT_r.r(h * 128, (h + 1) * 128), Vt(c, h * 128, (h + 1) * 128), True, False)
                    mm(P, po.r(h * 128, (h + 1) * 128), qT_r.r(h * 128, (h + 1) * 128), Sb_.r(h * 128, (h + 1) * 128), False, True)
                pds = P.psum()
                for h in range(4):
                    mm(P, pds.r(h * 128, (h + 1) * 128), Kt(c, h * 128, (h + 1) * 128), Vt(c, h * 128, (h + 1) * 128))
                tt(P, DVE, stmp.r(0, 512), pds.r(), S_.r(), ALU.add)
                tt(P, POOL, S_.r().m(lambda ap: ap.rearrange("p (h d) -> p h d", h=4)),
                   stmp.r(0, 512).m(lambda ap: ap.rearrange("p (h d) -> p h d", h=4)),
                   C(C_GC, 4).m(lambda ap: b3(ap, 4, 128)), ALU.mult)
                cp(P, POOL, Sb_.r(), S_.r())
                group_post(po, lambda h: Gt(c, h * 128, (h + 1) * 128), pcol(PT_RN), c, junk_r, og_r)
            if STOP[0] == 3:
                raise _Stop()
            P.phase = "ret_merge"
            wo_b = w_next()
            wmg = [w_next(), w_next()]
            merge_branch(0, l, lambda dc: (wo_b, dc * 128), 4, 1024, wmg)
            w_done(); w_done(); w_done()

            if STOP[0] == 4:
                raise _Stop()
            P.phase = "ssd_proj_conv"
            Zt = lambda c, c0=0, c1=1024: mix.r(c * 1024 + c0, c * 1024 + c1)
            XBC = lambda cc, c0=0, c1=TT: mix.r(4096 + cc * TT + c0, 4096 + cc * TT + c1)
            for half in range(2):
                wb = w_next()
                proj_tm(wb, 512, (lambda hf: (lambda c, ps: act(P, Zt(c, hf * 512, hf * 512 + 512), ps.r(), AF.Silu)))(half))
                w_done()
            for j in range(3):
                wb = w_next()
                for q in range(4):
                    cc = j * 4 + q
                    ps = proj_fm(wb, lambda kc: hk(kc), 8, 512, q * 128)
                    st = stage[cc % 2]
                    ca = cacc[cc % 2]
                    cp(P, POOL, st.r(0, 3), halo[l].r(cc * 4, cc * 4 + 3))
                    cp(P, ACT, st.r(3, 3 + TT), ps.r())
                    cp(P, POOL, halo[l].r(cc * 4, cc * 4 + 3), st.r(TT, TT + 3))
                    act(P, ca.r(), st.r(0, TT), AF.Copy, scale=pt.r(PT_CW + cc * 4, PT_CW + cc * 4 + 1))
                    for k in range(1, 4):
                        stt(P, DVE, ca.r(), st.r(k, k + TT), pt.r(PT_CW + cc * 4 + k, PT_CW + cc * 4 + k + 1),
                            ca.r(), ALU.mult, ALU.add)
                    act(P, XBC(cc), ca.r(), AF.Silu, bias=pt.r(PT_CB + cc, PT_CB + cc + 1))
                w_done()
            if STOP[0] == 5:
                raise _Stop()
            P.phase = "ssd_chunks"
            S_, Sb_ = ssdS[l], ssdSb[l]
            bt = btab[l]
            pdt = P.psum()
            for c in range(NCH):
                for kc in range(8):
                    mm(P, pdt.r(c * 16, c * 16 + 16), hk(kc, c * 128, (c + 1) * 128), wsm[l].r(kc * 32, kc * 32 + 16), kc == 0, kc == 7)
            v4c = lambda r_: r_.m(lambda ap: ap.rearrange("p (c h) -> p c h", c=NCH))
            tt(P, DVE, v4c(dtt.r()), v4c(pdt.r(0, 64)), bt.r(BT_DTB, BT_DTB + 16).m(lambda ap: ap.unsqueeze(1).broadcast_to([128, NCH, 16])), ALU.add)
            act(P, dtt.r(), dtt.r(), AF.Exp)
            act(P, dt_all.r(), dtt.r(), AF.Ln, bias=1.0)
            tt(P, DVE, v4c(a_all.r()), v4c(dt_all.r()), negA[l].r().m(lambda ap: ap.unsqueeze(1).broadcast_to([128, NCH, 16])), ALU.mult)
            pacs = P.psum()
            for c in range(NCH):
                mm(P, pacs.r(c * 16, c * 16 + 16), tri, a_all.r(c * 16, c * 16 + 16))
            cp(P, DVE, acs_all.r(), pacs.r(0, 64))
            act(P, eacs_all.r(), pacs.r(0, 64), AF.Exp)
            for c in range(NCH):
                cs_ = slice(c * 128, (c + 1) * 128)
                dt_sb = Sub(dt_all, c * 16, 16)
                a_sb = Sub(a_all, c * 16, 16)
                acs_sb = Sub(acs_all, c * 16, 16)
                eacs = Sub(eacs_all, c * 16, 16)
                pcb = P.psum()
                for g in range(2):
                    mm(P, pcb.r(g * 128, (g + 1) * 128), XBC(8 + g, c * 128, (c + 1) * 128), XBC(10 + g, c * 128, (c + 1) * 128))
                cp(P, ACT, cb_sb.r(), pcb.r(0, 256))
                for hg in range(4):
                    a1 = A1[hg % 2]
                    tt(P, POOL, a1.r().m(lambda ap: ap.rearrange("p (h l) -> p h l", h=4)),
                       a_sb.r(hg * 4, hg * 4 + 4).m(lambda ap: b3(ap, 4, 128)),
                       tri.m(lambda ap: ap.unsqueeze(1).broadcast_to([128, 4, 128])), ALU.mult)
                    pb = P.psum()
                    mm(P, pb.r(), C(C_ONES, 128), a1.r())
                    et = ET[hg % 2]
                    for hh in range(4):
                        h = hg * 4 + hh
                        stt(P, DVE, et.r(hh * 128, (hh + 1) * 128), pb.r(hh * 128, (hh + 1) * 128), acs_sb.r(h, h + 1),
                            C(C_NEG, 128), ALU.subtract, ALU.add)
                    cp(P, DVE, sdec.r(hg * 4, hg * 4 + 4),
                       pb.r().m(lambda ap: ap.rearrange("p (h l) -> p h l", h=4)[:, :, 127]))
                    lt = et
                    act(P, lt.r(), et.r(), AF.Exp)
                    g = hg // 2
                    v4_ = lambda r_: r_.m(lambda ap: ap.rearrange("p (h l) -> p h l", h=4))
                    tt(P, DVE, v4_(lt.r()), v4_(lt.r()),
                       cb_sb.r(g * 128, (g + 1) * 128).m(lambda ap: ap.unsqueeze(1).broadcast_to([128, 4, 128])), ALU.mult)
                    tt(P, POOL, v4_(GT.r(hg * 512, (hg + 1) * 512)), v4_(lt.r()),
                       dt_sb.r(hg * 4, hg * 4 + 4).m(lambda ap: b3(ap, 4, 128)), ALU.mult)
                tt(P, DVE, wdec.r(), sdec.r(), acs_sb.r(), ALU.subtract)
                act(P, wdec.r(), wdec.r(), AF.Exp)
                tt(P, DVE, wdec.r(), wdec.r(), dt_sb.r(), ALU.mult)
                act(P, sdec.r(), sdec.r(), AF.Exp)
                px = P.psum().as_dtype(BF16)
                for hc in range(8):
                    tr(P, px.r(hc * 128, (hc + 1) * 128), XBC(hc, c * 128, (c + 1) * 128), identb_r)
                cp(P, ACT, Xbf.r(), px.r(0, 1024))
                tt(P, DVE, XW.r().m(lambda ap: ap.rearrange("p (h d) -> p h d", h=16)),
                   px.r(0, 1024).m(lambda ap: ap.rearrange("p (h d) -> p h d", h=16)),
                   wdec.r().m(lambda ap: b3(ap, 16, 64)), ALU.mult)
                pbt = P.psum().as_dtype(BF16)
                for g in range(2):
                    tr(P, pbt.r(g * 128, (g + 1) * 128), XBC(8 + g, c * 128, (c + 1) * 128), identb_r)
                cp(P, ACT, Bbf.r(), pbt.r(0, 256))
                pyd = [P.psum(), P.psum()]
                for h in range(16):
                    g, hl = h // 8, h % 8
                    mm(P, pyd[g].r(hl * 64, (hl + 1) * 64), GT.r(h * 128, (h + 1) * 128), Xbf.r(h * 64, (h + 1) * 64))
                pyo = [P.psum(), P.psum()]
                for g in range(2):
                    mm(P, pyo[g].r(), XBC(10 + g, c * 128, (c + 1) * 128), Sb_.r(g * 512, (g + 1) * 512))
                for g in range(2):
                    gs = slice(g * 512, (g + 1) * 512)
                    v8 = lambda r_: r_.m(lambda ap: ap.rearrange("p (h d) -> p h d", h=8))
                    tt(P, DVE, v8(y1.r(g * 512, (g + 1) * 512)), v8(pyo[g].r()),
                       eacs.r(g * 8, g * 8 + 8).m(lambda ap: b3(ap, 8, 64)), ALU.mult)
                    tt(P, DVE, y1.r(g * 512, (g + 1) * 512), y1.r(g * 512, (g + 1) * 512), pyd[g].r(), ALU.add)
                    tt(P, POOL, v8(y2.r(g * 512, (g + 1) * 512)), v8(Xbf.r(g * 512, (g + 1) * 512)),
                       bt.r(BT_D + g * 8, BT_D + g * 8 + 8).m(lambda ap: b3(ap, 8, 64)), ALU.mult)
                    tt(P, POOL, y1.r(g * 512, (g + 1) * 512), y1.r(g * 512, (g + 1) * 512), y2.r(g * 512, (g + 1) * 512), ALU.add)
                    tt(P, DVE, y1.r(g * 512, (g + 1) * 512), y1.r(g * 512, (g + 1) * 512), Zt(c, g * 512, (g + 1) * 512), ALU.mult)
                    act(P, junk_s.r(), y1.r(g * 512, (g + 1) * 512), AF.Square, scale=512.0 ** -0.5, accum=ss4.r(g, g + 1))
                act(P, rs4.r(0, 2), ss4.r(0, 2), AF.Ln, bias=EPS)
                act(P, rs4.r(4, 6), rs4.r(0, 2), AF.Exp, scale=-0.5)
                for g in range(2):
                    ts(P, DVE, y2.r(g * 512, (g + 1) * 512), y1.r(g * 512, (g + 1) * 512), rs4.r(4 + g, 5 + g), None, ALU.mult)
                for g in range(2):
                    pt_ = P.psum()
                    for hl in range(4):
                        hc = g * 4 + hl
                        tr(P, pt_.r(hl * 128, (hl + 1) * 128), y2.r(hc * 128, (hc + 1) * 128), ident)
                    for hl in range(4):
                        hc = g * 4 + hl
                        act(P, outT(hc, c * 128, (c + 1) * 128), pt_.r(hl * 128, (hl + 1) * 128), AF.Copy,
                            scale=pt.r(PT_SN + hc, PT_SN + hc + 1))
                pds = [P.psum(), P.psum()]
                for g in range(2):
                    mm(P, pds[g].r(), Bbf.r(g * 128, (g + 1) * 128), XW.r(g * 512, (g + 1) * 512))
                for g in range(2):
                    v8 = lambda r_: r_.m(lambda ap: ap.rearrange("p (h d) -> p h d", h=8))
                    tt(P, POOL, v8(S_.r(g * 512, (g + 1) * 512)), v8(S_.r(g * 512, (g + 1) * 512)),
                       sdec.r(g * 8, g * 8 + 8).m(lambda ap: b3(ap, 8, 64)), ALU.mult)
                    tt(P, DVE, S_.r(g * 512, (g + 1) * 512), S_.r(g * 512, (g + 1) * 512), pds[g].r(), ALU.add)
                    cp(P, ACT, Sb_.r(g * 512, (g + 1) * 512), S_.r(g * 512, (g + 1) * 512))
            if STOP[0] == 6:
                raise _Stop()
            P.phase = "ssd_merge"
            for q in range(2):
                wo_q = w_next()
                wmg_q = w_next()
                merge_branch(1, l, (lambda wq: (lambda dc: (wq, (dc % 4) * 128)))(wo_q), 8, 512, [wmg_q, wmg_q], range(q * 4, q * 4 + 4))
                w_done(); w_done()

            if STOP[0] == 7:
                raise _Stop()
            P.phase = "gla_proj"
            GQK = lambda c, c0=0, c1=512: mix.r(c * 512 + c0, c * 512 + c1)
            GV = lambda c, c0=0, c1=512: mix.r(2048 + c * 512 + c0, 2048 + c * 512 + c1)
            GG = lambda c, c0=0, c1=512: mix.r(4096 + c * 512 + c0, 4096 + c * 512 + c1)
            wb = w_next(); proj_tm(wb, 512, lambda c, ps: cp(P, ACT, GQK(c), ps.r())); w_done()
            wb = w_next(); proj_tm(wb, 512, lambda c, ps: cp(P, ACT, GV(c), ps.r())); w_done()
            wb = w_next(); proj_tm(wb, 512, lambda c, ps: act(P, GG(c), ps.r(), AF.Silu)); w_done()
            pg = P.psum()
            for kc in range(8):
                mm(P, pg.r(0, TT, 0, 16), wsm[l].r(kc * 32 + 16, kc * 32 + 32), hk(kc), kc == 0, kc == 7)
            cp(P, ACT, glr_sb.r(), pg.r(0, TT, 0, 16))
            if STOP[0] == 8:
                raise _Stop()
            P.phase = "gla_chunks"
            S_, Sb_ = glaS[l], glaSb[l]
            for c in range(NCH):
                pga = P.psum()
                mm(P, pga.r(0, 256), glr_sb.r(c * 128, (c + 1) * 128), gwt[l].r())
                tt(P, DVE, gtm.r(), pga.r(0, 256), bt.r(BT_GB, BT_GB + 256), ALU.add)
                act(P, gtm.r(), gtm.r(), AF.Exp, scale=-1.0)
                act(P, lp.r(), gtm.r(), AF.Ln, bias=1.0)
                if STOP[0] == 81:
                    raise _Stop()
                pb_ = P.psum()
                mm(P, pb_.r(0, 256), tri, lp.r())
                mm(P, pb_.r(256, 512), C(C_TRIR, 128), lp.r())
                if STOP[0] == 82:
                    raise _Stop()
                ptot = P.psum()
                for pr in range(2):
                    mm(P, ptot.r(pr, pr + 1), lp.r(pr * 128, (pr + 1) * 128), C(C_ONES, 1))
                if STOP[0] == 83:
                    raise _Stop()
                act(P, eb.r(), pb_.r(0, 256), AF.Exp, scale=-1.0 / 16.0)
                act(P, enb.r(), pb_.r(0, 256), AF.Exp, scale=1.0 / 16.0)
                act(P, erev.r(), pb_.r(256, 512), AF.Exp, scale=-1.0 / 16.0)
                act(P, gdec.r(), ptot.r(0, 2), AF.Exp, scale=-1.0 / 16.0)
                if STOP[0] == 84:
                    raise _Stop()
                stt(P, DVE, qin.r(), GQK(c, 0, 256), 0.125, eb.r(), ALU.mult, ALU.mult)
                tt(P, POOL, kin.r(), GQK(c, 256, 512), enb.r(), ALU.mult)
                tt(P, POOL, kst.r(), GQK(c, 256, 512), erev.r(), ALU.mult)
                if STOP[0] == 85:
                    raise _Stop()
                pq = P.psum().as_dtype(BF16)
                for pr in range(2):
                    tr(P, pq.r(pr * 128, (pr + 1) * 128), qin.r(pr * 128, (pr + 1) * 128), identb_r)
                    tr(P, pq.r(256 + pr * 128, 256 + (pr + 1) * 128), kin.r(pr * 128, (pr + 1) * 128), identb_r)
                cp(P, ACT, qT_g.r(0, 256), pq.r(0, 256))
                cp(P, DVE, kT_g.r(0, 256), pq.r(256, 512))
                if STOP[0] == 86:
                    raise _Stop()
                psc2 = [P.psum(), P.psum()]
                for h in range(4):
                    pr, hl = h // 2, h % 2
                    mm(P, psc2[hl].r(pr * 128, (pr + 1) * 128), kT_g.r(pr * 128, (pr + 1) * 128, hl * 64, (hl + 1) * 64),
                       qT_g.r(pr * 128, (pr + 1) * 128, hl * 64, (hl + 1) * 64))
                for hl in range(2):
                    tt(P, DVE, sT_g.r().m((lambda hl_: (lambda ap: ap.rearrange("p (pr hl d) -> p pr hl d", pr=2, hl=2)[:, :, hl_, :]))(hl)),
                       psc2[hl].r(0, 256).m(lambda ap: ap.rearrange("p (h d) -> p h d", h=2)),
                       tri.m(lambda ap: ap.unsqueeze(1).broadcast_to([128, 2, 128])), ALU.mult)
                if STOP[0] == 87:
                    raise _Stop()
                po = P.psum()
                for h in range(4):
                    pr, hl = h // 2, h % 2
                    mm(P, po.r(h * 128, (h + 1) * 128), sT_g.r(h * 128, (h + 1) * 128), GV(c, h * 128, (h + 1) * 128), True, False)
                    mm(P, po.r(h * 128, (h + 1) * 128), qT_g.r(pr * 128, (pr + 1) * 128, hl * 64, (hl + 1) * 64),
                       Sb_.r(pr * 128, (pr + 1) * 128, hl * 64, (hl + 1) * 64), False, True)
                if STOP[0] == 88:
                    raise _Stop()
                pds = P.psum()
                for pr in range(2):
                    mm(P, pds.r(pr * 256, (pr + 1) * 256), kst.r(pr * 128, (pr + 1) * 128), GV(c, pr * 256, (pr + 1) * 256))
                for h in range(4):
                    pr, hl = h // 2, h % 2
                    stt(P, DVE, S_.r(pr * 128, (pr + 1) * 128, hl * 64, (hl + 1) * 64),
                        S_.r(pr * 128, (pr + 1) * 128, hl * 64, (hl + 1) * 64),
                        gdec.r(pr, pr + 1, hl * 64, (hl + 1) * 64),
                        pds.r(pr * 256 + hl * 128, pr * 256 + (hl + 1) * 128, hl * 64, (hl + 1) * 64), ALU.mult, ALU.add)
                if STOP[0] == 89:
                    raise _Stop()
                cp(P, POOL, Sb_.r(), S_.r())
                group_post(po, lambda h: GG(c, h * 128, (h + 1) * 128), pcol(PT_GN), c, junk_g, og_g)
            if STOP[0] == 9:
                raise _Stop()
            P.phase = "gla_merge"
            wo_b = w_next()
            wmg = [w_next(), w_next()]
            merge_branch(2, l, lambda dc: (wo_b, dc * 128), 4, 1024, wmg)
            w_done(); w_done(); w_done()
            if after_gla is not None:
                after_gla()

            if STOP[0] == 10:
                raise _Stop()
            P.phase = "w_out"
            wo = [w_next(), w_next()]
            for dc in range(8):
                ps = proj_fm(wo[dc // 4], lambda kc: mrg.r(kc * TT, (kc + 1) * TT), 8, 512, (dc % 4) * 128)
                tt(P, DVE, xk(dc), xk(dc), ps.r(), ALU.add)
            w_done(); w_done()

            if STOP[0] == 11:
                raise _Stop()
            P.phase = "mlp"
            rmsnorm(pcol(PT_MN), hk)
            hid = lambda j: bigr.r(j * TT, (j + 1) * TT)
            for g in range(4):
                wu = [w_next(), w_next()]
                for j in range(8):
                    ps = proj_fm(wu[j // 4], lambda kc: hk(kc), 8, 512, (j % 4) * 128)
                    rt = relu_t[j % 2]
                    act(P, rt.r(), ps.r(), AF.Relu)
                    act(P, hid(j), rt.r(), AF.Square)
                w_done(); w_done()
                wd = [w_next(), w_next()]
                for dc in range(8):
                    ps = proj_fm(wd[dc // 4], hid, 8, 512, (dc % 4) * 128)
                    tt(P, DVE, xk(dc), xk(dc), ps.r(), ALU.add)
                w_done(); w_done()

        xin = mix.as_dtype(F32)

        def load_inputs(ti):
            t0 = min(ti, n_tiles - 1) * TT
            P.dma(SP, "xin", xin.r(0, 4096).m(lambda ap: ap.rearrange("p (c d) -> p c d", c=NCH)).ap,
                  x_d[t0:t0 + TT, :].rearrange("(c p) d -> p c d", p=128), writes=[xin.r(0, 4096)])
            csr = csb[ti % 2]
            P.dma(SP, "cs%d" % (ti % 2), csr.r().m(lambda ap: ap.rearrange("p (c d) -> p c d", c=NCH)).ap,
                  cs_d[ti * TT:(ti + 1) * TT, :].rearrange("(c p) d -> p c d", p=128), writes=[csr.r()])

        load_inputs(0)
        for ti in range(n_steps):
            P.phase = "boundary"
            blend = PP and ti >= 1
            if blend:
                P.dma(POOL, "gin", scr.r(0, 4096).ap, gath.ap()[0:128, :], reads=[tg.r()], writes=[scr.r(0, 4096)])
                for dc in range(8):
                    if dc % 2:
                        act(P, scr.r(dc * TT, (dc + 1) * TT), scr.r(dc * TT, (dc + 1) * TT), AF.Copy, scale=flag.r(1, 2))
                    else:
                        ts(P, DVE, scr.r(dc * TT, (dc + 1) * TT), scr.r(dc * TT, (dc + 1) * TT), flag.r(1, 2), None, ALU.mult)
            for dc in range(8):
                ps = P.psum()
                for c in range(NCH):
                    tr(P, ps.r(c * 128, (c + 1) * 128), xin.r(c * D + dc * 128, c * D + (dc + 1) * 128), ident)
                if blend:
                    stt(P, DVE, xk(dc), ps.r(), flag.r(0, 1), scr.r(dc * TT, (dc + 1) * TT), ALU.mult, ALU.add)
                else:
                    cp(P, ACT if dc % 2 else DVE, xk(dc), ps.r())
            nxt = (lambda t_: (lambda: load_inputs(t_)))(ti + 1) if ti + 1 < n_steps else None
            for l in range(n_layers):
                try:
                    layer(l, ti, nxt if l == n_layers - 1 else None)
                except _Stop:
                    pass
            P.phase = "boundary"
            if PP:
                P.dma(POOL, "ccin", cc_in.ap(), xres.r().ap, reads=[xres.r()], writes=[tin.r()])
                P.cc("ag", lambda e: e.collective_compute("AllGather", ALU.bypass, replica_groups=pp_groups,
                                                         ins=[cc_in.ap().opt()], outs=[gath.ap().opt()]),
                     reads=[tin.r()], writes=[tg.r()])
                if ti == 0:
                    for l in range(n_layers):
                        for t_ in (retS[l], retSb[l], ssdS[l], ssdSb[l], glaS[l], glaSb[l], halo[l]):
                            ts(P, DVE, t_.r(), t_.r(), flag.r(0, 1), None, ALU.mult)
                    continue
            o0 = (ti - 1) * TT if PP else ti * TT
            for c in range(NCH):
                pss = [P.psum(), P.psum()]
                for q in range(2):
                    for j in range(4):
                        dc = q * 4 + j
                        tr(P, pss[q].r(j * 128, (j + 1) * 128), xk(dc, c * 128, (c + 1) * 128), ident)
                for q in range(2):
                    act(P, ojunk.r(), pss[q].r(), AF.Square, scale=1.0 / 32.0, accum=oss.r(q, q + 1))
                tt(P, DVE, oss.r(2, 3), oss.r(0, 1), oss.r(1, 2), ALU.add)
                act(P, oss.r(3, 4), oss.r(2, 3), AF.Ln, bias=EPS)
                act(P, oss.r(4, 5), oss.r(3, 4), AF.Exp, scale=-0.5)
                stg = ostage[c % 2]
                for q in range(2):
                    stt(P, DVE, stg.r(q * 512, (q + 1) * 512), pss[q].r(), oss.r(4, 5), fnw_bc.r(q * 512, (q + 1) * 512),
                        ALU.mult, ALU.mult)
                P.dma(SP, "xout%d" % (c % 2), out_d[o0 + c * 128:o0 + (c + 1) * 128, :], stg.r().ap, reads=[stg.r()], final=True)
        P.emit()
    return nc


_CACHE = {}


def _get_prog(n_tiles, n_layers, groups):
    key = (n_tiles, n_layers, str(groups))
    if key not in _CACHE:
        _CACHE[key] = build_program(n_tiles, n_layers, groups)
    return _CACHE[key]


def _layer_maps(inp, layers):
    ws, wsms, pts, bts, gws = [], [], [], [], []
    for l in layers:
        wst, wsm, pt, bt, gw = host_layer_weights(inp, l)
        ws.append(wst); wsms.append(wsm); pts.append(pt); bts.append(bt); gws.append(gw)
    return {
        "wst": np.ascontiguousarray(np.concatenate(ws, 0)),
        "wsm": np.stack(wsms, 0),
        "ptab": np.stack(pts, 0),
        "btab": np.stack(bts, 0),
        "gw": np.stack(gws, 0),
        "fnw": np.ascontiguousarray(np.asarray(inp["final_norm_w"], np.float32).reshape(8, 128).T),
        "fnwb": np.ascontiguousarray(np.asarray(inp["final_norm_w"], np.float32)),
        "cst": host_consts(),
        "cstr": np.full((128, 128), 1.0 / 1024.0, np.float32),
    }


def make_in_maps(inp, seqs, n_layers):
    S = seqs[0].shape[0]
    common = _layer_maps(inp, range(n_layers))
    common["cs"] = host_cs(S)
    return [dict(common, x=np.ascontiguousarray(s, dtype=np.float32)) for s in seqs]


def make_in_maps_pp(inp, seqs):
    n = len(seqs)
    S = seqs[0].shape[0]
    cs = host_cs(S)
    pad = np.zeros((TT, 128), np.float32)
    la = _layer_maps(inp, [0])
    lb = _layer_maps(inp, [1])
    fa = np.zeros((128, 2), np.float32); fa[:, 0] = 1.0
    fb = np.zeros((128, 2), np.float32); fb[:, 1] = 1.0
    zeros = np.zeros((S, D), np.float32)
    maps = []
    for i in range(n):
        maps.append(dict(la, x=np.ascontiguousarray(seqs[i], dtype=np.float32), cs=np.concatenate([cs, pad], 0), flag=fa))
    for i in range(n):
        maps.append(dict(lb, x=zeros, cs=np.concatenate([pad, cs], 0), flag=fb))
    return maps


def kernel(**inputs):
    inp = {k: np.asarray(v) for k, v in inputs.items()}
    x = inp["x"].astype(np.float32, copy=False)
    B, S, _ = x.shape
    groups = [[b, b + B] for b in range(B)]
    nc = _get_prog(S // TT, 1, groups)
    in_maps = make_in_maps_pp(inp, [x[b] for b in range(B)])
    res = run_bass_kernel_spmd(nc, in_maps, core_ids=list(range(2 * B)))
    out = np.stack([res.results[B + b]["out"] for b in range(B)], 0)
    return out.astype(np.float32, copy=False)
```

```python
import math
from contextlib import ExitStack

import numpy as np
import concourse.bass as bass
import concourse.mybir as mybir
from concourse.bass_utils import run_bass_kernel_spmd

F32 = mybir.dt.float32
BF16 = mybir.dt.bfloat16
F32R = mybir.dt.float32r
AF = mybir.ActivationFunctionType
ALU = mybir.AluOpType
GRAN = 512
ESZ = {F32: 4, BF16: 2, F32R: 4}
PE, ACT, DVE, POOL, SP = "tensor", "scalar", "vector", "gpsimd", "sync"
ENGS = [PE, ACT, DVE, POOL, SP]

D = 1024
TT = 512
NCH = 4
EPS = 1e-6
NBLK = 40
N_CORES = 8


class Reg:
    __slots__ = ("t", "c0", "c1", "ap")

    def __init__(self, t, c0, c1, ap):
        self.t, self.c0, self.c1, self.ap = t, c0, c1, ap

    def grans(self):
        e = self.t.esz
        tid = self.t.tid
        return [(tid, g) for g in range((self.c0 * e) // GRAN, (self.c1 * e - 1) // GRAN + 1)]

    def m(self, fn):
        return Reg(self.t, self.c0, self.c1, fn(self.ap))


class T:
    _n = 0

    def __init__(self, handle, cols, dtype, parts=128, psum=False):
        self.h, self.cols, self.dtype, self.esz, self.parts = handle, cols, dtype, ESZ[dtype], parts
        T._n += 1
        self.tid = T._n
        self.psum = psum

    def r(self, c0=0, c1=None, p0=0, p1=None):
        c1 = self.cols if c1 is None else c1
        p1 = self.parts if p1 is None else p1
        return Reg(self, c0, c1, self.h[p0:p1, c0:c1])

    def as_dtype(self, dtype):
        t = T.__new__(T)
        t.h = self.h.bitcast(dtype)
        t.dtype, t.esz, t.parts, t.tid = dtype, ESZ[dtype], self.parts, self.tid
        t.psum = self.psum
        t.cols = self.cols * self.esz // t.esz
        return t


class Sub:
    def __init__(self, parent, off, cols, parts=128):
        self.p, self.off, self.cols, self.parts = parent, off, cols, parts

    def r(self, c0=0, c1=None, p0=0, p1=None):
        c1 = self.cols if c1 is None else c1
        p1 = self.parts if p1 is None else p1
        return self.p.r(self.off + c0, self.off + c1, p0, p1)


class DTrack:
    def __init__(self):
        T._n += 1
        self.tid, self.esz, self.psum = T._n, 4, False

    def r(self):
        return Reg(self, 0, 1, None)


class Prog:
    def __init__(self, nc):
        self.nc = nc
        self.ops = {e: [] for e in ENGS}
        self.gr = {}
        self.waited = {e: {} for e in ENGS}
        self.needed = {e: set() for e in ENGS}
        self.dma_cnt = {}
        self.cc_cnt = {}
        self.final = []
        self.banks = []
        self.bank_i = 0
        self.bank_last = {}
        self.phase = None
        self.annotate = False

    def psum(self):
        b = self.banks[self.bank_i % len(self.banks)]
        self.bank_i += 1
        return b

    def _deps(self, eng, reads, writes):
        deps = {}

        def add(ev):
            if ev is None:
                return
            k, v = ev
            if k == PE and eng == PE:
                return
            if deps.get(k, -1) < v:
                deps[k] = v
        gr = self.gr
        for r in reads:
            for g in r.grans():
                st = gr.get(g)
                if st is not None:
                    add(st[0])
        for w in writes:
            for g in w.grans():
                st = gr.get(g)
                if st is not None:
                    add(st[0])
                    for ev in st[1]:
                        add(ev)
        for r in list(reads) + list(writes):
            if r.t.psum:
                bl = self.bank_last.get(r.t.tid)
                if bl:
                    for e2, ev in bl.items():
                        if e2 != eng:
                            add(ev)
        out = []
        wd = self.waited[eng]
        for k, v in deps.items():
            if wd.get(k, -1) >= v:
                continue
            wd[k] = v
            out.append((k, v))
            if k in self.needed:
                self.needed[k].add(v)
        return out

    def _mark(self, ev, reads, writes):
        gr = self.gr
        for r in list(reads) + list(writes):
            if r.t.psum:
                self.bank_last.setdefault(r.t.tid, {})[ev[0]] = ev
        for r in reads:
            for g in r.grans():
                st = gr.get(g)
                if st is None:
                    gr[g] = [None, [ev]]
                else:
                    st[1].append(ev)
        for w in writes:
            for g in w.grans():
                gr[g] = [ev, []]

    def op(self, eng, fn, reads=(), writes=()):
        waits = self._deps(eng, reads, writes)
        seq = len(self.ops[eng])
        self.ops[eng].append([fn, waits, None, self.phase])
        self._mark((eng, seq), reads, writes)

    def dma(self, q, semkey, out_ap, in_ap, reads=(), writes=(), final=False):
        waits = self._deps(q, reads, writes)
        n = self.dma_cnt.get(semkey, 0) + 1
        self.dma_cnt[semkey] = n
        ev = (("dma", semkey), n * 16)
        self.ops[q].append([lambda e: e.dma_start(out=out_ap, in_=in_ap), waits, ev, self.phase])
        self._mark(ev, reads, writes)
        if final:
            self.final.append(ev)

    def cc(self, semkey, fn, reads=(), writes=()):
        waits = self._deps(POOL, reads, writes)
        n = self.cc_cnt.get(semkey, 0) + 1
        self.cc_cnt[semkey] = n
        ev = (("cc", semkey), n)
        self.ops[POOL].append([fn, waits, ev, self.phase])
        self._mark(ev, reads, writes)

    def emit(self):
        nc = self.nc
        with ExitStack() as es:
            sems = {}
            for e in ENGS:
                sems[e] = es.enter_context(nc.semaphore("s_" + e))
            for k in self.dma_cnt:
                sems[("dma", k)] = es.enter_context(nc.semaphore("d_" + str(k)))
            for k in self.cc_cnt:
                sems[("cc", k)] = es.enter_context(nc.semaphore("c_" + str(k)))
            block = es.enter_context(nc.Block())
            val = {}
            for e in ENGS:
                c = 0
                nd = self.needed[e]
                for i in range(len(self.ops[e])):
                    if i in nd:
                        c += 1
                        val[(e, i)] = c

            def run(ename, eobj):
                nd = self.needed[ename]
                for i, (fn, waits, dmaev, phase) in enumerate(self.ops[ename]):
                    for k, v in waits:
                        if isinstance(k, tuple):
                            eobj.wait_ge(sems[k], v)
                        else:
                            eobj.wait_ge(sems[k], val[(k, v)])
                    ins = fn(eobj)
                    if self.annotate and phase:
                        ins.annotate(phase)
                    if dmaev is not None and dmaev[0][0] == "cc":
                        ins.then_inc(sems[dmaev[0]])
                    elif dmaev is not None:
                        ins.then_inc(sems[dmaev[0]], 16)
                    elif i in nd:
                        ins.then_inc(sems[ename], 1)
                if ename == SP:
                    fin = {}
                    for k, v in self.final:
                        fin[k] = max(fin.get(k, 0), v)
                    for k, v in fin.items():
                        eobj.wait_ge(sems[k], v)

            @block.tensor
            def _(e):
                run(PE, e)

            @block.scalar
            def _(e):
                run(ACT, e)

            @block.vector
            def _(e):
                run(DVE, e)

            @block.gpsimd
            def _(e):
                run(POOL, e)

            @block.sync
            def _(e):
                run(SP, e)


def _regs(*xs):
    return [x for x in xs if isinstance(x, Reg)]


def _a(x):
    return x.ap if isinstance(x, Reg) else x


def mm(P, out, lhsT, rhs, start=True, stop=True):
    P.op(PE, lambda e: e.matmul(out.ap, lhsT.ap, rhs.ap, start=start, stop=stop), reads=[lhsT, rhs], writes=[out])


def tr(P, out, in_, ident):
    P.op(PE, lambda e: e.transpose(out.ap, in_.ap, ident.ap), reads=[in_, ident], writes=[out])


def act(P, out, in_, func, bias=None, scale=None, accum=None, eng=ACT):
    kw = {}
    if bias is not None:
        kw["bias"] = _a(bias)
    if scale is not None:
        kw["scale"] = _a(scale)
    if accum is not None:
        kw["accum_out"] = accum.ap
    P.op(eng, lambda e: e.activation(out.ap, in_.ap, func, **kw),
         reads=_regs(in_, bias, scale), writes=_regs(out, accum))


def tt(P, eng, out, in0, in1, op):
    P.op(eng, lambda e: e.tensor_tensor(out.ap, in0.ap, in1.ap, op), reads=[in0, in1], writes=[out])


def ts(P, eng, out, in0, s1, s2, op0, op1=None):
    if op1 is None:
        P.op(eng, lambda e: e.tensor_scalar(out.ap, in0.ap, _a(s1), None, op0), reads=_regs(in0, s1), writes=[out])
    else:
        P.op(eng, lambda e: e.tensor_scalar(out.ap, in0.ap, _a(s1), _a(s2), op0, op1),
             reads=_regs(in0, s1, s2), writes=[out])


def stt(P, eng, out, in0, scalar, in1, op0, op1):
    P.op(eng, lambda e: e.scalar_tensor_tensor(out.ap, in0.ap, _a(scalar), in1.ap, op0, op1),
         reads=_regs(in0, scalar, in1), writes=[out])


def cp(P, eng, out, in_):
    if eng == ACT:
        act(P, out, in_, AF.Copy)
    else:
        P.op(eng, lambda e: e.tensor_copy(out.ap, in_.ap), reads=[in_], writes=[out])


C_IDENT, C_TRI, C_NEG, C_QDEC, C_KDEC, C_GC, C_ONES, C_TRIR, NCST = 0, 128, 256, 384, 388, 392, 396, 524, 652
PT_AN, PT_MN, PT_RN, PT_SN, PT_GN, PT_CW, PT_CB, PT_MB, NPT = 0, 8, 16, 20, 28, 32, 80, 92, 116
BT_DTB, BT_ALOG, BT_D, BT_GB, NBT = 0, 16, 32, 48, 304

O_RQ, O_RK, O_RV, O_RG, O_SZ, O_SXBC, O_SDT, O_GQ, O_GK, O_GV, O_GR, O_GLR, O_MG = (
    0, 512, 1024, 1536, 2048, 3072, 4608, 4624, 4880, 5136, 5648, 6160, 6176)


def host_consts():
    c = np.zeros((128, NCST), np.float32)
    idx = np.arange(128)
    c[:, C_IDENT:C_IDENT + 128] = np.eye(128, dtype=np.float32)
    tri = (idx[:, None] <= idx[None, :]).astype(np.float32)
    c[:, C_TRI:C_TRI + 128] = tri
    c[:, C_NEG:C_NEG + 128] = (tri - 1.0) * 30000.0
    h = np.arange(4, dtype=np.float64)
    lg = np.log1p(-np.exp2(-5.0 - h))
    pos = idx.astype(np.float64)
    c[:, C_QDEC:C_QDEC + 4] = np.exp(lg[None, :] * (pos[:, None] + 1.0))
    c[:, C_KDEC:C_KDEC + 4] = np.exp(-lg[None, :] * (pos[:, None] + 1.0)) * (128.0 ** -0.5)
    c[:, C_GC:C_GC + 4] = np.exp(lg * 128.0)[None, :]
    c[:, C_ONES:C_ONES + 128] = 1.0
    c[:, C_TRIR:C_TRIR + 128] = (idx[:, None] > idx[None, :]).astype(np.float32)
    return c


def host_cs(seq):
    inv_freq = (10000.0 ** (-np.arange(0, 128, 2, dtype=np.float32) / np.float32(128))).astype(np.float32)
    ang = np.arange(seq, dtype=np.float32)[:, None] * inv_freq[None, :]
    return np.concatenate([np.cos(ang), np.sin(ang)], axis=1).astype(np.float32)


def _blk(w):
    K, N = w.shape
    kc = K // 128
    return np.ascontiguousarray(w.reshape(kc, 128, N).transpose(1, 0, 2).reshape(128, kc * N))


def host_layer_weights(inp, l):
    w_in = inp["w_in"][l]
    blocks = []
    add = lambda w: blocks.append(_blk(w))
    add(w_in[:, O_RQ:O_RQ + 512]); add(w_in[:, O_RK:O_RK + 512]); add(w_in[:, O_RV:O_RV + 512]); add(w_in[:, O_RG:O_RG + 512])
    add(inp["ret_w_o"][l])
    add(w_in[:, O_MG:O_MG + 512]); add(w_in[:, O_MG + 512:O_MG + 1024])
    add(w_in[:, O_SZ:O_SZ + 512]); add(w_in[:, O_SZ + 512:O_SZ + 1024])
    for j in range(3):
        add(w_in[:, O_SXBC + 512 * j:O_SXBC + 512 * (j + 1)])
    add(inp["ssd_w_o"][l][:, 0:512]); add(w_in[:, O_MG + 1024:O_MG + 1536])
    add(inp["ssd_w_o"][l][:, 512:1024]); add(w_in[:, O_MG + 1536:O_MG + 2048])
    add(w_in[:, O_GQ:O_GQ + 512]); add(w_in[:, O_GV:O_GV + 512]); add(w_in[:, O_GR:O_GR + 512])
    add(inp["gla_w_o"][l])
    add(w_in[:, O_MG + 2048:O_MG + 2560]); add(w_in[:, O_MG + 2560:O_MG + 3072])
    add(inp["w_out"][l][:, 0:512]); add(inp["w_out"][l][:, 512:1024])
    for g in range(4):
        add(inp["w_up"][l][:, g * 1024:g * 1024 + 512]); add(inp["w_up"][l][:, g * 1024 + 512:(g + 1) * 1024])
        add(inp["w_down"][l][g * 1024:(g + 1) * 1024, 0:512]); add(inp["w_down"][l][g * 1024:(g + 1) * 1024, 512:1024])
    assert len(blocks) == NBLK
    wst = np.stack(blocks, 0)
    sm = np.concatenate([w_in[:, O_SDT:O_SDT + 16], w_in[:, O_GLR:O_GLR + 16]], axis=1)
    wsm = _blk(sm)
    pt = np.zeros((128, NPT), np.float32)
    colmaj = lambda v: np.ascontiguousarray(v.reshape(-1, 128).T)
    pt[:, PT_AN:PT_AN + 8] = colmaj(inp["attn_norm_w"][l])
    pt[:, PT_MN:PT_MN + 8] = colmaj(inp["mlp_norm_w"][l])
    pt[:, PT_RN:PT_RN + 4] = colmaj(inp["ret_norm_w"][l])
    pt[:, PT_SN:PT_SN + 8] = colmaj(inp["ssd_norm_w"][l])
    pt[:, PT_GN:PT_GN + 4] = colmaj(inp["gla_norm_w"][l])
    cw = inp["ssd_conv_w"][l]
    pt[:, PT_CW:PT_CW + 48] = cw.T.reshape(12, 128, 4).transpose(1, 0, 2).reshape(128, 48)
    pt[:, PT_CB:PT_CB + 12] = colmaj(inp["ssd_conv_b"][l])
    pt[:, PT_MB:PT_MB + 24] = colmaj(inp["merge_gate_b"][l])
    bt = np.concatenate([inp["ssd_dt_bias"][l], inp["ssd_a_log"][l], inp["ssd_d"][l], inp["gla_gate_b"][l]]).astype(np.float32)
    return wst, wsm, pt, bt, np.ascontiguousarray(inp["gla_gate_w"][l])


STOP = [None]
ANNOTATE = [False]


class _Stop(Exception):
    pass


def build_program(n_tiles, n_layers, pp_groups=None):
    PP = pp_groups is not None
    n_steps = n_tiles + 1 if PP else n_tiles
    S = n_tiles * TT
    nc = bass.Bass("TRN2", target_bir_lowering=False)
    nc.dge_precook = False
    T._n = 0
    dram = lambda name, shape, dt, kind="ExternalInput": nc.dram_tensor(name, shape, dt, kind=kind).ap()
    x_d = dram("x", [S, D], F32)
    cs_d = dram("cs", [n_steps * TT, 128], F32)
    if PP:
        flag_d = dram("flag", [128, 2], F32)
        cc_in = nc.dram_tensor("cc_in", [128, 8 * TT], F32)
        gath = nc.dram_tensor("gath", [256, 8 * TT], F32)
    wst_d = dram("wst", [n_layers * NBLK, 128, 4096], F32R)
    wsm_d = dram("wsm", [n_layers, 128, 256], F32R)
    ptab_d = dram("ptab", [n_layers, 128, NPT], F32)
    btab_d = dram("btab", [n_layers, NBT], F32)
    gw_d = dram("gw", [n_layers, 16, 256], F32)
    fnw_d = dram("fnw", [128, 8], F32)
    fnwb_d = dram("fnwb", [D], F32)
    cst_d = dram("cst", [128, NCST], F32)
    cstr_d = dram("cstr", [128, 128], F32R)
    out_d = dram("out", [S, D], F32, kind="ExternalOutput")
    dbg_d = {}

    with ExitStack() as es:
        P = Prog(nc)
        P.annotate = ANNOTATE[0]

        def sb(name, cols, dt=F32, parts=128):
            return T(es.enter_context(nc.sbuf_tensor("sb_" + name, [parts, cols], dt)), cols, dt, parts)

        for i in range(8):
            P.banks.append(T(es.enter_context(nc.psum_tensor("psb%d" % i, [128, 512], F32)), 512, F32, psum=True))

        xres = sb("xres", 8 * TT)
        hbuf = sb("hbuf", 8 * TT, F32R)
        mrg = sb("mrg", 8 * TT, F32R)
        big = sb("big", 4096)
        bigr = big.as_dtype(F32R)
        NWB = 4 if PP else 3
        wbuf = [sb("wbuf%d" % i, 4096, F32R) for i in range(NWB)]
        mix = sb("mix", 10240, BF16)
        cst = sb("cst", NCST)
        cstr = sb("cstr", 128, F32R)
        identb = sb("identb", 128, BF16)
        csb = [sb("csb%d" % i, 512) for i in range(2)]
        fnw = sb("fnw", 8)
        flag = sb("flag", 2)
        fnw_bc = sb("fnw_bc", D)
        oss = sb("oss", 8)
        ptab = [sb("ptab%d" % l, NPT) for l in range(n_layers)]
        btab = [sb("btab%d" % l, NBT) for l in range(n_layers)]
        negA = [sb("negA%d" % l, 16) for l in range(n_layers)]
        gwt = [sb("gwt%d" % l, 256, F32, 16) for l in range(n_layers)]
        wsm = [sb("wsm%d" % l, 256, F32R) for l in range(n_layers)]
        retS = [sb("retS%d" % l, 512) for l in range(n_layers)]
        retSb = [sb("retSb%d" % l, 512, BF16) for l in range(n_layers)]
        ssdS = [sb("ssdS%d" % l, 1024) for l in range(n_layers)]
        ssdSb = [sb("ssdSb%d" % l, 1024, BF16) for l in range(n_layers)]
        glaS = [sb("glaS%d" % l, 256) for l in range(n_layers)]
        glaSb = [sb("glaSb%d" % l, 256, BF16) for l in range(n_layers)]
        halo = [sb("halo%d" % l, 48) for l in range(n_layers)]
        scr = sb("scr", 7168)
        scr_r = scr.as_dtype(F32R)
        scr_b = scr.as_dtype(BF16)
        K32 = lambda kb, cols, parts=128: Sub(scr, int(kb * 256), cols, parts)
        K32R = lambda kb, cols: Sub(scr_r, int(kb * 256), cols)
        KBF = lambda kb, cols: Sub(scr_b, int(kb * 512), cols)
        sq = [sb("sq0", TT, F32R), sb("sq1", TT, F32R)]
        nrm_a = K32(4, TT)
        nrm_r = K32(6, TT)
        gate_sb = [K32(8, TT), K32(10, TT)]
        gtmp = [K32(12, TT), K32(14, TT)]
        relu_t = [K32(16, TT), K32(18, TT)]
        rtmp = [K32(i, 256) for i in range(4)]
        rot = K32(4, 512)
        qT_r, kT_r, sT_r = KBF(6, 512), KBF(7, 512), KBF(8, 512)
        junk_r, og_r, stmp = K32(10, 512), K32(12, 512), K32(14, 512)
        stage = [K32(16, TT + 4), K32(18.5, TT + 4)]
        cacc = [K32(21, TT), K32(23, TT)]
        A1 = [K32(0, 512), K32(2, 512)]
        ET = [K32(4, 512), K32(6, 512)]
        GT = KBF(8, 2048)
        cb_sb = K32(12, 256)
        Xbf = KBF(13, 1024)
        XW = KBF(15, 1024)
        Bbf = KBF(17, 256)
        y1 = K32(18, 1024)
        y2 = K32(22, 1024)
        junk_s = K32(26, 512)
        ostage = [K32(16, 1024), K32(20, 1024)]
        ojunk = K32(24, 512)
        glr_sb = K32(0, TT, 16)
        lp, gtm, eb, enb, erev = K32(2, 256), K32(3, 256), K32(4, 256), K32(5, 256), K32(6, 256)
        qin, kin, kst = KBF(7, 256), KBF(7.5, 256), KBF(8, 256)
        qT_g, kT_g, sT_g = KBF(9, 512), KBF(10, 512), KBF(11, 512)
        junk_g, og_g = K32(12, 512), K32(14, 512)
        ss4 = sb("ss4", 8)
        rs4 = sb("rs4", 8)
        dt_all = sb("dt_all", 64)
        dtt = sb("dtt", 64)
        a_all = sb("a_all", 64)
        acs_all = sb("acs_all", 64)
        eacs_all = sb("eacs_all", 64)
        wdec = sb("wdec", 16)
        sdec = sb("sdec", 16)
        gdec = sb("gdec", 2)

        def C(c0, n, p0=0, p1=128):
            return cst.r(c0, c0 + n, p0, p1)

        ident = C(C_IDENT, 128)
        tri = C(C_TRI, 128)
        identb_r = identb.r()

        P.dma(SP, "cst", cst.r().ap, cst_d, writes=[cst.r()])
        P.dma(SP, "cstr", cstr.r().ap, cstr_d, writes=[cstr.r()])
        P.dma(SP, "fnw", fnw.r().ap, fnw_d, writes=[fnw.r()])
        P.dma(POOL, "fnwb", fnw_bc.r().ap, fnwb_d.partition_broadcast(128), writes=[fnw_bc.r()])
        if PP:
            P.dma(SP, "flag", flag.r().ap, flag_d, writes=[flag.r()])
            tin, tg = DTrack(), DTrack()
        for l in range(n_layers):
            P.dma(SP, "ptab", ptab[l].r().ap, ptab_d[l], writes=[ptab[l].r()])
            P.dma(POOL, "btab", btab[l].r().ap, btab_d[l].partition_broadcast(128), writes=[btab[l].r()])
            P.dma(SP, "gwt", gwt[l].r().ap, gw_d[l], writes=[gwt[l].r()])
            P.dma(SP, "wsm", wsm[l].r().ap, wsm_d[l], writes=[wsm[l].r()])
        cp(P, DVE, identb_r, ident)
        for l in range(n_layers):
            act(P, negA[l].r(), btab[l].r(BT_ALOG, BT_ALOG + 16), AF.Exp)
            ts(P, DVE, negA[l].r(), negA[l].r(), -1.0, None, ALU.mult)
            for t_ in (retS[l], ssdS[l], glaS[l], halo[l]):
                P.op(POOL, (lambda tt_: (lambda e: e.memset(tt_.r().ap, 0.0)))(t_), writes=[t_.r()])
            for t_ in (retSb[l], ssdSb[l], glaSb[l]):
                P.op(POOL, (lambda tt_: (lambda e: e.memset(tt_.r().ap, 0.0)))(t_), writes=[t_.r()])

        sched = []
        for ti in range(n_steps):
            for l in range(n_layers):
                for b in range(NBLK):
                    sched.append(l * NBLK + b)
        wstate = {"issued": 0, "used": 0}

        def w_issue():
            i = wstate["issued"]
            if i >= len(sched):
                return
            buf = wbuf[i % NWB]
            P.dma(SP, "w%d" % (i % NWB), buf.r().ap, wst_d[sched[i]], writes=[buf.r()])
            wstate["issued"] = i + 1

        def w_next():
            i = wstate["used"]
            wstate["used"] = i + 1
            return wbuf[i % NWB]

        def w_done():
            w_issue()

        for _ in range(NWB):
            w_issue()

        def xk(kc, c0=0, c1=TT):
            return xres.r(kc * TT + c0, kc * TT + c1)

        def hk(kc, c0=0, c1=TT):
            return hbuf.r(kc * TT + c0, kc * TT + c1)

        def rmsnorm(wcol, out_fn):
            ps = P.psum()
            for kc in range(8):
                s = sq[kc % 2]
                act(P, s.r(), xk(kc), AF.Square)
                mm(P, ps.r(), cstr.r(), s.r(), start=(kc == 0), stop=(kc == 7))
            act(P, nrm_a.r(), ps.r(), AF.Ln, bias=EPS)
            act(P, nrm_r.r(), nrm_a.r(), AF.Exp, scale=-0.5)
            for kc in range(8):
                stt(P, DVE, out_fn(kc), xk(kc), wcol(kc), nrm_r.r(), ALU.mult, ALU.mult)

        def proj_tm(wb, ncols, evac, kstride=512, col0=0):
            for c in range(NCH):
                ps = P.psum()
                for kc in range(8):
                    mm(P, ps.r(0, ncols), hk(kc, c * 128, (c + 1) * 128),
                       wb.r(kc * kstride + col0, kc * kstride + col0 + ncols), start=(kc == 0), stop=(kc == 7))
                evac(c, ps)

        def proj_fm(wb, rhs_fn, nk, kstride, col0, m=128):
            ps = P.psum()
            for kc in range(nk):
                mm(P, ps.r(0, TT, 0, m), wb.r(kc * kstride + col0, kc * kstride + col0 + m), rhs_fn(kc),
                   start=(kc == 0), stop=(kc == nk - 1))
            return ps

        def outT(kc, c0=0, c1=TT):
            return bigr.r(kc * TT + c0, kc * TT + c1)

        def group_post(ps_o, G_fn, nw_col, c, junk, og):
            for h in range(4):
                act(P, junk.r(h * 128, (h + 1) * 128), ps_o.r(h * 128, (h + 1) * 128), AF.Square,
                    scale=128.0 ** -0.5, accum=ss4.r(h, h + 1))
            act(P, rs4.r(0, 4), ss4.r(0, 4), AF.Ln, bias=EPS)
            act(P, rs4.r(4, 8), rs4.r(0, 4), AF.Exp, scale=-0.5)
            for h in range(4):
                stt(P, DVE, og.r(h * 128, (h + 1) * 128), ps_o.r(h * 128, (h + 1) * 128), rs4.r(4 + h, 5 + h),
                    G_fn(h), ALU.mult, ALU.mult)
            pt_ = P.psum()
            for h in range(4):
                tr(P, pt_.r(h * 128, (h + 1) * 128), og.r(h * 128, (h + 1) * 128), ident)
            for h in range(4):
                act(P, outT(h, c * 128, (c + 1) * 128), pt_.r(h * 128, (h + 1) * 128), AF.Copy, scale=nw_col(h))

        def merge_branch(br, l, wo, nk, kstride_o, wmg_blocks, dcs=range(8)):
            for dc in dcs:
                wo_b, col_o = wo(dc)
                ps_o = proj_fm(wo_b, lambda kc: outT(kc), nk, kstride_o, col_o)
                ps_g = proj_fm(wmg_blocks[dc // 4], lambda kc: hk(kc), 8, 512, (dc % 4) * 128)
                g = gate_sb[dc % 2]
                act(P, g.r(), ps_g.r(), AF.Sigmoid, bias=ptab[l].r(PT_MB + br * 8 + dc, PT_MB + br * 8 + dc + 1))
                dst = mrg.r(dc * TT, (dc + 1) * TT)
                if br == 0:
                    tt(P, DVE, dst, ps_o.r(), g.r(), ALU.mult)
                else:
                    tmp = gtmp[dc % 2]
                    tt(P, DVE, tmp.r(), ps_o.r(), g.r(), ALU.mult)
                    tt(P, POOL, dst, dst, tmp.r(), ALU.add)

        def b3(ap, a, b):
            return ap.unsqueeze(2).broadcast_to([ap.shape[0], a, b])

        def layer(l, ti, after_gla=None):
            pt = ptab[l]
            pcol = lambda base: (lambda k: pt.r(base + k, base + k + 1))
            csr = csb[ti % 2]
            P.phase = "norm1"
            rmsnorm(pcol(PT_AN), hk)

            if STOP[0] == 1:
                raise _Stop()
            P.phase = "ret_proj"
            Qt = lambda c, c0=0, c1=512: mix.r(c * 512 + c0, c * 512 + c1)
            Kt = lambda c, c0=0, c1=512: mix.r(2048 + c * 512 + c0, 2048 + c * 512 + c1)
            Vt = lambda c, c0=0, c1=512: mix.r(4096 + c * 512 + c0, 4096 + c * 512 + c1)
            Gt = lambda c, c0=0, c1=512: mix.r(6144 + c * 512 + c0, 6144 + c * 512 + c1)

            def rotary_evac(dst_fn, dec_c0):
                def f(c, ps):
                    v4 = lambda r_: r_.m(lambda ap: ap.rearrange("p (h t d) -> p h t d", h=4, t=2))
                    x1 = v4(ps.r()).m(lambda ap: ap[:, :, 0, :])
                    x2 = v4(ps.r()).m(lambda ap: ap[:, :, 1, :])
                    cos = csr.r(c * 128, c * 128 + 64).m(lambda ap: ap.unsqueeze(1).broadcast_to([128, 4, 64]))
                    sin = csr.r(c * 128 + 64, c * 128 + 128).m(lambda ap: ap.unsqueeze(1).broadcast_to([128, 4, 64]))
                    v3 = lambda r_: r_.m(lambda ap: ap.rearrange("p (h d) -> p h d", h=4))
                    t1, t2, t3, t4 = [v3(rtmp[i].r()) for i in range(4)]
                    tt(P, DVE, t1, x1, cos, ALU.mult)
                    tt(P, DVE, t2, x2, sin, ALU.mult)
                    tt(P, DVE, t3, x1, sin, ALU.mult)
                    tt(P, DVE, t4, x2, cos, ALU.mult)
                    r1 = v4(rot.r()).m(lambda ap: ap[:, :, 0, :])
                    r2 = v4(rot.r()).m(lambda ap: ap[:, :, 1, :])
                    tt(P, POOL, r1, t1, t2, ALU.subtract)
                    tt(P, POOL, r2, t3, t4, ALU.add)
                    dec = C(dec_c0, 4).m(lambda ap: b3(ap, 4, 128))
                    tt(P, POOL, dst_fn(c).m(lambda ap: ap.rearrange("p (h d) -> p h d", h=4)),
                       rot.r().m(lambda ap: ap.rearrange("p (h d) -> p h d", h=4)), dec, ALU.mult)
                return f

            wb = w_next(); proj_tm(wb, 512, rotary_evac(Qt, C_QDEC)); w_done()
            wb = w_next(); proj_tm(wb, 512, rotary_evac(Kt, C_KDEC)); w_done()
            wb = w_next(); proj_tm(wb, 512, lambda c, ps: cp(P, ACT, Vt(c), ps.r())); w_done()
            wb = w_next(); proj_tm(wb, 512, lambda c, ps: act(P, Gt(c), ps.r(), AF.Silu)); w_done()
            if STOP[0] == 2:
                raise _Stop()
            P.phase = "ret_chunks"
            S_, Sb_ = retS[l], retSb[l]
            for c in range(NCH):
                pq = P.psum().as_dtype(BF16)
                for h in range(4):
                    tr(P, pq.r(h * 128, (h + 1) * 128), Qt(c, h * 128, (h + 1) * 128), identb_r)
                cp(P, ACT, qT_r.r(), pq.r(0, 512))
                pk = P.psum().as_dtype(BF16)
                for h in range(4):
                    tr(P, pk.r(h * 128, (h + 1) * 128), Kt(c, h * 128, (h + 1) * 128), identb_r)
                cp(P, DVE, kT_r.r(), pk.r(0, 512))
                psc = P.psum()
                for h in range(4):
                    mm(P, psc.r(h * 128, (h + 1) * 128), kT_r.r(h * 128, (h + 1) * 128), qT_r.r(h * 128, (h + 1) * 128))
                tt(P, DVE, sT_r.r().m(lambda ap: ap.rearrange("p (h d) -> p h d", h=4)),
                   psc.r().m(lambda ap: ap.rearrange("p (h d) -> p h d", h=4)),
                   tri.m(lambda ap: ap.unsqueeze(1).broadcast_to([128, 4, 128])), ALU.mult)
                po = P.psum()
                for h in range(4):
                    hs = slice(h * 128, (h + 1) * 128)
                    mm(P, po.r(h * 128, (h + 1) * 128), sT_r.r(h * 128, (h + 1) * 128), Vt(c, h * 128, (h + 1) * 128), True, False)
                    mm(P, po.r(h * 128, (h + 1) * 128), qT_r.r(h * 128, (h + 1) * 128), Sb_.r(h * 128, (h + 1) * 128), False, True)
                pds = P.psum()
                for h in range(4):
                    mm(P, pds.r(h * 128, (h + 1) * 128), Kt(c, h * 128, (h + 1) * 128), Vt(c, h * 128, (h + 1) * 128))
                tt(P, DVE, stmp.r(0, 512), pds.r(), S_.r(), ALU.add)
                tt(P, POOL, S_.r().m(lambda ap: ap.rearrange("p (h d) -> p h d", h=4)),
                   stmp.r(0, 512).m(lambda ap: ap.rearrange("p (h d) -> p h d", h=4)),
                   C(C_GC, 4).m(lambda ap: b3(ap, 4, 128)), ALU.mult)
                cp(P, POOL, Sb_.r(), S_.r())
                group_post(po, lambda h: Gt(c, h * 128, (h + 1) * 128), pcol(PT_RN), c, junk_r, og_r)
            if STOP[0] == 3:
                raise _Stop()
            P.phase = "ret_merge"
            wo_b = w_next()
            wmg = [w_next(), w_next()]
            merge_branch(0, l, lambda dc: (wo_b, dc * 128), 4, 1024, wmg)
            w_done(); w_done(); w_done()

            if STOP[0] == 4:
                raise _Stop()
            P.phase = "ssd_proj_conv"
            Zt = lambda c, c0=0, c1=1024: mix.r(c * 1024 + c0, c * 1024 + c1)
            XBC = lambda cc, c0=0, c1=TT: mix.r(4096 + cc * TT + c0, 4096 + cc * TT + c1)
            for half in range(2):
                wb = w_next()
                proj_tm(wb, 512, (lambda hf: (lambda c, ps: act(P, Zt(c, hf * 512, hf * 512 + 512), ps.r(), AF.Silu)))(half))
                w_done()
            for j in range(3):
                wb = w_next()
                for q in range(4):
                    cc = j * 4 + q
                    ps = proj_fm(wb, lambda kc: hk(kc), 8, 512, q * 128)
                    st = stage[cc % 2]
                    ca = cacc[cc % 2]
                    cp(P, POOL, st.r(0, 3), halo[l].r(cc * 4, cc * 4 + 3))
                    cp(P, ACT, st.r(3, 3 + TT), ps.r())
                    cp(P, POOL, halo[l].r(cc * 4, cc * 4 + 3), st.r(TT, TT + 3))
                    act(P, ca.r(), st.r(0, TT), AF.Copy, scale=pt.r(PT_CW + cc * 4, PT_CW + cc * 4 + 1))
                    for k in range(1, 4):
                        stt(P, DVE, ca.r(), st.r(k, k + TT), pt.r(PT_CW + cc * 4 + k, PT_CW + cc * 4 + k + 1),
                            ca.r(), ALU.mult, ALU.add)
                    act(P, XBC(cc), ca.r(), AF.Silu, bias=pt.r(PT_CB + cc, PT_CB + cc + 1))
                w_done()
            if STOP[0] == 5:
                raise _Stop()
            P.phase = "ssd_chunks"
            S_, Sb_ = ssdS[l], ssdSb[l]
            bt = btab[l]
            pdt = P.psum()
            for c in range(NCH):
                for kc in range(8):
                    mm(P, pdt.r(c * 16, c * 16 + 16), hk(kc, c * 128, (c + 1) * 128), wsm[l].r(kc * 32, kc * 32 + 16), kc == 0, kc == 7)
            v4c = lambda r_: r_.m(lambda ap: ap.rearrange("p (c h) -> p c h", c=NCH))
            tt(P, DVE, v4c(dtt.r()), v4c(pdt.r(0, 64)), bt.r(BT_DTB, BT_DTB + 16).m(lambda ap: ap.unsqueeze(1).broadcast_to([128, NCH, 16])), ALU.add)
            act(P, dtt.r(), dtt.r(), AF.Exp)
            act(P, dt_all.r(), dtt.r(), AF.Ln, bias=1.0)
            tt(P, DVE, v4c(a_all.r()), v4c(dt_all.r()), negA[l].r().m(lambda ap: ap.unsqueeze(1).broadcast_to([128, NCH, 16])), ALU.mult)
            pacs = P.psum()
            for c in range(NCH):
                mm(P, pacs.r(c * 16, c * 16 + 16), tri, a_all.r(c * 16, c * 16 + 16))
            cp(P, DVE, acs_all.r(), pacs.r(0, 64))
            act(P, eacs_all.r(), pacs.r(0, 64), AF.Exp)
            for c in range(NCH):
                cs_ = slice(c * 128, (c + 1) * 128)
                dt_sb = Sub(dt_all, c * 16, 16)
                a_sb = Sub(a_all, c * 16, 16)
                acs_sb = Sub(acs_all, c * 16, 16)
                eacs = Sub(eacs_all, c * 16, 16)
                pcb = P.psum()
                for g in range(2):
                    mm(P, pcb.r(g * 128, (g + 1) * 128), XBC(8 + g, c * 128, (c + 1) * 128), XBC(10 + g, c * 128, (c + 1) * 128))
                cp(P, ACT, cb_sb.r(), pcb.r(0, 256))
                def build_a1(hg_):
                    tt(P, POOL, A1[hg_ % 2].r().m(lambda ap: ap.rearrange("p (h l) -> p h l", h=4)),
                       a_sb.r(hg_ * 4, hg_ * 4 + 4).m(lambda ap: b3(ap, 4, 128)),
                       tri.m(lambda ap: ap.unsqueeze(1).broadcast_to([128, 4, 128])), ALU.mult)
                build_a1(0)
                for hg in range(4):
                    a1 = A1[hg % 2]
                    pb = P.psum()
                    mm(P, pb.r(), C(C_ONES, 128), a1.r())
                    if hg + 1 < 4:
                        build_a1(hg + 1)
                    et = ET[hg % 2]
                    for hh in range(4):
                        h = hg * 4 + hh
                        stt(P, DVE, et.r(hh * 128, (hh + 1) * 128), pb.r(hh * 128, (hh + 1) * 128), acs_sb.r(h, h + 1),
                            C(C_NEG, 128), ALU.subtract, ALU.add)
                    cp(P, DVE, sdec.r(hg * 4, hg * 4 + 4),
                       pb.r().m(lambda ap: ap.rearrange("p (h l) -> p h l", h=4)[:, :, 127]))
                    lt = et
                    act(P, lt.r(), et.r(), AF.Exp)
                    g = hg // 2
                    v4_ = lambda r_: r_.m(lambda ap: ap.rearrange("p (h l) -> p h l", h=4))
                    tt(P, DVE, v4_(lt.r()), v4_(lt.r()),
                       cb_sb.r(g * 128, (g + 1) * 128).m(lambda ap: ap.unsqueeze(1).broadcast_to([128, 4, 128])), ALU.mult)
                    tt(P, POOL, v4_(GT.r(hg * 512, (hg + 1) * 512)), v4_(lt.r()),
                       dt_sb.r(hg * 4, hg * 4 + 4).m(lambda ap: b3(ap, 4, 128)), ALU.mult)
                tt(P, DVE, wdec.r(), sdec.r(), acs_sb.r(), ALU.subtract)
                act(P, wdec.r(), wdec.r(), AF.Exp)
                tt(P, DVE, wdec.r(), wdec.r(), dt_sb.r(), ALU.mult)
                act(P, sdec.r(), sdec.r(), AF.Exp)
                px = P.psum().as_dtype(BF16)
                for hc in range(8):
                    tr(P, px.r(hc * 128, (hc + 1) * 128), XBC(hc, c * 128, (c + 1) * 128), identb_r)
                cp(P, ACT, Xbf.r(), px.r(0, 1024))
                tt(P, DVE, XW.r().m(lambda ap: ap.rearrange("p (h d) -> p h d", h=16)),
                   px.r(0, 1024).m(lambda ap: ap.rearrange("p (h d) -> p h d", h=16)),
                   wdec.r().m(lambda ap: b3(ap, 16, 64)), ALU.mult)
                pbt = P.psum().as_dtype(BF16)
                for g in range(2):
                    tr(P, pbt.r(g * 128, (g + 1) * 128), XBC(8 + g, c * 128, (c + 1) * 128), identb_r)
                cp(P, ACT, Bbf.r(), pbt.r(0, 256))
                pyd = [P.psum(), P.psum()]
                for h in range(16):
                    g, hl = h // 8, h % 8
                    mm(P, pyd[g].r(hl * 64, (hl + 1) * 64), GT.r(h * 128, (h + 1) * 128), Xbf.r(h * 64, (h + 1) * 64))
                pyo = [P.psum(), P.psum()]
                for g in range(2):
                    mm(P, pyo[g].r(), XBC(10 + g, c * 128, (c + 1) * 128), Sb_.r(g * 512, (g + 1) * 512))
                for g in range(2):
                    gs = slice(g * 512, (g + 1) * 512)
                    v8 = lambda r_: r_.m(lambda ap: ap.rearrange("p (h d) -> p h d", h=8))
                    tt(P, DVE, v8(y1.r(g * 512, (g + 1) * 512)), v8(pyo[g].r()),
                       eacs.r(g * 8, g * 8 + 8).m(lambda ap: b3(ap, 8, 64)), ALU.mult)
                    tt(P, DVE, y1.r(g * 512, (g + 1) * 512), y1.r(g * 512, (g + 1) * 512), pyd[g].r(), ALU.add)
                    tt(P, POOL, v8(y2.r(g * 512, (g + 1) * 512)), v8(Xbf.r(g * 512, (g + 1) * 512)),
                       bt.r(BT_D + g * 8, BT_D + g * 8 + 8).m(lambda ap: b3(ap, 8, 64)), ALU.mult)
                    tt(P, POOL, y1.r(g * 512, (g + 1) * 512), y1.r(g * 512, (g + 1) * 512), y2.r(g * 512, (g + 1) * 512), ALU.add)
                    tt(P, DVE, y1.r(g * 512, (g + 1) * 512), y1.r(g * 512, (g + 1) * 512), Zt(c, g * 512, (g + 1) * 512), ALU.mult)
                    act(P, junk_s.r(), y1.r(g * 512, (g + 1) * 512), AF.Square, scale=512.0 ** -0.5, accum=ss4.r(g, g + 1))
                act(P, rs4.r(0, 2), ss4.r(0, 2), AF.Ln, bias=EPS)
                act(P, rs4.r(4, 6), rs4.r(0, 2), AF.Exp, scale=-0.5)
                for g in range(2):
                    ts(P, DVE, y2.r(g * 512, (g + 1) * 512), y1.r(g * 512, (g + 1) * 512), rs4.r(4 + g, 5 + g), None, ALU.mult)
                for g in range(2):
                    pt_ = P.psum()
                    for hl in range(4):
                        hc = g * 4 + hl
                        tr(P, pt_.r(hl * 128, (hl + 1) * 128), y2.r(hc * 128, (hc + 1) * 128), ident)
                    for hl in range(4):
                        hc = g * 4 + hl
                        act(P, outT(hc, c * 128, (c + 1) * 128), pt_.r(hl * 128, (hl + 1) * 128), AF.Copy,
                            scale=pt.r(PT_SN + hc, PT_SN + hc + 1))
                pds = [P.psum(), P.psum()]
                for g in range(2):
                    mm(P, pds[g].r(), Bbf.r(g * 128, (g + 1) * 128), XW.r(g * 512, (g + 1) * 512))
                for g in range(2):
                    v8 = lambda r_: r_.m(lambda ap: ap.rearrange("p (h d) -> p h d", h=8))
                    tt(P, POOL, v8(S_.r(g * 512, (g + 1) * 512)), v8(S_.r(g * 512, (g + 1) * 512)),
                       sdec.r(g * 8, g * 8 + 8).m(lambda ap: b3(ap, 8, 64)), ALU.mult)
                    tt(P, DVE, S_.r(g * 512, (g + 1) * 512), S_.r(g * 512, (g + 1) * 512), pds[g].r(), ALU.add)
                    cp(P, ACT, Sb_.r(g * 512, (g + 1) * 512), S_.r(g * 512, (g + 1) * 512))
            if STOP[0] == 6:
                raise _Stop()
            P.phase = "ssd_merge"
            for q in range(2):
                wo_q = w_next()
                wmg_q = w_next()
                merge_branch(1, l, (lambda wq: (lambda dc: (wq, (dc % 4) * 128)))(wo_q), 8, 512, [wmg_q, wmg_q], range(q * 4, q * 4 + 4))
                w_done(); w_done()

            if STOP[0] == 7:
                raise _Stop()
            P.phase = "gla_proj"
            GQK = lambda c, c0=0, c1=512: mix.r(c * 512 + c0, c * 512 + c1)
            GV = lambda c, c0=0, c1=512: mix.r(2048 + c * 512 + c0, 2048 + c * 512 + c1)
            GG = lambda c, c0=0, c1=512: mix.r(4096 + c * 512 + c0, 4096 + c * 512 + c1)
            wb = w_next(); proj_tm(wb, 512, lambda c, ps: cp(P, ACT, GQK(c), ps.r())); w_done()
            wb = w_next(); proj_tm(wb, 512, lambda c, ps: cp(P, ACT, GV(c), ps.r())); w_done()
            wb = w_next(); proj_tm(wb, 512, lambda c, ps: act(P, GG(c), ps.r(), AF.Silu)); w_done()
            pg = P.psum()
            for kc in range(8):
                mm(P, pg.r(0, TT, 0, 16), wsm[l].r(kc * 32 + 16, kc * 32 + 32), hk(kc), kc == 0, kc == 7)
            cp(P, ACT, glr_sb.r(), pg.r(0, TT, 0, 16))
            if STOP[0] == 8:
                raise _Stop()
            P.phase = "gla_chunks"
            S_, Sb_ = glaS[l], glaSb[l]
            for c in range(NCH):
                pga = P.psum()
                mm(P, pga.r(0, 256), glr_sb.r(c * 128, (c + 1) * 128), gwt[l].r())
                tt(P, DVE, gtm.r(), pga.r(0, 256), bt.r(BT_GB, BT_GB + 256), ALU.add)
                act(P, gtm.r(), gtm.r(), AF.Exp, scale=-1.0)
                act(P, lp.r(), gtm.r(), AF.Ln, bias=1.0)
                if STOP[0] == 81:
                    raise _Stop()
                pb_ = P.psum()
                mm(P, pb_.r(0, 256), tri, lp.r())
                mm(P, pb_.r(256, 512), C(C_TRIR, 128), lp.r())
                if STOP[0] == 82:
                    raise _Stop()
                ptot = P.psum()
                for pr in range(2):
                    mm(P, ptot.r(pr, pr + 1), lp.r(pr * 128, (pr + 1) * 128), C(C_ONES, 1))
                if STOP[0] == 83:
                    raise _Stop()
                act(P, eb.r(), pb_.r(0, 256), AF.Exp, scale=-1.0 / 16.0)
                act(P, enb.r(), pb_.r(0, 256), AF.Exp, scale=1.0 / 16.0)
                act(P, erev.r(), pb_.r(256, 512), AF.Exp, scale=-1.0 / 16.0)
                act(P, gdec.r(), ptot.r(0, 2), AF.Exp, scale=-1.0 / 16.0)
                if STOP[0] == 84:
                    raise _Stop()
                stt(P, DVE, qin.r(), GQK(c, 0, 256), 0.125, eb.r(), ALU.mult, ALU.mult)
                tt(P, POOL, kin.r(), GQK(c, 256, 512), enb.r(), ALU.mult)
                tt(P, POOL, kst.r(), GQK(c, 256, 512), erev.r(), ALU.mult)
                if STOP[0] == 85:
                    raise _Stop()
                pq = P.psum().as_dtype(BF16)
                for pr in range(2):
                    tr(P, pq.r(pr * 128, (pr + 1) * 128), qin.r(pr * 128, (pr + 1) * 128), identb_r)
                    tr(P, pq.r(256 + pr * 128, 256 + (pr + 1) * 128), kin.r(pr * 128, (pr + 1) * 128), identb_r)
                cp(P, ACT, qT_g.r(0, 256), pq.r(0, 256))
                cp(P, DVE, kT_g.r(0, 256), pq.r(256, 512))
                if STOP[0] == 86:
                    raise _Stop()
                psc2 = [P.psum(), P.psum()]
                for h in range(4):
                    pr, hl = h // 2, h % 2
                    mm(P, psc2[hl].r(pr * 128, (pr + 1) * 128), kT_g.r(pr * 128, (pr + 1) * 128, hl * 64, (hl + 1) * 64),
                       qT_g.r(pr * 128, (pr + 1) * 128, hl * 64, (hl + 1) * 64))
                for hl in range(2):
                    tt(P, DVE, sT_g.r().m((lambda hl_: (lambda ap: ap.rearrange("p (pr hl d) -> p pr hl d", pr=2, hl=2)[:, :, hl_, :]))(hl)),
                       psc2[hl].r(0, 256).m(lambda ap: ap.rearrange("p (h d) -> p h d", h=2)),
                       tri.m(lambda ap: ap.unsqueeze(1).broadcast_to([128, 2, 128])), ALU.mult)
                if STOP[0] == 87:
                    raise _Stop()
                po = P.psum()
                for h in range(4):
                    pr, hl = h // 2, h % 2
                    mm(P, po.r(h * 128, (h + 1) * 128), sT_g.r(h * 128, (h + 1) * 128), GV(c, h * 128, (h + 1) * 128), True, False)
                    mm(P, po.r(h * 128, (h + 1) * 128), qT_g.r(pr * 128, (pr + 1) * 128, hl * 64, (hl + 1) * 64),
                       Sb_.r(pr * 128, (pr + 1) * 128, hl * 64, (hl + 1) * 64), False, True)
                if STOP[0] == 88:
                    raise _Stop()
                pds = P.psum()
                for pr in range(2):
                    mm(P, pds.r(pr * 256, (pr + 1) * 256), kst.r(pr * 128, (pr + 1) * 128), GV(c, pr * 256, (pr + 1) * 256))
                for h in range(4):
                    pr, hl = h // 2, h % 2
                    stt(P, DVE, S_.r(pr * 128, (pr + 1) * 128, hl * 64, (hl + 1) * 64),
                        S_.r(pr * 128, (pr + 1) * 128, hl * 64, (hl + 1) * 64),
                        gdec.r(pr, pr + 1, hl * 64, (hl + 1) * 64),
                        pds.r(pr * 256 + hl * 128, pr * 256 + (hl + 1) * 128, hl * 64, (hl + 1) * 64), ALU.mult, ALU.add)
                if STOP[0] == 89:
                    raise _Stop()
                cp(P, POOL, Sb_.r(), S_.r())
                group_post(po, lambda h: GG(c, h * 128, (h + 1) * 128), pcol(PT_GN), c, junk_g, og_g)
            if STOP[0] == 9:
                raise _Stop()
            P.phase = "gla_merge"
            wo_b = w_next()
            wmg = [w_next(), w_next()]
            merge_branch(2, l, lambda dc: (wo_b, dc * 128), 4, 1024, wmg)
            w_done(); w_done(); w_done()
            if after_gla is not None:
                after_gla()

            if STOP[0] == 10:
                raise _Stop()
            P.phase = "w_out"
            wo = [w_next(), w_next()]
            for dc in range(8):
                ps = proj_fm(wo[dc // 4], lambda kc: mrg.r(kc * TT, (kc + 1) * TT), 8, 512, (dc % 4) * 128)
                tt(P, DVE, xk(dc), xk(dc), ps.r(), ALU.add)
            w_done(); w_done()

            if STOP[0] == 11:
                raise _Stop()
            P.phase = "mlp"
            rmsnorm(pcol(PT_MN), hk)
            hid = lambda j: bigr.r(j * TT, (j + 1) * TT)
            for g in range(4):
                wu = [w_next(), w_next()]
                for j in range(8):
                    ps = proj_fm(wu[j // 4], lambda kc: hk(kc), 8, 512, (j % 4) * 128)
                    rt = relu_t[j % 2]
                    act(P, rt.r(), ps.r(), AF.Relu)
                    act(P, hid(j), rt.r(), AF.Square)
                w_done(); w_done()
                wd = [w_next(), w_next()]
                for dc in range(8):
                    ps = proj_fm(wd[dc // 4], hid, 8, 512, (dc % 4) * 128)
                    tt(P, DVE, xk(dc), xk(dc), ps.r(), ALU.add)
                w_done(); w_done()

        xin = mix.as_dtype(F32)

        def load_inputs(ti):
            t0 = min(ti, n_tiles - 1) * TT
            P.dma(SP, "xin", xin.r(0, 4096).m(lambda ap: ap.rearrange("p (c d) -> p c d", c=NCH)).ap,
                  x_d[t0:t0 + TT, :].rearrange("(c p) d -> p c d", p=128), writes=[xin.r(0, 4096)])
            csr = csb[ti % 2]
            P.dma(SP, "cs%d" % (ti % 2), csr.r().m(lambda ap: ap.rearrange("p (c d) -> p c d", c=NCH)).ap,
                  cs_d[ti * TT:(ti + 1) * TT, :].rearrange("(c p) d -> p c d", p=128), writes=[csr.r()])

        load_inputs(0)
        for ti in range(n_steps):
            P.phase = "boundary"
            blend = PP and ti >= 1
            if blend:
                P.dma(POOL, "gin", scr.r(0, 4096).ap, gath.ap()[0:128, :], reads=[tg.r()], writes=[scr.r(0, 4096)])
                for dc in range(8):
                    if dc % 2:
                        act(P, scr.r(dc * TT, (dc + 1) * TT), scr.r(dc * TT, (dc + 1) * TT), AF.Copy, scale=flag.r(1, 2))
                    else:
                        ts(P, DVE, scr.r(dc * TT, (dc + 1) * TT), scr.r(dc * TT, (dc + 1) * TT), flag.r(1, 2), None, ALU.mult)
            for dc in range(8):
                ps = P.psum()
                for c in range(NCH):
                    tr(P, ps.r(c * 128, (c + 1) * 128), xin.r(c * D + dc * 128, c * D + (dc + 1) * 128), ident)
                if blend:
                    stt(P, DVE, xk(dc), ps.r(), flag.r(0, 1), scr.r(dc * TT, (dc + 1) * TT), ALU.mult, ALU.add)
                else:
                    cp(P, ACT if dc % 2 else DVE, xk(dc), ps.r())
            nxt = (lambda t_: (lambda: load_inputs(t_)))(ti + 1) if ti + 1 < n_steps else None
            for l in range(n_layers):
                try:
                    layer(l, ti, nxt if l == n_layers - 1 else None)
                except _Stop:
                    pass
            P.phase = "boundary"
            if PP:
                P.dma(POOL, "ccin", cc_in.ap(), xres.r().ap, reads=[xres.r()], writes=[tin.r()])
                P.cc("ag", lambda e: e.collective_compute("AllGather", ALU.bypass, replica_groups=pp_groups,
                                                         ins=[cc_in.ap().opt()], outs=[gath.ap().opt()]),
                     reads=[tin.r()], writes=[tg.r()])
                if ti == 0:
                    for l in range(n_layers):
                        for t_ in (retS[l], retSb[l], ssdS[l], ssdSb[l], glaS[l], glaSb[l], halo[l]):
                            ts(P, DVE, t_.r(), t_.r(), flag.r(0, 1), None, ALU.mult)
                    continue
            o0 = (ti - 1) * TT if PP else ti * TT
            for c in range(NCH):
                pss = [P.psum(), P.psum()]
                for q in range(2):
                    for j in range(4):
                        dc = q * 4 + j
                        tr(P, pss[q].r(j * 128, (j + 1) * 128), xk(dc, c * 128, (c + 1) * 128), ident)
                for q in range(2):
                    act(P, ojunk.r(), pss[q].r(), AF.Square, scale=1.0 / 32.0, accum=oss.r(q, q + 1))
                tt(P, DVE, oss.r(2, 3), oss.r(0, 1), oss.r(1, 2), ALU.add)
                act(P, oss.r(3, 4), oss.r(2, 3), AF.Ln, bias=EPS)
                act(P, oss.r(4, 5), oss.r(3, 4), AF.Exp, scale=-0.5)
                stg = ostage[c % 2]
                for q in range(2):
                    stt(P, DVE, stg.r(q * 512, (q + 1) * 512), pss[q].r(), oss.r(4, 5), fnw_bc.r(q * 512, (q + 1) * 512),
                        ALU.mult, ALU.mult)
                P.dma(SP, "xout%d" % (c % 2), out_d[o0 + c * 128:o0 + (c + 1) * 128, :], stg.r().ap, reads=[stg.r()], final=True)
        P.emit()
    return nc


_CACHE = {}


def _get_prog(n_tiles, n_layers, groups):
    key = (n_tiles, n_layers, str(groups))
    if key not in _CACHE:
        _CACHE[key] = build_program(n_tiles, n_layers, groups)
    return _CACHE[key]


def _layer_maps(inp, layers):
    ws, wsms, pts, bts, gws = [], [], [], [], []
    for l in layers:
        wst, wsm, pt, bt, gw = host_layer_weights(inp, l)
        ws.append(wst); wsms.append(wsm); pts.append(pt); bts.append(bt); gws.append(gw)
    return {
        "wst": np.ascontiguousarray(np.concatenate(ws, 0)),
        "wsm": np.stack(wsms, 0),
        "ptab": np.stack(pts, 0),
        "btab": np.stack(bts, 0),
        "gw": np.stack(gws, 0),
        "fnw": np.ascontiguousarray(np.asarray(inp["final_norm_w"], np.float32).reshape(8, 128).T),
        "fnwb": np.ascontiguousarray(np.asarray(inp["final_norm_w"], np.float32)),
        "cst": host_consts(),
        "cstr": np.full((128, 128), 1.0 / 1024.0, np.float32),
    }


def make_in_maps(inp, seqs, n_layers):
    S = seqs[0].shape[0]
    common = _layer_maps(inp, range(n_layers))
    common["cs"] = host_cs(S)
    return [dict(common, x=np.ascontiguousarray(s, dtype=np.float32)) for s in seqs]


def make_in_maps_pp(inp, seqs):
    n = len(seqs)
    S = seqs[0].shape[0]
    cs = host_cs(S)
    pad = np.zeros((TT, 128), np.float32)
    la = _layer_maps(inp, [0])
    lb = _layer_maps(inp, [1])
    fa = np.zeros((128, 2), np.float32); fa[:, 0] = 1.0
    fb = np.zeros((128, 2), np.float32); fb[:, 1] = 1.0
    zeros = np.zeros((S, D), np.float32)
    maps = []
    for i in range(n):
        maps.append(dict(la, x=np.ascontiguousarray(seqs[i], dtype=np.float32), cs=np.concatenate([cs, pad], 0), flag=fa))
    for i in range(n):
        maps.append(dict(lb, x=zeros, cs=np.concatenate([pad, cs], 0), flag=fb))
    return maps


def kernel(**inputs):
    inp = {k: np.asarray(v) for k, v in inputs.items()}
    x = inp["x"].astype(np.float32, copy=False)
    B, S, _ = x.shape
    groups = [[b, b + B] for b in range(B)]
    nc = _get_prog(S // TT, 1, groups)
    in_maps = make_in_maps_pp(inp, [x[b] for b in range(B)])
    res = run_bass_kernel_spmd(nc, in_maps, core_ids=list(range(2 * B)))
    out = np.stack([res.results[B + b]["out"] for b in range(B)], 0)
    return out.astype(np.float32, copy=False)
```

```python
import math
from contextlib import ExitStack

import numpy as np
import concourse.bass as bass
import concourse.mybir as mybir
from concourse.bass_utils import run_bass_kernel_spmd

F32 = mybir.dt.float32
BF16 = mybir.dt.bfloat16
F32R = mybir.dt.float32r
AF = mybir.ActivationFunctionType
ALU = mybir.AluOpType
GRAN = 512
ESZ = {F32: 4, BF16: 2, F32R: 4}
PE, ACT, DVE, POOL, SP = "tensor", "scalar", "vector", "gpsimd", "sync"
ENGS = [PE, ACT, DVE, POOL, SP]

D = 1024
TT = 512
NCH = 4
EPS = 1e-6
NBLK = 40
N_CORES = 8


class Reg:
    __slots__ = ("t", "c0", "c1", "ap")

    def __init__(self, t, c0, c1, ap):
        self.t, self.c0, self.c1, self.ap = t, c0, c1, ap

    def grans(self):
        e = self.t.esz
        tid = self.t.tid
        return [(tid, g) for g in range((self.c0 * e) // GRAN, (self.c1 * e - 1) // GRAN + 1)]

    def m(self, fn):
        return Reg(self.t, self.c0, self.c1, fn(self.ap))


class T:
    _n = 0

    def __init__(self, handle, cols, dtype, parts=128, psum=False):
        self.h, self.cols, self.dtype, self.esz, self.parts = handle, cols, dtype, ESZ[dtype], parts
        T._n += 1
        self.tid = T._n
        self.psum = psum

    def r(self, c0=0, c1=None, p0=0, p1=None):
        c1 = self.cols if c1 is None else c1
        p1 = self.parts if p1 is None else p1
        return Reg(self, c0, c1, self.h[p0:p1, c0:c1])

    def as_dtype(self, dtype):
        t = T.__new__(T)
        t.h = self.h.bitcast(dtype)
        t.dtype, t.esz, t.parts, t.tid = dtype, ESZ[dtype], self.parts, self.tid
        t.psum = self.psum
        t.cols = self.cols * self.esz // t.esz
        return t


class Sub:
    def __init__(self, parent, off, cols, parts=128):
        self.p, self.off, self.cols, self.parts = parent, off, cols, parts

    def r(self, c0=0, c1=None, p0=0, p1=None):
        c1 = self.cols if c1 is None else c1
        p1 = self.parts if p1 is None else p1
        return self.p.r(self.off + c0, self.off + c1, p0, p1)


class DTrack:
    def __init__(self):
        T._n += 1
        self.tid, self.esz, self.psum = T._n, 4, False

    def r(self):
        return Reg(self, 0, 1, None)


class Prog:
    def __init__(self, nc):
        self.nc = nc
        self.ops = {e: [] for e in ENGS}
        self.gr = {}
        self.waited = {e: {} for e in ENGS}
        self.needed = {e: set() for e in ENGS}
        self.dma_cnt = {}
        self.cc_cnt = {}
        self.final = []
        self.banks = []
        self.bank_i = 0
        self.bank_last = {}
        self.phase = None
        self.annotate = False

    def psum(self):
        b = self.banks[self.bank_i % len(self.banks)]
        self.bank_i += 1
        return b

    def _deps(self, eng, reads, writes):
        deps = {}

        def add(ev):
            if ev is None:
                return
            k, v = ev
            if k == PE and eng == PE:
                return
            if deps.get(k, -1) < v:
                deps[k] = v
        gr = self.gr
        for r in reads:
            for g in r.grans():
                st = gr.get(g)
                if st is not None:
                    add(st[0])
        for w in writes:
            for g in w.grans():
                st = gr.get(g)
                if st is not None:
                    add(st[0])
                    for ev in st[1]:
                        add(ev)
        for r in list(reads) + list(writes):
            if r.t.psum:
                bl = self.bank_last.get(r.t.tid)
                if bl:
                    for e2, ev in bl.items():
                        if e2 != eng:
                            add(ev)
        out = []
        wd = self.waited[eng]
        for k, v in deps.items():
            if wd.get(k, -1) >= v:
                continue
            wd[k] = v
            out.append((k, v))
            if k in self.needed:
                self.needed[k].add(v)
        return out

    def _mark(self, ev, reads, writes):
        gr = self.gr
        for r in list(reads) + list(writes):
            if r.t.psum:
                self.bank_last.setdefault(r.t.tid, {})[ev[0]] = ev
        for r in reads:
            for g in r.grans():
                st = gr.get(g)
                if st is None:
                    gr[g] = [None, [ev]]
                else:
                    st[1].append(ev)
        for w in writes:
            for g in w.grans():
                gr[g] = [ev, []]

    def op(self, eng, fn, reads=(), writes=()):
        waits = self._deps(eng, reads, writes)
        seq = len(self.ops[eng])
        self.ops[eng].append([fn, waits, None, self.phase])
        self._mark((eng, seq), reads, writes)

    def dma(self, q, semkey, out_ap, in_ap, reads=(), writes=(), final=False):
        waits = self._deps(q, reads, writes)
        n = self.dma_cnt.get(semkey, 0) + 1
        self.dma_cnt[semkey] = n
        ev = (("dma", semkey), n * 16)
        self.ops[q].append([lambda e: e.dma_start(out=out_ap, in_=in_ap), waits, ev, self.phase])
        self._mark(ev, reads, writes)
        if final:
            self.final.append(ev)

    def cc(self, semkey, fn, reads=(), writes=()):
        waits = self._deps(POOL, reads, writes)
        n = self.cc_cnt.get(semkey, 0) + 1
        self.cc_cnt[semkey] = n
        ev = (("cc", semkey), n)
        self.ops[POOL].append([fn, waits, ev, self.phase])
        self._mark(ev, reads, writes)

    def emit(self):
        nc = self.nc
        with ExitStack() as es:
            sems = {}
            for e in ENGS:
                sems[e] = es.enter_context(nc.semaphore("s_" + e))
            for k in self.dma_cnt:
                sems[("dma", k)] = es.enter_context(nc.semaphore("d_" + str(k)))
            for k in self.cc_cnt:
                sems[("cc", k)] = es.enter_context(nc.semaphore("c_" + str(k)))
            block = es.enter_context(nc.Block())
            val = {}
            for e in ENGS:
                c = 0
                nd = self.needed[e]
                for i in range(len(self.ops[e])):
                    if i in nd:
                        c += 1
                        val[(e, i)] = c

            def run(ename, eobj):
                nd = self.needed[ename]
                for i, (fn, waits, dmaev, phase) in enumerate(self.ops[ename]):
                    for k, v in waits:
                        if isinstance(k, tuple):
                            eobj.wait_ge(sems[k], v)
                        else:
                            eobj.wait_ge(sems[k], val[(k, v)])
                    ins = fn(eobj)
                    if self.annotate and phase:
                        ins.annotate(phase)
                    if dmaev is not None and dmaev[0][0] == "cc":
                        ins.then_inc(sems[dmaev[0]])
                    elif dmaev is not None:
                        ins.then_inc(sems[dmaev[0]], 16)
                    elif i in nd:
                        ins.then_inc(sems[ename], 1)
                if ename == SP:
                    fin = {}
                    for k, v in self.final:
                        fin[k] = max(fin.get(k, 0), v)
                    for k, v in fin.items():
                        eobj.wait_ge(sems[k], v)

            @block.tensor
            def _(e):
                run(PE, e)

            @block.scalar
            def _(e):
                run(ACT, e)

            @block.vector
            def _(e):
                run(DVE, e)

            @block.gpsimd
            def _(e):
                run(POOL, e)

            @block.sync
            def _(e):
                run(SP, e)


def _regs(*xs):
    return [x for x in xs if isinstance(x, Reg)]


def _a(x):
    return x.ap if isinstance(x, Reg) else x


def mm(P, out, lhsT, rhs, start=True, stop=True):
    P.op(PE, lambda e: e.matmul(out.ap, lhsT.ap, rhs.ap, start=start, stop=stop), reads=[lhsT, rhs], writes=[out])


def tr(P, out, in_, ident):
    P.op(PE, lambda e: e.transpose(out.ap, in_.ap, ident.ap), reads=[in_, ident], writes=[out])


def act(P, out, in_, func, bias=None, scale=None, accum=None, eng=ACT):
    kw = {}
    if bias is not None:
        kw["bias"] = _a(bias)
    if scale is not None:
        kw["scale"] = _a(scale)
    if accum is not None:
        kw["accum_out"] = accum.ap
    P.op(eng, lambda e: e.activation(out.ap, in_.ap, func, **kw),
         reads=_regs(in_, bias, scale), writes=_regs(out, accum))


def tt(P, eng, out, in0, in1, op):
    P.op(eng, lambda e: e.tensor_tensor(out.ap, in0.ap, in1.ap, op), reads=[in0, in1], writes=[out])


def ts(P, eng, out, in0, s1, s2, op0, op1=None):
    if op1 is None:
        P.op(eng, lambda e: e.tensor_scalar(out.ap, in0.ap, _a(s1), None, op0), reads=_regs(in0, s1), writes=[out])
    else:
        P.op(eng, lambda e: e.tensor_scalar(out.ap, in0.ap, _a(s1), _a(s2), op0, op1),
             reads=_regs(in0, s1, s2), writes=[out])


def stt(P, eng, out, in0, scalar, in1, op0, op1):
    P.op(eng, lambda e: e.scalar_tensor_tensor(out.ap, in0.ap, _a(scalar), in1.ap, op0, op1),
         reads=_regs(in0, scalar, in1), writes=[out])


def cp(P, eng, out, in_):
    if eng == ACT:
        act(P, out, in_, AF.Copy)
    else:
        P.op(eng, lambda e: e.tensor_copy(out.ap, in_.ap), reads=[in_], writes=[out])


C_IDENT, C_TRI, C_NEG, C_QDEC, C_KDEC, C_GC, C_ONES, C_TRIR, NCST = 0, 128, 256, 384, 388, 392, 396, 524, 652
PT_AN, PT_MN, PT_RN, PT_SN, PT_GN, PT_CW, PT_CB, PT_MB, NPT = 0, 8, 16, 20, 28, 32, 80, 92, 116
BT_DTB, BT_ALOG, BT_D, BT_GB, NBT = 0, 16, 32, 48, 304

O_RQ, O_RK, O_RV, O_RG, O_SZ, O_SXBC, O_SDT, O_GQ, O_GK, O_GV, O_GR, O_GLR, O_MG = (
    0, 512, 1024, 1536, 2048, 3072, 4608, 4624, 4880, 5136, 5648, 6160, 6176)


def host_consts():
    c = np.zeros((128, NCST), np.float32)
    idx = np.arange(128)
    c[:, C_IDENT:C_IDENT + 128] = np.eye(128, dtype=np.float32)
    tri = (idx[:, None] <= idx[None, :]).astype(np.float32)
    c[:, C_TRI:C_TRI + 128] = tri
    c[:, C_NEG:C_NEG + 128] = (tri - 1.0) * 30000.0
    h = np.arange(4, dtype=np.float64)
    lg = np.log1p(-np.exp2(-5.0 - h))
    pos = idx.astype(np.float64)
    c[:, C_QDEC:C_QDEC + 4] = np.exp(lg[None, :] * (pos[:, None] + 1.0))
    c[:, C_KDEC:C_KDEC + 4] = np.exp(-lg[None, :] * (pos[:, None] + 1.0)) * (128.0 ** -0.5)
    c[:, C_GC:C_GC + 4] = np.exp(lg * 128.0)[None, :]
    c[:, C_ONES:C_ONES + 128] = 1.0
    c[:, C_TRIR:C_TRIR + 128] = (idx[:, None] > idx[None, :]).astype(np.float32)
    return c


def host_cs(seq):
    inv_freq = (10000.0 ** (-np.arange(0, 128, 2, dtype=np.float32) / np.float32(128))).astype(np.float32)
    ang = np.arange(seq, dtype=np.float32)[:, None] * inv_freq[None, :]
    return np.concatenate([np.cos(ang), np.sin(ang)], axis=1).astype(np.float32)


def _blk(w):
    K, N = w.shape
    kc = K // 128
    return np.ascontiguousarray(w.reshape(kc, 128, N).transpose(1, 0, 2).reshape(128, kc * N))


def host_layer_weights(inp, l):
    w_in = inp["w_in"][l]
    blocks = []
    add = lambda w: blocks.append(_blk(w))
    add(w_in[:, O_RQ:O_RQ + 512]); add(w_in[:, O_RK:O_RK + 512]); add(w_in[:, O_RV:O_RV + 512]); add(w_in[:, O_RG:O_RG + 512])
    add(inp["ret_w_o"][l])
    add(w_in[:, O_MG:O_MG + 512]); add(w_in[:, O_MG + 512:O_MG + 1024])
    add(w_in[:, O_SZ:O_SZ + 512]); add(w_in[:, O_SZ + 512:O_SZ + 1024])
    for j in range(3):
        add(w_in[:, O_SXBC + 512 * j:O_SXBC + 512 * (j + 1)])
    add(inp["ssd_w_o"][l][:, 0:512]); add(w_in[:, O_MG + 1024:O_MG + 1536])
    add(inp["ssd_w_o"][l][:, 512:1024]); add(w_in[:, O_MG + 1536:O_MG + 2048])
    add(w_in[:, O_GQ:O_GQ + 512]); add(w_in[:, O_GV:O_GV + 512]); add(w_in[:, O_GR:O_GR + 512])
    add(inp["gla_w_o"][l])
    add(w_in[:, O_MG + 2048:O_MG + 2560]); add(w_in[:, O_MG + 2560:O_MG + 3072])
    add(inp["w_out"][l][:, 0:512]); add(inp["w_out"][l][:, 512:1024])
    for g in range(4):
        add(inp["w_up"][l][:, g * 1024:g * 1024 + 512]); add(inp["w_up"][l][:, g * 1024 + 512:(g + 1) * 1024])
        add(inp["w_down"][l][g * 1024:(g + 1) * 1024, 0:512]); add(inp["w_down"][l][g * 1024:(g + 1) * 1024, 512:1024])
    assert len(blocks) == NBLK
    wst = np.stack(blocks, 0)
    sm = np.concatenate([w_in[:, O_SDT:O_SDT + 16], w_in[:, O_GLR:O_GLR + 16]], axis=1)
    wsm = _blk(sm)
    pt = np.zeros((128, NPT), np.float32)
    colmaj = lambda v: np.ascontiguousarray(v.reshape(-1, 128).T)
    pt[:, PT_AN:PT_AN + 8] = colmaj(inp["attn_norm_w"][l])
    pt[:, PT_MN:PT_MN + 8] = colmaj(inp["mlp_norm_w"][l])
    pt[:, PT_RN:PT_RN + 4] = colmaj(inp["ret_norm_w"][l])
    pt[:, PT_SN:PT_SN + 8] = colmaj(inp["ssd_norm_w"][l])
    pt[:, PT_GN:PT_GN + 4] = colmaj(inp["gla_norm_w"][l])
    cw = inp["ssd_conv_w"][l]
    pt[:, PT_CW:PT_CW + 48] = cw.T.reshape(12, 128, 4).transpose(1, 0, 2).reshape(128, 48)
    pt[:, PT_CB:PT_CB + 12] = colmaj(inp["ssd_conv_b"][l])
    pt[:, PT_MB:PT_MB + 24] = colmaj(inp["merge_gate_b"][l])
    bt = np.concatenate([inp["ssd_dt_bias"][l], inp["ssd_a_log"][l], inp["ssd_d"][l], inp["gla_gate_b"][l]]).astype(np.float32)
    return wst, wsm, pt, bt, np.ascontiguousarray(inp["gla_gate_w"][l])


STOP = [None]
ANNOTATE = [False]


class _Stop(Exception):
    pass


def build_program(n_tiles, n_layers, pp_groups=None):
    PP = pp_groups is not None
    n_steps = n_tiles + 1 if PP else n_tiles
    S = n_tiles * TT
    nc = bass.Bass("TRN2", target_bir_lowering=False)
    nc.dge_precook = False
    T._n = 0
    dram = lambda name, shape, dt, kind="ExternalInput": nc.dram_tensor(name, shape, dt, kind=kind).ap()
    x_d = dram("x", [S, D], F32)
    cs_d = dram("cs", [n_steps * TT, 128], F32)
    if PP:
        flag_d = dram("flag", [128, 2], F32)
        cc_in = nc.dram_tensor("cc_in", [128, 8 * TT], F32)
        gath = nc.dram_tensor("gath", [256, 8 * TT], F32)
    wst_d = dram("wst", [n_layers * NBLK, 128, 4096], F32R)
    wsm_d = dram("wsm", [n_layers, 128, 256], F32R)
    ptab_d = dram("ptab", [n_layers, 128, NPT], F32)
    btab_d = dram("btab", [n_layers, NBT], F32)
    gw_d = dram("gw", [n_layers, 16, 256], F32)
    fnw_d = dram("fnw", [128, 8], F32)
    fnwb_d = dram("fnwb", [D], F32)
    cst_d = dram("cst", [128, NCST], F32)
    cstr_d = dram("cstr", [128, 128], F32R)
    out_d = dram("out", [S, D], F32, kind="ExternalOutput")
    dbg_d = {}

    with ExitStack() as es:
        P = Prog(nc)
        P.annotate = ANNOTATE[0]

        def sb(name, cols, dt=F32, parts=128):
            return T(es.enter_context(nc.sbuf_tensor("sb_" + name, [parts, cols], dt)), cols, dt, parts)

        for i in range(8):
            P.banks.append(T(es.enter_context(nc.psum_tensor("psb%d" % i, [128, 512], F32)), 512, F32, psum=True))

        xres = sb("xres", 8 * TT)
        hbuf = sb("hbuf", 8 * TT, F32R)
        mrg = sb("mrg", 8 * TT, F32R)
        big = sb("big", 4096)
        bigr = big.as_dtype(F32R)
        NWB = 4 if PP else 3
        wbuf = [sb("wbuf%d" % i, 4096, F32R) for i in range(NWB)]
        mix = sb("mix", 10240, BF16)
        cst = sb("cst", NCST)
        cstr = sb("cstr", 128, F32R)
        identb = sb("identb", 128, BF16)
        csb = [sb("csb%d" % i, 512) for i in range(2)]
        fnw = sb("fnw", 8)
        flag = sb("flag", 2)
        fnw_bc = sb("fnw_bc", D)
        oss = sb("oss", 8)
        ptab = [sb("ptab%d" % l, NPT) for l in range(n_layers)]
        btab = [sb("btab%d" % l, NBT) for l in range(n_layers)]
        negA = [sb("negA%d" % l, 16) for l in range(n_layers)]
        gwt = [sb("gwt%d" % l, 256, F32, 16) for l in range(n_layers)]
        wsm = [sb("wsm%d" % l, 256, F32R) for l in range(n_layers)]
        retS = [sb("retS%d" % l, 512) for l in range(n_layers)]
        retSb = [sb("retSb%d" % l, 512, BF16) for l in range(n_layers)]
        ssdS = [sb("ssdS%d" % l, 1024) for l in range(n_layers)]
        ssdSb = [sb("ssdSb%d" % l, 1024, BF16) for l in range(n_layers)]
        glaS = [sb("glaS%d" % l, 256) for l in range(n_layers)]
        glaSb = [sb("glaSb%d" % l, 256, BF16) for l in range(n_layers)]
        halo = [sb("halo%d" % l, 48) for l in range(n_layers)]
        scr = sb("scr", 7168)
        scr_r = scr.as_dtype(F32R)
        scr_b = scr.as_dtype(BF16)
        K32 = lambda kb, cols, parts=128: Sub(scr, int(kb * 256), cols, parts)
        K32R = lambda kb, cols: Sub(scr_r, int(kb * 256), cols)
        KBF = lambda kb, cols: Sub(scr_b, int(kb * 512), cols)
        sq = [sb("sq0", TT, F32R), sb("sq1", TT, F32R)]
        nrm_a = K32(4, TT)
        nrm_r = K32(6, TT)
        gate_sb = [K32(8, TT), K32(10, TT)]
        gtmp = [K32(12, TT), K32(14, TT)]
        relu_t = [K32(16, TT), K32(18, TT)]
        rtmp = [K32(i, 256) for i in range(4)]
        rot = K32(4, 512)
        qT_r, kT_r, sT_r = KBF(6, 512), KBF(7, 512), KBF(8, 512)
        junk_r, og_r, stmp = K32(10, 512), K32(12, 512), K32(14, 512)
        stage = [K32(16, TT + 4), K32(18.5, TT + 4)]
        cacc = [K32(21, TT), K32(23, TT)]
        A1 = [K32(0, 512), K32(2, 512)]
        ET = [K32(4, 512), K32(6, 512)]
        GT = KBF(8, 2048)
        cb_sb = K32(12, 256)
        Xbf = KBF(13, 1024)
        XW = KBF(15, 1024)
        Bbf = KBF(17, 256)
        y1 = K32(18, 1024)
        y2 = K32(22, 1024)
        junk_s = K32(26, 512)
        ostage = [K32(16, 1024), K32(20, 1024)]
        ojunk = K32(24, 512)
        glr_sb = K32(0, TT, 16)
        lp, gtm, eb, enb, erev = K32(2, 256), K32(3, 256), K32(4, 256), K32(5, 256), K32(6, 256)
        qin, kin, kst = KBF(7, 256), KBF(7.5, 256), KBF(8, 256)
        qT_g, kT_g, sT_g = KBF(9, 512), KBF(10, 512), KBF(11, 512)
        junk_g, og_g = K32(12, 512), K32(14, 512)
        ss4 = sb("ss4", 8)
        rs4 = sb("rs4", 8)
        dt_all = sb("dt_all", 64)
        dtt = sb("dtt", 64)
        a_all = sb("a_all", 64)
        acs_all = sb("acs_all", 64)
        eacs_all = sb("eacs_all", 64)
        wdec = sb("wdec", 16)
        sdec = sb("sdec", 16)
        gdec = sb("gdec", 2)

        def C(c0, n, p0=0, p1=128):
            return cst.r(c0, c0 + n, p0, p1)

        ident = C(C_IDENT, 128)
        tri = C(C_TRI, 128)
        identb_r = identb.r()

        P.dma(SP, "cst", cst.r().ap, cst_d, writes=[cst.r()])
        P.dma(SP, "cstr", cstr.r().ap, cstr_d, writes=[cstr.r()])
        P.dma(SP, "fnw", fnw.r().ap, fnw_d, writes=[fnw.r()])
        P.dma(POOL, "fnwb", fnw_bc.r().ap, fnwb_d.partition_broadcast(128), writes=[fnw_bc.r()])
        if PP:
            P.dma(SP, "flag", flag.r().ap, flag_d, writes=[flag.r()])
            tin, tg = DTrack(), DTrack()
        for l in range(n_layers):
            P.dma(SP, "ptab", ptab[l].r().ap, ptab_d[l], writes=[ptab[l].r()])
            P.dma(POOL, "btab", btab[l].r().ap, btab_d[l].partition_broadcast(128), writes=[btab[l].r()])
            P.dma(SP, "gwt", gwt[l].r().ap, gw_d[l], writes=[gwt[l].r()])
            P.dma(SP, "wsm", wsm[l].r().ap, wsm_d[l], writes=[wsm[l].r()])
        cp(P, DVE, identb_r, ident)
        for l in range(n_layers):
            act(P, negA[l].r(), btab[l].r(BT_ALOG, BT_ALOG + 16), AF.Exp)
            ts(P, DVE, negA[l].r(), negA[l].r(), -1.0, None, ALU.mult)
            for t_ in (retS[l], ssdS[l], glaS[l], halo[l]):
                P.op(POOL, (lambda tt_: (lambda e: e.memset(tt_.r().ap, 0.0)))(t_), writes=[t_.r()])
            for t_ in (retSb[l], ssdSb[l], glaSb[l]):
                P.op(POOL, (lambda tt_: (lambda e: e.memset(tt_.r().ap, 0.0)))(t_), writes=[t_.r()])

        sched = []
        for ti in range(n_steps):
            for l in range(n_layers):
                for b in range(NBLK):
                    sched.append(l * NBLK + b)
        wstate = {"issued": 0, "used": 0}

        def w_issue():
            i = wstate["issued"]
            if i >= len(sched):
                return
            buf = wbuf[i % NWB]
            P.dma(SP, "w%d" % (i % NWB), buf.r().ap, wst_d[sched[i]], writes=[buf.r()])
            wstate["issued"] = i + 1

        def w_next():
            i = wstate["used"]
            wstate["used"] = i + 1
            return wbuf[i % NWB]

        def w_done():
            w_issue()

        for _ in range(NWB):
            w_issue()

        def xk(kc, c0=0, c1=TT):
            return xres.r(kc * TT + c0, kc * TT + c1)

        def hk(kc, c0=0, c1=TT):
            return hbuf.r(kc * TT + c0, kc * TT + c1)

        def rmsnorm(wcol, out_fn):
            ps = P.psum()
            for kc in range(8):
                s = sq[kc % 2]
                act(P, s.r(), xk(kc), AF.Square)
                mm(P, ps.r(), cstr.r(), s.r(), start=(kc == 0), stop=(kc == 7))
            act(P, nrm_a.r(), ps.r(), AF.Ln, bias=EPS)
            act(P, nrm_r.r(), nrm_a.r(), AF.Exp, scale=-0.5)
            for kc in range(8):
                stt(P, DVE, out_fn(kc), xk(kc), wcol(kc), nrm_r.r(), ALU.mult, ALU.mult)

        def proj_tm(wb, ncols, evac, kstride=512, col0=0):
            for c in range(NCH):
                ps = P.psum()
                for kc in range(8):
                    mm(P, ps.r(0, ncols), hk(kc, c * 128, (c + 1) * 128),
                       wb.r(kc * kstride + col0, kc * kstride + col0 + ncols), start=(kc == 0), stop=(kc == 7))
                evac(c, ps)

        def proj_fm(wb, rhs_fn, nk, kstride, col0, m=128):
            ps = P.psum()
            for kc in range(nk):
                mm(P, ps.r(0, TT, 0, m), wb.r(kc * kstride + col0, kc * kstride + col0 + m), rhs_fn(kc),
                   start=(kc == 0), stop=(kc == nk - 1))
            return ps

        def outT(kc, c0=0, c1=TT):
            return bigr.r(kc * TT + c0, kc * TT + c1)

        def group_post(ps_o, G_fn, nw_col, c, junk, og):
            for h in range(4):
                act(P, junk.r(h * 128, (h + 1) * 128), ps_o.r(h * 128, (h + 1) * 128), AF.Square,
                    scale=128.0 ** -0.5, accum=ss4.r(h, h + 1))
            act(P, rs4.r(0, 4), ss4.r(0, 4), AF.Ln, bias=EPS)
            act(P, rs4.r(4, 8), rs4.r(0, 4), AF.Exp, scale=-0.5)
            for h in range(4):
                stt(P, DVE, og.r(h * 128, (h + 1) * 128), ps_o.r(h * 128, (h + 1) * 128), rs4.r(4 + h, 5 + h),
                    G_fn(h), ALU.mult, ALU.mult)
            pt_ = P.psum()
            for h in range(4):
                tr(P, pt_.r(h * 128, (h + 1) * 128), og.r(h * 128, (h + 1) * 128), ident)
            for h in range(4):
                act(P, outT(h, c * 128, (c + 1) * 128), pt_.r(h * 128, (h + 1) * 128), AF.Copy, scale=nw_col(h))

        def merge_branch(br, l, wo, nk, kstride_o, wmg_blocks, dcs=range(8)):
            for dc in dcs:
                wo_b, col_o = wo(dc)
                ps_o = proj_fm(wo_b, lambda kc: outT(kc), nk, kstride_o, col_o)
                ps_g = proj_fm(wmg_blocks[dc // 4], lambda kc: hk(kc), 8, 512, (dc % 4) * 128)
                g = gate_sb[dc % 2]
                act(P, g.r(), ps_g.r(), AF.Sigmoid, bias=ptab[l].r(PT_MB + br * 8 + dc, PT_MB + br * 8 + dc + 1))
                dst = mrg.r(dc * TT, (dc + 1) * TT)
                if br == 0:
                    tt(P, DVE, dst, ps_o.r(), g.r(), ALU.mult)
                else:
                    tmp = gtmp[dc % 2]
                    tt(P, DVE, tmp.r(), ps_o.r(), g.r(), ALU.mult)
                    tt(P, POOL, dst, dst, tmp.r(), ALU.add)

        def b3(ap, a, b):
            return ap.unsqueeze(2).broadcast_to([ap.shape[0], a, b])

        def layer(l, ti, after_gla=None):
            pt = ptab[l]
            pcol = lambda base: (lambda k: pt.r(base + k, base + k + 1))
            csr = csb[ti % 2]
            P.phase = "norm1"
            rmsnorm(pcol(PT_AN), hk)

            if STOP[0] == 1:
                raise _Stop()
            P.phase = "ret_proj"
            Qt = lambda c, c0=0, c1=512: mix.r(c * 512 + c0, c * 512 + c1)
            Kt = lambda c, c0=0, c1=512: mix.r(2048 + c * 512 + c0, 2048 + c * 512 + c1)
            Vt = lambda c, c0=0, c1=512: mix.r(4096 + c * 512 + c0, 4096 + c * 512 + c1)
            Gt = lambda c, c0=0, c1=512: mix.r(6144 + c * 512 + c0, 6144 + c * 512 + c1)

            def rotary_evac(dst_fn, dec_c0):
                def f(c, ps):
                    v4 = lambda r_: r_.m(lambda ap: ap.rearrange("p (h t d) -> p h t d", h=4, t=2))
                    x1 = v4(ps.r()).m(lambda ap: ap[:, :, 0, :])
                    x2 = v4(ps.r()).m(lambda ap: ap[:, :, 1, :])
                    cos = csr.r(c * 128, c * 128 + 64).m(lambda ap: ap.unsqueeze(1).broadcast_to([128, 4, 64]))
                    sin = csr.r(c * 128 + 64, c * 128 + 128).m(lambda ap: ap.unsqueeze(1).broadcast_to([128, 4, 64]))
                    v3 = lambda r_: r_.m(lambda ap: ap.rearrange("p (h d) -> p h d", h=4))
                    t1, t2, t3, t4 = [v3(rtmp[i].r()) for i in range(4)]
                    tt(P, DVE, t1, x1, cos, ALU.mult)
                    tt(P, DVE, t2, x2, sin, ALU.mult)
                    tt(P, DVE, t3, x1, sin, ALU.mult)
                    tt(P, DVE, t4, x2, cos, ALU.mult)
                    r1 = v4(rot.r()).m(lambda ap: ap[:, :, 0, :])
                    r2 = v4(rot.r()).m(lambda ap: ap[:, :, 1, :])
                    tt(P, POOL, r1, t1, t2, ALU.subtract)
                    tt(P, POOL, r2, t3, t4, ALU.add)
                    dec = C(dec_c0, 4).m(lambda ap: b3(ap, 4, 128))
                    tt(P, POOL, dst_fn(c).m(lambda ap: ap.rearrange("p (h d) -> p h d", h=4)),
                       rot.r().m(lambda ap: ap.rearrange("p (h d) -> p h d", h=4)), dec, ALU.mult)
                return f

            wb = w_next(); proj_tm(wb, 512, rotary_evac(Qt, C_QDEC)); w_done()
            wb = w_next(); proj_tm(wb, 512, rotary_evac(Kt, C_KDEC)); w_done()
            wb = w_next(); proj_tm(wb, 512, lambda c, ps: cp(P, ACT, Vt(c), ps.r())); w_done()
            wb = w_next(); proj_tm(wb, 512, lambda c, ps: act(P, Gt(c), ps.r(), AF.Silu)); w_done()
            if STOP[0] == 2:
                raise _Stop()
            P.phase = "ret_chunks"
            S_, Sb_ = retS[l], retSb[l]
            for c in range(NCH):
                pq = P.psum().as_dtype(BF16)
                for h in range(4):
                    tr(P, pq.r(h * 128, (h + 1) * 128), Qt(c, h * 128, (h + 1) * 128), identb_r)
                cp(P, ACT, qT_r.r(), pq.r(0, 512))
                pk = P.psum().as_dtype(BF16)
                for h in range(4):
                    tr(P, pk.r(h * 128, (h + 1) * 128), Kt(c, h * 128, (h + 1) * 128), identb_r)
                cp(P, DVE, kT_r.r(), pk.r(0, 512))
                psc = P.psum()
                for h in range(4):
                    mm(P, psc.r(h * 128, (h + 1) * 128), kT_r.r(h * 128, (h + 1) * 128), qT_r.r(h * 128, (h + 1) * 128))
                tt(P, DVE, sT_r.r().m(lambda ap: ap.rearrange("p (h d) -> p h d", h=4)),
                   psc.r().m(lambda ap: ap.rearrange("p (h d) -> p h d", h=4)),
                   tri.m(lambda ap: ap.unsqueeze(1).broadcast_to([128, 4, 128])), ALU.mult)
                po = P.psum()
                for h in range(4):
                    hs = slice(h * 128, (h + 1) * 128)
                    mm(P, po.r(h * 128, (h + 1) * 128), sT_r.r(h * 128, (h + 1) * 128), Vt(c, h * 128, (h + 1) * 128), True, False)
                    mm(P, po.r(h * 128, (h + 1) * 128), qT_r.r(h * 128, (h + 1) * 128), Sb_.r(h * 128, (h + 1) * 128), False, True)
                pds = P.psum()
                for h in range(4):
                    mm(P, pds.r(h * 128, (h + 1) * 128), Kt(c, h * 128, (h + 1) * 128), Vt(c, h * 128, (h + 1) * 128))
                tt(P, DVE, stmp.r(0, 512), pds.r(), S_.r(), ALU.add)
                tt(P, POOL, S_.r().m(lambda ap: ap.rearrange("p (h d) -> p h d", h=4)),
                   stmp.r(0, 512).m(lambda ap: ap.rearrange("p (h d) -> p h d", h=4)),
                   C(C_GC, 4).m(lambda ap: b3(ap, 4, 128)), ALU.mult)
                cp(P, POOL, Sb_.r(), S_.r())
                group_post(po, lambda h: Gt(c, h * 128, (h + 1) * 128), pcol(PT_RN), c, junk_r, og_r)
            if STOP[0] == 3:
                raise _Stop()
            P.phase = "ret_merge"
            wo_b = w_next()
            wmg = [w_next(), w_next()]
            merge_branch(0, l, lambda dc: (wo_b, dc * 128), 4, 1024, wmg)
            w_done(); w_done(); w_done()

            if STOP[0] == 4:
                raise _Stop()
            P.phase = "ssd_proj_conv"
            Zt = lambda c, c0=0, c1=1024: mix.r(c * 1024 + c0, c * 1024 + c1)
            XBC = lambda cc, c0=0, c1=TT: mix.r(4096 + cc * TT + c0, 4096 + cc * TT + c1)
            for half in range(2):
                wb = w_next()
                proj_tm(wb, 512, (lambda hf: (lambda c, ps: act(P, Zt(c, hf * 512, hf * 512 + 512), ps.r(), AF.Silu)))(half))
                w_done()
            conv_w = {}

            def convA(cc):
                j, q = cc // 4, cc % 4
                if q == 0:
                    conv_w[j] = w_next()
                ps = proj_fm(conv_w[j], lambda kc: hk(kc), 8, 512, q * 128)
                st = stage[cc % 2]
                ca = cacc[cc % 2]
                cp(P, POOL, st.r(0, 3), halo[l].r(cc * 4, cc * 4 + 3))
                cp(P, ACT, st.r(3, 3 + TT), ps.r())
                cp(P, POOL, halo[l].r(cc * 4, cc * 4 + 3), st.r(TT, TT + 3))
                act(P, ca.r(), st.r(0, TT), AF.Copy, scale=pt.r(PT_CW + cc * 4, PT_CW + cc * 4 + 1))
                if q == 3:
                    w_done()

            def convB(cc):
                st = stage[cc % 2]
                ca = cacc[cc % 2]
                for k in range(1, 4):
                    stt(P, DVE, ca.r(), st.r(k, k + TT), pt.r(PT_CW + cc * 4 + k, PT_CW + cc * 4 + k + 1),
                        ca.r(), ALU.mult, ALU.add)
                act(P, XBC(cc), ca.r(), AF.Silu, bias=pt.r(PT_CB + cc, PT_CB + cc + 1))

            convA(0)
            for cc in range(12):
                if cc + 1 < 12:
                    convA(cc + 1)
                convB(cc)
            P.phase = "ssd_chunks"
            S_, Sb_ = ssdS[l], ssdSb[l]
            bt = btab[l]
            pdt = P.psum()
            for c in range(NCH):
                for kc in range(8):
                    mm(P, pdt.r(c * 16, c * 16 + 16), hk(kc, c * 128, (c + 1) * 128), wsm[l].r(kc * 32, kc * 32 + 16), kc == 0, kc == 7)
            v4c = lambda r_: r_.m(lambda ap: ap.rearrange("p (c h) -> p c h", c=NCH))
            tt(P, DVE, v4c(dtt.r()), v4c(pdt.r(0, 64)), bt.r(BT_DTB, BT_DTB + 16).m(lambda ap: ap.unsqueeze(1).broadcast_to([128, NCH, 16])), ALU.add)
            act(P, dtt.r(), dtt.r(), AF.Exp)
            act(P, dt_all.r(), dtt.r(), AF.Ln, bias=1.0)
            tt(P, DVE, v4c(a_all.r()), v4c(dt_all.r()), negA[l].r().m(lambda ap: ap.unsqueeze(1).broadcast_to([128, NCH, 16])), ALU.mult)
            pacs = P.psum()
            for c in range(NCH):
                mm(P, pacs.r(c * 16, c * 16 + 16), tri, a_all.r(c * 16, c * 16 + 16))
            cp(P, DVE, acs_all.r(), pacs.r(0, 64))
            act(P, eacs_all.r(), pacs.r(0, 64), AF.Exp)
            for c in range(NCH):
                cs_ = slice(c * 128, (c + 1) * 128)
                dt_sb = Sub(dt_all, c * 16, 16)
                a_sb = Sub(a_all, c * 16, 16)
                acs_sb = Sub(acs_all, c * 16, 16)
                eacs = Sub(eacs_all, c * 16, 16)
                pcb = P.psum()
                for g in range(2):
                    mm(P, pcb.r(g * 128, (g + 1) * 128), XBC(8 + g, c * 128, (c + 1) * 128), XBC(10 + g, c * 128, (c + 1) * 128))
                cp(P, ACT, cb_sb.r(), pcb.r(0, 256))
                def build_a1(hg_):
                    tt(P, POOL, A1[hg_ % 2].r().m(lambda ap: ap.rearrange("p (h l) -> p h l", h=4)),
                       a_sb.r(hg_ * 4, hg_ * 4 + 4).m(lambda ap: b3(ap, 4, 128)),
                       tri.m(lambda ap: ap.unsqueeze(1).broadcast_to([128, 4, 128])), ALU.mult)
                build_a1(0)
                for hg in range(4):
                    a1 = A1[hg % 2]
                    pb = P.psum()
                    mm(P, pb.r(), C(C_ONES, 128), a1.r())
                    if hg + 1 < 4:
                        build_a1(hg + 1)
                    et = ET[hg % 2]
                    for hh in range(4):
                        h = hg * 4 + hh
                        stt(P, DVE, et.r(hh * 128, (hh + 1) * 128), pb.r(hh * 128, (hh + 1) * 128), acs_sb.r(h, h + 1),
                            C(C_NEG, 128), ALU.subtract, ALU.add)
                    cp(P, DVE, sdec.r(hg * 4, hg * 4 + 4),
                       pb.r().m(lambda ap: ap.rearrange("p (h l) -> p h l", h=4)[:, :, 127]))
                    lt = et
                    act(P, lt.r(), et.r(), AF.Exp)
                    g = hg // 2
                    v4_ = lambda r_: r_.m(lambda ap: ap.rearrange("p (h l) -> p h l", h=4))
                    tt(P, DVE, v4_(lt.r()), v4_(lt.r()),
                       cb_sb.r(g * 128, (g + 1) * 128).m(lambda ap: ap.unsqueeze(1).broadcast_to([128, 4, 128])), ALU.mult)
                    tt(P, POOL, v4_(GT.r(hg * 512, (hg + 1) * 512)), v4_(lt.r()),
                       dt_sb.r(hg * 4, hg * 4 + 4).m(lambda ap: b3(ap, 4, 128)), ALU.mult)
                tt(P, DVE, wdec.r(), sdec.r(), acs_sb.r(), ALU.subtract)
                act(P, wdec.r(), wdec.r(), AF.Exp)
                tt(P, DVE, wdec.r(), wdec.r(), dt_sb.r(), ALU.mult)
                act(P, sdec.r(), sdec.r(), AF.Exp)
                px = P.psum().as_dtype(BF16)
                for hc in range(8):
                    tr(P, px.r(hc * 128, (hc + 1) * 128), XBC(hc, c * 128, (c + 1) * 128), identb_r)
                cp(P, ACT, Xbf.r(), px.r(0, 1024))
                tt(P, DVE, XW.r().m(lambda ap: ap.rearrange("p (h d) -> p h d", h=16)),
                   px.r(0, 1024).m(lambda ap: ap.rearrange("p (h d) -> p h d", h=16)),
                   wdec.r().m(lambda ap: b3(ap, 16, 64)), ALU.mult)
                pbt = P.psum().as_dtype(BF16)
                for g in range(2):
                    tr(P, pbt.r(g * 128, (g + 1) * 128), XBC(8 + g, c * 128, (c + 1) * 128), identb_r)
                cp(P, ACT, Bbf.r(), pbt.r(0, 256))
                pyd = [P.psum(), P.psum()]
                for h in range(16):
                    g, hl = h // 8, h % 8
                    mm(P, pyd[g].r(hl * 64, (hl + 1) * 64), GT.r(h * 128, (h + 1) * 128), Xbf.r(h * 64, (h + 1) * 64))
                pyo = [P.psum(), P.psum()]
                for g in range(2):
                    mm(P, pyo[g].r(), XBC(10 + g, c * 128, (c + 1) * 128), Sb_.r(g * 512, (g + 1) * 512))
                for g in range(2):
                    gs = slice(g * 512, (g + 1) * 512)
                    v8 = lambda r_: r_.m(lambda ap: ap.rearrange("p (h d) -> p h d", h=8))
                    tt(P, DVE, v8(y1.r(g * 512, (g + 1) * 512)), v8(pyo[g].r()),
                       eacs.r(g * 8, g * 8 + 8).m(lambda ap: b3(ap, 8, 64)), ALU.mult)
                    tt(P, DVE, y1.r(g * 512, (g + 1) * 512), y1.r(g * 512, (g + 1) * 512), pyd[g].r(), ALU.add)
                    tt(P, POOL, v8(y2.r(g * 512, (g + 1) * 512)), v8(Xbf.r(g * 512, (g + 1) * 512)),
                       bt.r(BT_D + g * 8, BT_D + g * 8 + 8).m(lambda ap: b3(ap, 8, 64)), ALU.mult)
                    tt(P, POOL, y1.r(g * 512, (g + 1) * 512), y1.r(g * 512, (g + 1) * 512), y2.r(g * 512, (g + 1) * 512), ALU.add)
                    tt(P, DVE, y1.r(g * 512, (g + 1) * 512), y1.r(g * 512, (g + 1) * 512), Zt(c, g * 512, (g + 1) * 512), ALU.mult)
                    act(P, junk_s.r(), y1.r(g * 512, (g + 1) * 512), AF.Square, scale=512.0 ** -0.5, accum=ss4.r(g, g + 1))
                act(P, rs4.r(0, 2), ss4.r(0, 2), AF.Ln, bias=EPS)
                act(P, rs4.r(4, 6), rs4.r(0, 2), AF.Exp, scale=-0.5)
                for g in range(2):
                    ts(P, DVE, y2.r(g * 512, (g + 1) * 512), y1.r(g * 512, (g + 1) * 512), rs4.r(4 + g, 5 + g), None, ALU.mult)
                for g in range(2):
                    pt_ = P.psum()
                    for hl in range(4):
                        hc = g * 4 + hl
                        tr(P, pt_.r(hl * 128, (hl + 1) * 128), y2.r(hc * 128, (hc + 1) * 128), ident)
                    for hl in range(4):
                        hc = g * 4 + hl
                        act(P, outT(hc, c * 128, (c + 1) * 128), pt_.r(hl * 128, (hl + 1) * 128), AF.Copy,
                            scale=pt.r(PT_SN + hc, PT_SN + hc + 1))
                pds = [P.psum(), P.psum()]
                for g in range(2):
                    mm(P, pds[g].r(), Bbf.r(g * 128, (g + 1) * 128), XW.r(g * 512, (g + 1) * 512))
                for g in range(2):
                    v8 = lambda r_: r_.m(lambda ap: ap.rearrange("p (h d) -> p h d", h=8))
                    tt(P, POOL, v8(S_.r(g * 512, (g + 1) * 512)), v8(S_.r(g * 512, (g + 1) * 512)),
                       sdec.r(g * 8, g * 8 + 8).m(lambda ap: b3(ap, 8, 64)), ALU.mult)
                    tt(P, DVE, S_.r(g * 512, (g + 1) * 512), S_.r(g * 512, (g + 1) * 512), pds[g].r(), ALU.add)
                    cp(P, ACT, Sb_.r(g * 512, (g + 1) * 512), S_.r(g * 512, (g + 1) * 512))
            if STOP[0] == 6:
                raise _Stop()
            P.phase = "ssd_merge"
            for q in range(2):
                wo_q = w_next()
                wmg_q = w_next()
                merge_branch(1, l, (lambda wq: (lambda dc: (wq, (dc % 4) * 128)))(wo_q), 8, 512, [wmg_q, wmg_q], range(q * 4, q * 4 + 4))
                w_done(); w_done()

            if STOP[0] == 7:
                raise _Stop()
            P.phase = "gla_proj"
            GQK = lambda c, c0=0, c1=512: mix.r(c * 512 + c0, c * 512 + c1)
            GV = lambda c, c0=0, c1=512: mix.r(2048 + c * 512 + c0, 2048 + c * 512 + c1)
            GG = lambda c, c0=0, c1=512: mix.r(4096 + c * 512 + c0, 4096 + c * 512 + c1)
            wb = w_next(); proj_tm(wb, 512, lambda c, ps: cp(P, ACT, GQK(c), ps.r())); w_done()
            wb = w_next(); proj_tm(wb, 512, lambda c, ps: cp(P, ACT, GV(c), ps.r())); w_done()
            wb = w_next(); proj_tm(wb, 512, lambda c, ps: act(P, GG(c), ps.r(), AF.Silu)); w_done()
            pg = P.psum()
            for kc in range(8):
                mm(P, pg.r(0, TT, 0, 16), wsm[l].r(kc * 32 + 16, kc * 32 + 32), hk(kc), kc == 0, kc == 7)
            cp(P, ACT, glr_sb.r(), pg.r(0, TT, 0, 16))
            if STOP[0] == 8:
                raise _Stop()
            P.phase = "gla_chunks"
            S_, Sb_ = glaS[l], glaSb[l]
            for c in range(NCH):
                pga = P.psum()
                mm(P, pga.r(0, 256), glr_sb.r(c * 128, (c + 1) * 128), gwt[l].r())
                tt(P, DVE, gtm.r(), pga.r(0, 256), bt.r(BT_GB, BT_GB + 256), ALU.add)
                act(P, gtm.r(), gtm.r(), AF.Exp, scale=-1.0)
                act(P, lp.r(), gtm.r(), AF.Ln, bias=1.0)
                if STOP[0] == 81:
                    raise _Stop()
                pb_ = P.psum()
                mm(P, pb_.r(0, 256), tri, lp.r())
                mm(P, pb_.r(256, 512), C(C_TRIR, 128), lp.r())
                if STOP[0] == 82:
                    raise _Stop()
                ptot = P.psum()
                for pr in range(2):
                    mm(P, ptot.r(pr, pr + 1), lp.r(pr * 128, (pr + 1) * 128), C(C_ONES, 1))
                if STOP[0] == 83:
                    raise _Stop()
                act(P, eb.r(), pb_.r(0, 256), AF.Exp, scale=-1.0 / 16.0)
                act(P, enb.r(), pb_.r(0, 256), AF.Exp, scale=1.0 / 16.0)
                act(P, erev.r(), pb_.r(256, 512), AF.Exp, scale=-1.0 / 16.0)
                act(P, gdec.r(), ptot.r(0, 2), AF.Exp, scale=-1.0 / 16.0)
                if STOP[0] == 84:
                    raise _Stop()
                stt(P, DVE, qin.r(), GQK(c, 0, 256), 0.125, eb.r(), ALU.mult, ALU.mult)
                tt(P, POOL, kin.r(), GQK(c, 256, 512), enb.r(), ALU.mult)
                tt(P, POOL, kst.r(), GQK(c, 256, 512), erev.r(), ALU.mult)
                if STOP[0] == 85:
                    raise _Stop()
                pq = P.psum().as_dtype(BF16)
                for pr in range(2):
                    tr(P, pq.r(pr * 128, (pr + 1) * 128), qin.r(pr * 128, (pr + 1) * 128), identb_r)
                    tr(P, pq.r(256 + pr * 128, 256 + (pr + 1) * 128), kin.r(pr * 128, (pr + 1) * 128), identb_r)
                cp(P, ACT, qT_g.r(0, 256), pq.r(0, 256))
                cp(P, DVE, kT_g.r(0, 256), pq.r(256, 512))
                if STOP[0] == 86:
                    raise _Stop()
                psc2 = [P.psum(), P.psum()]
                for h in range(4):
                    pr, hl = h // 2, h % 2
                    mm(P, psc2[hl].r(pr * 128, (pr + 1) * 128), kT_g.r(pr * 128, (pr + 1) * 128, hl * 64, (hl + 1) * 64),
                       qT_g.r(pr * 128, (pr + 1) * 128, hl * 64, (hl + 1) * 64))
                for hl in range(2):
                    tt(P, DVE, sT_g.r().m((lambda hl_: (lambda ap: ap.rearrange("p (pr hl d) -> p pr hl d", pr=2, hl=2)[:, :, hl_, :]))(hl)),
                       psc2[hl].r(0, 256).m(lambda ap: ap.rearrange("p (h d) -> p h d", h=2)),
                       tri.m(lambda ap: ap.unsqueeze(1).broadcast_to([128, 2, 128])), ALU.mult)
                if STOP[0] == 87:
                    raise _Stop()
                po = P.psum()
                for h in range(4):
                    pr, hl = h // 2, h % 2
                    mm(P, po.r(h * 128, (h + 1) * 128), sT_g.r(h * 128, (h + 1) * 128), GV(c, h * 128, (h + 1) * 128), True, False)
                    mm(P, po.r(h * 128, (h + 1) * 128), qT_g.r(pr * 128, (pr + 1) * 128, hl * 64, (hl + 1) * 64),
                       Sb_.r(pr * 128, (pr + 1) * 128, hl * 64, (hl + 1) * 64), False, True)
                if STOP[0] == 88:
                    raise _Stop()
                pds = P.psum()
                for pr in range(2):
                    mm(P, pds.r(pr * 256, (pr + 1) * 256), kst.r(pr * 128, (pr + 1) * 128), GV(c, pr * 256, (pr + 1) * 256))
                for h in range(4):
                    pr, hl = h // 2, h % 2
                    stt(P, DVE, S_.r(pr * 128, (pr + 1) * 128, hl * 64, (hl + 1) * 64),
                        S_.r(pr * 128, (pr + 1) * 128, hl * 64, (hl + 1) * 64),
                        gdec.r(pr, pr + 1, hl * 64, (hl + 1) * 64),
                        pds.r(pr * 256 + hl * 128, pr * 256 + (hl + 1) * 128, hl * 64, (hl + 1) * 64), ALU.mult, ALU.add)
                if STOP[0] == 89:
                    raise _Stop()
                cp(P, POOL, Sb_.r(), S_.r())
                group_post(po, lambda h: GG(c, h * 128, (h + 1) * 128), pcol(PT_GN), c, junk_g, og_g)
            if STOP[0] == 9:
                raise _Stop()
            P.phase = "gla_merge"
            wo_b = w_next()
            wmg = [w_next(), w_next()]
            merge_branch(2, l, lambda dc: (wo_b, dc * 128), 4, 1024, wmg)
            w_done(); w_done(); w_done()
            if after_gla is not None:
                after_gla()

            if STOP[0] == 10:
                raise _Stop()
            P.phase = "w_out"
            wo = [w_next(), w_next()]
            for dc in range(8):
                ps = proj_fm(wo[dc // 4], lambda kc: mrg.r(kc * TT, (kc + 1) * TT), 8, 512, (dc % 4) * 128)
                tt(P, DVE, xk(dc), xk(dc), ps.r(), ALU.add)
            w_done(); w_done()

            if STOP[0] == 11:
                raise _Stop()
            P.phase = "mlp"
            rmsnorm(pcol(PT_MN), hk)
            hid = lambda j: bigr.r(j * TT, (j + 1) * TT)
            for g in range(4):
                wu = [w_next(), w_next()]
                for j in range(8):
                    ps = proj_fm(wu[j // 4], lambda kc: hk(kc), 8, 512, (j % 4) * 128)
                    rt = relu_t[j % 2]
                    act(P, rt.r(), ps.r(), AF.Relu)
                    act(P, hid(j), rt.r(), AF.Square)
                w_done(); w_done()
                wd = [w_next(), w_next()]
                for dc in range(8):
                    ps = proj_fm(wd[dc // 4], hid, 8, 512, (dc % 4) * 128)
                    tt(P, DVE, xk(dc), xk(dc), ps.r(), ALU.add)
                w_done(); w_done()

        xin = mix.as_dtype(F32)

        def load_inputs(ti):
            t0 = min(ti, n_tiles - 1) * TT
            P.dma(SP, "xin", xin.r(0, 4096).m(lambda ap: ap.rearrange("p (c d) -> p c d", c=NCH)).ap,
                  x_d[t0:t0 + TT, :].rearrange("(c p) d -> p c d", p=128), writes=[xin.r(0, 4096)])
            csr = csb[ti % 2]
            P.dma(SP, "cs%d" % (ti % 2), csr.r().m(lambda ap: ap.rearrange("p (c d) -> p c d", c=NCH)).ap,
                  cs_d[ti * TT:(ti + 1) * TT, :].rearrange("(c p) d -> p c d", p=128), writes=[csr.r()])

        load_inputs(0)
        for ti in range(n_steps):
            P.phase = "boundary"
            blend = PP and ti >= 1
            if blend:
                P.dma(POOL, "gin", scr.r(0, 4096).ap, gath.ap()[0:128, :], reads=[tg.r()], writes=[scr.r(0, 4096)])
                for dc in range(8):
                    if dc % 2:
                        act(P, scr.r(dc * TT, (dc + 1) * TT), scr.r(dc * TT, (dc + 1) * TT), AF.Copy, scale=flag.r(1, 2))
                    else:
                        ts(P, DVE, scr.r(dc * TT, (dc + 1) * TT), scr.r(dc * TT, (dc + 1) * TT), flag.r(1, 2), None, ALU.mult)
            for dc in range(8):
                ps = P.psum()
                for c in range(NCH):
                    tr(P, ps.r(c * 128, (c + 1) * 128), xin.r(c * D + dc * 128, c * D + (dc + 1) * 128), ident)
                if blend:
                    stt(P, DVE, xk(dc), ps.r(), flag.r(0, 1), scr.r(dc * TT, (dc + 1) * TT), ALU.mult, ALU.add)
                else:
                    cp(P, ACT if dc % 2 else DVE, xk(dc), ps.r())
            nxt = (lambda t_: (lambda: load_inputs(t_)))(ti + 1) if ti + 1 < n_steps else None
            for l in range(n_layers):
                try:
                    layer(l, ti, nxt if l == n_layers - 1 else None)
                except _Stop:
                    pass
            P.phase = "boundary"
            if PP:
                P.dma(POOL, "ccin", cc_in.ap(), xres.r().ap, reads=[xres.r()], writes=[tin.r()])
                P.cc("ag", lambda e: e.collective_compute("AllGather", ALU.bypass, replica_groups=pp_groups,
                                                         ins=[cc_in.ap().opt()], outs=[gath.ap().opt()]),
                     reads=[tin.r()], writes=[tg.r()])
                if ti == 0:
                    for l in range(n_layers):
                        for t_ in (retS[l], retSb[l], ssdS[l], ssdSb[l], glaS[l], glaSb[l], halo[l]):
                            ts(P, DVE, t_.r(), t_.r(), flag.r(0, 1), None, ALU.mult)
                    continue
            o0 = (ti - 1) * TT if PP else ti * TT
            for c in range(NCH):
                pss = [P.psum(), P.psum()]
                for q in range(2):
                    for j in range(4):
                        dc = q * 4 + j
                        tr(P, pss[q].r(j * 128, (j + 1) * 128), xk(dc, c * 128, (c + 1) * 128), ident)
                for q in range(2):
                    act(P, ojunk.r(), pss[q].r(), AF.Square, scale=1.0 / 32.0, accum=oss.r(q, q + 1))
                tt(P, DVE, oss.r(2, 3), oss.r(0, 1), oss.r(1, 2), ALU.add)
                act(P, oss.r(3, 4), oss.r(2, 3), AF.Ln, bias=EPS)
                act(P, oss.r(4, 5), oss.r(3, 4), AF.Exp, scale=-0.5)
                stg = ostage[c % 2]
                for q in range(2):
                    stt(P, DVE, stg.r(q * 512, (q + 1) * 512), pss[q].r(), oss.r(4, 5), fnw_bc.r(q * 512, (q + 1) * 512),
                        ALU.mult, ALU.mult)
                P.dma(SP, "xout%d" % (c % 2), out_d[o0 + c * 128:o0 + (c + 1) * 128, :], stg.r().ap, reads=[stg.r()], final=True)
        P.emit()
    return nc


_CACHE = {}


def _get_prog(n_tiles, n_layers, groups):
    key = (n_tiles, n_layers, str(groups))
    if key not in _CACHE:
        _CACHE[key] = build_program(n_tiles, n_layers, groups)
    return _CACHE[key]


def _layer_maps(inp, layers):
    ws, wsms, pts, bts, gws = [], [], [], [], []
    for l in layers:
        wst, wsm, pt, bt, gw = host_layer_weights(inp, l)
        ws.append(wst); wsms.append(wsm); pts.append(pt); bts.append(bt); gws.append(gw)
    return {
        "wst": np.ascontiguousarray(np.concatenate(ws, 0)),
        "wsm": np.stack(wsms, 0),
        "ptab": np.stack(pts, 0),
        "btab": np.stack(bts, 0),
        "gw": np.stack(gws, 0),
        "fnw": np.ascontiguousarray(np.asarray(inp["final_norm_w"], np.float32).reshape(8, 128).T),
        "fnwb": np.ascontiguousarray(np.asarray(inp["final_norm_w"], np.float32)),
        "cst": host_consts(),
        "cstr": np.full((128, 128), 1.0 / 1024.0, np.float32),
    }


def make_in_maps(inp, seqs, n_layers):
    S = seqs[0].shape[0]
    common = _layer_maps(inp, range(n_layers))
    common["cs"] = host_cs(S)
    return [dict(common, x=np.ascontiguousarray(s, dtype=np.float32)) for s in seqs]


def make_in_maps_pp(inp, seqs):
    n = len(seqs)
    S = seqs[0].shape[0]
    cs = host_cs(S)
    pad = np.zeros((TT, 128), np.float32)
    la = _layer_maps(inp, [0])
    lb = _layer_maps(inp, [1])
    fa = np.zeros((128, 2), np.float32); fa[:, 0] = 1.0
    fb = np.zeros((128, 2), np.float32); fb[:, 1] = 1.0
    zeros = np.zeros((S, D), np.float32)
    maps = []
    for i in range(n):
        maps.append(dict(la, x=np.ascontiguousarray(seqs[i], dtype=np.float32), cs=np.concatenate([cs, pad], 0), flag=fa))
    for i in range(n):
        maps.append(dict(lb, x=zeros, cs=np.concatenate([pad, cs], 0), flag=fb))
    return maps


def kernel(**inputs):
    inp = {k: np.asarray(v) for k, v in inputs.items()}
    x = inp["x"].astype(np.float32, copy=False)
    B, S, _ = x.shape
    groups = [[b, b + B] for b in range(B)]
    nc = _get_prog(S // TT, 1, groups)
    in_maps = make_in_maps_pp(inp, [x[b] for b in range(B)])
    res = run_bass_kernel_spmd(nc, in_maps, core_ids=list(range(2 * B)))
    out = np.stack([res.results[B + b]["out"] for b in range(B)], 0)
    return out.astype(np.float32, copy=False)
```
